# Optimizing a Trainium2 kernel written in Bass

```python
import math
import jax, jax.numpy as jnp
from jax import lax
import numpy as np

D_MODEL = 1024
BATCH = 8
SEQ = 4096
DEPTH = 1

GRID_W = 64
CTX_LEN = 256
N_MOD = 9
D_FF = 2816
EPS = 1e-6
SSD_HEADS = 8
SSD_HEAD_DIM = 64
SSD_WIDTH = SSD_HEADS * SSD_HEAD_DIM
SSD_GROUPS = 2
SSD_STATE = 128
SSD_CONV = 5
SSD_CHUNK = 128
N_DIR = 2
CONV_CH = SSD_WIDTH + 2 * SSD_GROUPS * SSD_STATE
MLA_HEADS = 8
QK_NOPE = 64
QK_ROPE = 32
V_DIM = 64
Q_LORA = 768
KV_LORA = 256
MLA_WIDTH = MLA_HEADS * V_DIM
MLA_SCALE = (QK_NOPE + QK_ROPE) ** -0.5
ROPE_BASE = 10000.0
Q_BLOCK = 128
MIX_WIDTH = SSD_WIDTH + MLA_WIDTH
SPLIT_Z = SSD_WIDTH
SPLIT_XBC = SPLIT_Z + CONV_CH
SPLIT_DT = SPLIT_XBC + N_DIR * SSD_HEADS
SPLIT_CQ = SPLIT_DT + Q_LORA
SPLIT_CKV = SPLIT_CQ + KV_LORA
IN_DIM = SPLIT_CKV + QK_ROPE
IN_SPLITS = (SPLIT_Z, SPLIT_XBC, SPLIT_DT, SPLIT_CQ, SPLIT_CKV)

kernel_name = 'hymba_ssd_mla_macaron_dit'

F32 = jnp.float32


def rmsnorm(x, g):
    xf = x.astype(F32)
    y = xf * lax.rsqrt(jnp.mean(xf * xf, axis=-1, keepdims=True) + EPS)
    return y.astype(x.dtype) * g


def modulate(h, shift, scale):
    return h * (1 + scale) + shift


def swiglu(h, w_in, w_out):
    gate, up = jnp.split(h @ w_in, 2, axis=-1)
    return (jax.nn.silu(gate) * up) @ w_out


def ffn_half_step(h, m, k, norm_g, w_in, w_out):
    hn = modulate(rmsnorm(h, norm_g), m[:, :, k], m[:, :, k + 1])
    return h + 0.5 * m[:, :, k + 2] * swiglu(hn, w_in, w_out)


def _flip(t):
    return jnp.flip(t, axis=1)


def dwconv_centred(u, w, b):
    y = lax.conv_general_dilated(u, w[:, None, :], window_strides=(1,),
                                 padding=((SSD_CONV // 2, SSD_CONV // 2),),
                                 dimension_numbers=('NWC', 'WIO', 'NWC'),
                                 feature_group_count=u.shape[-1])
    return y + b


def ssd_inputs(xbc, dt_raw, conv_w, conv_b, dt_bias):
    b, L, _ = xbc.shape
    xbc = jax.nn.silu(dwconv_centred(xbc, conv_w, conv_b))
    xs, bm, cm = jnp.split(xbc, [SSD_WIDTH, SSD_WIDTH + SSD_GROUPS * SSD_STATE], axis=-1)
    xh = xs.reshape(b, L, SSD_HEADS, SSD_HEAD_DIM)
    bm = bm.reshape(b, L, SSD_GROUPS, SSD_STATE)
    cm = cm.reshape(b, L, SSD_GROUPS, SSD_STATE)
    dt = jax.nn.softplus(dt_raw.astype(F32).reshape(b, L, N_DIR, SSD_HEADS) + dt_bias.astype(F32))
    return xh, bm, cm, dt


def ssd_scan(xh, dt, a_neg, bm, cm, h0):
    b, L, H, P = xh.shape
    G, N = bm.shape[-2:]
    HG = H // G
    Q = SSD_CHUNK
    nc = L // Q
    xdt = (xh.astype(F32) * dt[..., None]).reshape(b, nc, Q, G, HG, P)
    a = (dt * a_neg).reshape(b, nc, Q, G, HG)
    a_cum = jnp.cumsum(jnp.moveaxis(a, 2, -1), axis=-1)
    bc = bm.astype(F32).reshape(b, nc, Q, G, N)
    cc = cm.astype(F32).reshape(b, nc, Q, G, N)
    seg = a_cum[..., :, None] - a_cum[..., None, :]
    lower = jnp.tril(jnp.ones((Q, Q), dtype=bool))
    decay_in = jnp.exp(jnp.where(lower, seg, -jnp.inf))
    cb = jnp.einsum('bcign,bcjgn->bcgij', cc, bc)
    y_diag = jnp.einsum('bcghij,bcjghp->bcighp', cb[:, :, :, None] * decay_in, xdt)
    decay_end = jnp.moveaxis(jnp.exp(a_cum[..., -1:] - a_cum), -1, 2)
    states = jnp.einsum('bcjgn,bcjghp->bcghpn', bc, xdt * decay_end[..., None])
    chunk_decay = jnp.exp(a_cum[..., -1])

    def step(h, inp):
        dec, st = inp
        return dec[..., None, None] * h + st, h

    h_last, h_enter = lax.scan(step, h0.reshape(b, G, HG, P, N),
                               (jnp.moveaxis(chunk_decay, 1, 0), jnp.moveaxis(states, 1, 0)))
    h_enter = jnp.moveaxis(h_enter, 0, 1)
    decay_from_start = jnp.moveaxis(jnp.exp(a_cum), -1, 2)
    y_off = jnp.einsum('bcign,bcghpn->bcighp', cc, h_enter) * decay_from_start[..., None]
    y = (y_diag + y_off).reshape(b, L, H, P)
    return y, h_last.reshape(b, H, P, N)


def ssd_merge(y_fwd, y_bwd, xh, z, d_skip, ssd_norm):
    b, L = xh.shape[:2]
    y = y_fwd + y_bwd + d_skip[:, None] * xh
    return rmsnorm(y.reshape(b, L, SSD_WIDTH) * jax.nn.silu(z), ssd_norm)


def axial_rope(t, row, col):
    half = QK_ROPE // 2
    inv = ROPE_BASE ** (-jnp.arange(0, half, 2, dtype=F32) / half)

    def rot(u, pos):
        ang = pos.astype(F32)[:, None] * inv
        cos = jnp.cos(ang)[:, None, :].astype(u.dtype)
        sin = jnp.sin(ang)[:, None, :].astype(u.dtype)
        u1, u2 = jnp.split(u, 2, axis=-1)
        return jnp.concatenate([u1 * cos - u2 * sin, u1 * sin + u2 * cos], axis=-1)

    t_row, t_col = jnp.split(t, 2, axis=-1)
    return jnp.concatenate([rot(t_row, row), rot(t_col, col)], axis=-1)


def mla_project(c_q, c_kv, k_pe, q_norm, w_uq, kv_norm, w_ukv, row, col):
    b, L, _ = c_q.shape
    q = (rmsnorm(c_q, q_norm) @ w_uq).reshape(b, L, MLA_HEADS, QK_NOPE + QK_ROPE)
    kv = (rmsnorm(c_kv, kv_norm) @ w_ukv).reshape(b, L, MLA_HEADS, QK_NOPE + V_DIM)
    q_nope, q_pe = jnp.split(q, [QK_NOPE], axis=-1)
    k_nope, v = jnp.split(kv, [QK_NOPE], axis=-1)
    k_pe = k_pe[:, :, None, :]
    if row is not None:
        q_pe = axial_rope(q_pe, row, col)
        k_pe = axial_rope(k_pe, row, col)
    q = jnp.concatenate([q_nope, q_pe], axis=-1)
    k = jnp.concatenate([k_nope, jnp.broadcast_to(k_pe, (b, L, MLA_HEADS, QK_ROPE))], axis=-1)
    return q, k, v


def softmax_attend(q, k, v):
    s = jnp.einsum('bqhd,bkhd->bhqk', q, k).astype(F32) * MLA_SCALE
    p = jax.nn.softmax(s, axis=-1).astype(v.dtype)
    return jnp.einsum('bhqk,bkhd->bqhd', p, v)


def hybrid_mixer(hx, hc, w_in, conv_w, conv_b, dt_bias, a_log, d_skip, ssd_norm,
                 q_norm, w_uq, kv_norm, w_ukv, w_out, need_ctx):
    b, L, _ = hx.shape
    cl = hc.shape[1]
    rows = L // GRID_W
    row = jnp.repeat(jnp.arange(rows, dtype=jnp.int32), GRID_W)
    col = jnp.arange(rows * GRID_W, dtype=jnp.int32) % GRID_W

    z_x, xbc_x, dt_x, cq_x, ckv_x, kpe_x = jnp.split(hx @ w_in, IN_SPLITS, axis=-1)
    z_c, xbc_c, dt_c, cq_c, ckv_c, kpe_c = jnp.split(hc @ w_in, IN_SPLITS, axis=-1)

    xh_x, bm_x, cm_x, dtv_x = ssd_inputs(xbc_x, dt_x, conv_w, conv_b, dt_bias)
    xh_c, bm_c, cm_c, dtv_c = ssd_inputs(xbc_c, dt_c, conv_w, conv_b, dt_bias)
    a_neg = -jnp.exp(a_log.astype(F32))
    h0 = jnp.zeros((b, SSD_HEADS, SSD_HEAD_DIM, SSD_STATE), F32)
    yf_c, hf = ssd_scan(xh_c, dtv_c[:, :, 0], a_neg[0], bm_c, cm_c, h0)
    yf_x, _ = ssd_scan(xh_x, dtv_x[:, :, 0], a_neg[0], bm_x, cm_x, hf)
    yb_c, hb = ssd_scan(_flip(xh_c), _flip(dtv_c[:, :, 1]), a_neg[1], _flip(bm_c), _flip(cm_c), h0)
    yb_x, _ = ssd_scan(_flip(xh_x), _flip(dtv_x[:, :, 1]), a_neg[1], _flip(bm_x), _flip(cm_x), hb)
    ssd_x = ssd_merge(yf_x, _flip(yb_x), xh_x, z_x, d_skip, ssd_norm)

    q_x, k_x, v_x = mla_project(cq_x, ckv_x, kpe_x, q_norm, w_uq, kv_norm, w_ukv, row, col)
    q_c, k_c, v_c = mla_project(cq_c, ckv_c, kpe_c, q_norm, w_uq, kv_norm, w_ukv, None, None)
    k_all = jnp.concatenate([k_c, k_x], axis=1)
    v_all = jnp.concatenate([v_c, v_x], axis=1)
    nb = L // Q_BLOCK
    q_blocks = jnp.swapaxes(q_x.reshape(b, nb, Q_BLOCK, MLA_HEADS, QK_NOPE + QK_ROPE), 0, 1)
    o_blocks = lax.map(lambda qb: softmax_attend(qb, k_all, v_all), q_blocks)
    mla_x = jnp.swapaxes(o_blocks, 0, 1).reshape(b, L, MLA_WIDTH)

    y_x = jnp.concatenate([ssd_x, mla_x], axis=-1) @ w_out
    if need_ctx:
        ssd_c = ssd_merge(yf_c, _flip(yb_c), xh_c, z_c, d_skip, ssd_norm)
        mla_c = softmax_attend(q_c, k_c, v_c).reshape(b, cl, MLA_WIDTH)
        y_c = jnp.concatenate([ssd_c, mla_c], axis=-1) @ w_out
        return y_x, y_c
    return y_x, None


def setup_inputs(seed: int = 0) -> dict:
    key = jax.random.key(seed)
    ks = jax.random.split(key, 28)

    def nrm(k, shape, scale):
        return jax.random.normal(k, shape, F32) * scale

    def gain(k, shape):
        return 1.0 + 0.02 * jax.random.normal(k, shape, F32)

    dt0 = jnp.exp(jax.random.uniform(ks[11], (DEPTH, N_DIR, SSD_HEADS), F32,
                                     math.log(1e-3), math.log(1e-1)))
    in_col_scale = jnp.ones((IN_DIM,), F32).at[SPLIT_XBC:SPLIT_DT].set(0.1)
    return {
        'x': nrm(ks[0], (BATCH, SEQ, D_MODEL), 1.0),
        'c': nrm(ks[1], (BATCH, D_MODEL), 1.0),
        'ctx': nrm(ks[2], (BATCH, CTX_LEN, D_MODEL), 1.0),
        'c_ctx': nrm(ks[3], (D_MODEL,), 1.0),
        'w_ada': nrm(ks[4], (DEPTH, D_MODEL, N_MOD * D_MODEL), 0.5 * D_MODEL ** -0.5),
        'b_ada': nrm(ks[5], (DEPTH, N_MOD * D_MODEL), 0.02),
        'norm_ffn1': gain(ks[6], (DEPTH, D_MODEL)),
        'w_ffn1_in': nrm(ks[7], (DEPTH, D_MODEL, 2 * D_FF), D_MODEL ** -0.5),
        'w_ffn1_out': nrm(ks[8], (DEPTH, D_FF, D_MODEL), D_FF ** -0.5),
        'norm_mix': gain(ks[9], (DEPTH, D_MODEL)),
        'w_in': nrm(ks[10], (DEPTH, D_MODEL, IN_DIM), D_MODEL ** -0.5) * in_col_scale,
        'conv_w': nrm(ks[12], (DEPTH, SSD_CONV, CONV_CH), SSD_CONV ** -0.5),
        'conv_b': nrm(ks[13], (DEPTH, CONV_CH), 0.02),
        'dt_bias': dt0 + jnp.log(-jnp.expm1(-dt0)),
        'a_log': jnp.log(jax.random.uniform(ks[14], (DEPTH, N_DIR, SSD_HEADS), F32, 1.0, 16.0)),
        'd_skip': gain(ks[15], (DEPTH, SSD_HEADS)),
        'ssd_norm': gain(ks[16], (DEPTH, SSD_WIDTH)),
        'q_norm': gain(ks[17], (DEPTH, Q_LORA)),
        'w_uq': nrm(ks[18], (DEPTH, Q_LORA, MLA_HEADS * (QK_NOPE + QK_ROPE)), Q_LORA ** -0.5),
        'kv_norm': gain(ks[19], (DEPTH, KV_LORA)),
        'w_ukv': nrm(ks[20], (DEPTH, KV_LORA, MLA_HEADS * (QK_NOPE + V_DIM)), KV_LORA ** -0.5),
        'w_out': nrm(ks[21], (DEPTH, MIX_WIDTH, D_MODEL), MIX_WIDTH ** -0.5),
        'norm_ffn2': gain(ks[22], (DEPTH, D_MODEL)),
        'w_ffn2_in': nrm(ks[23], (DEPTH, D_MODEL, 2 * D_FF), D_MODEL ** -0.5),
        'w_ffn2_out': nrm(ks[24], (DEPTH, D_FF, D_MODEL), D_FF ** -0.5),
        'final_norm': gain(ks[25], (D_MODEL,)),
    }


def reference(x, c, ctx, c_ctx, w_ada, b_ada, norm_ffn1, w_ffn1_in, w_ffn1_out, norm_mix,
              w_in, conv_w, conv_b, dt_bias, a_log, d_skip, ssd_norm, q_norm, w_uq,
              kv_norm, w_ukv, w_out, norm_ffn2, w_ffn2_in, w_ffn2_out, final_norm):
    b = x.shape[0]
    for i in range(DEPTH):
        last = i == DEPTH - 1
        m_x = (jax.nn.silu(c) @ w_ada[i] + b_ada[i]).reshape(b, 1, N_MOD, D_MODEL)
        m_c = (jax.nn.silu(c_ctx) @ w_ada[i] + b_ada[i]).reshape(1, 1, N_MOD, D_MODEL)
        x = ffn_half_step(x, m_x, 0, norm_ffn1[i], w_ffn1_in[i], w_ffn1_out[i])
        ctx = ffn_half_step(ctx, m_c, 0, norm_ffn1[i], w_ffn1_in[i], w_ffn1_out[i])
        hx = modulate(rmsnorm(x, norm_mix[i]), m_x[:, :, 3], m_x[:, :, 4])
        hc = modulate(rmsnorm(ctx, norm_mix[i]), m_c[:, :, 3], m_c[:, :, 4])
        y_x, y_c = hybrid_mixer(hx, hc, w_in[i], conv_w[i], conv_b[i], dt_bias[i], a_log[i],
                                d_skip[i], ssd_norm[i], q_norm[i], w_uq[i], kv_norm[i],
                                w_ukv[i], w_out[i], not last)
        x = x + m_x[:, :, 5] * y_x
        x = ffn_half_step(x, m_x, 6, norm_ffn2[i], w_ffn2_in[i], w_ffn2_out[i])
        if not last:
            ctx = ctx + m_c[:, :, 5] * y_c
            ctx = ffn_half_step(ctx, m_c, 6, norm_ffn2[i], w_ffn2_in[i], w_ffn2_out[i])
    return rmsnorm(x, final_norm)
```

```python
import contextlib
import numpy as np
import ml_dtypes
import concourse.bass as bass
import concourse.mybir as mybir
from concourse.bass_utils import run_bass_kernel_spmd

F32 = mybir.dt.float32
BF16 = mybir.dt.bfloat16
AF = mybir.ActivationFunctionType
ALU = mybir.AluOpType

D = 1024
L = 4096
CL = 256
T = L + CL
NT = T // 128
DFF = 2816
NKF = DFF // 128
EPS = 1e-6
IN_DIM = 2608
SCALE = 96 ** -0.5
ENABLE_SSD = True

C_ID = 0
C_LE = 128
C_GT = 256
C_GE = 384
C_LT = 512
C_ONE = 640
C_SEL = 768
NCONST = 1024


def _merge(d, s):
    for k, v in s.items():
        if d.get(k, 0) < v:
            d[k] = v


class Buf:
    __slots__ = ("name", "w", "r")

    def __init__(self, name=""):
        self.name = name
        self.w = {}
        self.r = {}


class TK:
    def __init__(self, nc, es, n_dma=14):
        self.nc = nc
        self.eng = {"pe": nc.tensor, "act": nc.scalar, "dve": nc.vector, "pool": nc.gpsimd, "sp": nc.sync}
        self.sem = {}
        self.cnt = {}
        for n in ("pe", "act", "dve", "pool"):
            self.sem[n] = es.enter_context(nc.semaphore("s_" + n))
            self.cnt[n] = 0
        self.nd = n_dma
        for i in range(n_dma):
            self.sem[f"d{i}"] = es.enter_context(nc.semaphore(f"s_d{i}"))
            self.cnt[f"d{i}"] = 0
        self.drr = 0
        self.seen = {n: {} for n in self.eng}

    def _wait(self, e, deps):
        eng = self.eng[e]
        seen = self.seen[e]
        for k, v in deps.items():
            if k == e:
                continue
            if seen.get(k, 0) < v:
                eng.wait_ge(self.sem[k], v)
                seen[k] = v

    def op(self, e, fn, reads=(), writes=()):
        deps = {}
        for b in reads:
            _merge(deps, b.w)
        own = deps.get(e, 0)
        if e != "pe" and own > 0 and self.cnt[e] - own < 2 and self.seen[e].get(e, 0) < own:
            self.eng[e].wait_ge(self.sem[e], own)
            self.seen[e][e] = own
        for b in writes:
            _merge(deps, b.r)
            _merge(deps, b.w)
        self._wait(e, deps)
        ins = fn(self.eng[e])
        self.cnt[e] += 1
        ins.then_inc(self.sem[e], 1)
        c = self.cnt[e]
        for b in reads:
            b.r[e] = c
        for b in writes:
            b.w[e] = c
        return ins

    def dma(self, out, in_, reads=(), writes=(), q="sp", **kw):
        k = self.drr
        self.drr = (self.drr + 1) % self.nd
        key = f"d{k}"
        deps = {}
        for b in reads:
            _merge(deps, b.w)
        for b in writes:
            _merge(deps, b.r)
            _merge(deps, b.w)
        if self.cnt[key] > 0:
            deps[key] = max(deps.get(key, 0), self.cnt[key])
        self._wait(q, deps)
        ins = self.eng[q].dma_start(out=out, in_=in_, **kw)
        self.cnt[key] += 16
        ins.then_inc(self.sem[key], 16)
        c = self.cnt[key]
        for b in reads:
            b.r[key] = c
        for b in writes:
            b.w[key] = c
        return ins

    def barrier(self, engines=("pe", "act", "dve", "pool", "sp")):
        for e in engines:
            deps = {k: v for k, v in self.cnt.items() if v > 0}
            self._wait(e, deps)


class SB:
    def __init__(self, t, nb=1):
        self.t = t
        self.bufs = [Buf() for _ in range(nb)]

    @property
    def b(self):
        return self.bufs[0]


def build(debug=None):
    nc = bass.Bass("TRN2", target_bir_lowering=False)

    def din(name, shape, dt=F32):
        return nc.dram_tensor(name, list(shape), dt, kind="ExternalInput").ap()

    x_d = din("x", [L, D])
    ctx_d = din("ctx", [CL, D])
    cc_d = din("cc", [2, D])
    w_ada_d = din("w_ada", [D, 9 * D])
    b_ada_d = din("b_ada", [1, 9 * D])
    norm_ffn1_d = din("norm_ffn1", [1, D])
    w_ffn1_in_d = din("w_ffn1_in", [D, 2 * DFF])
    w_ffn1_out_d = din("w_ffn1_out", [DFF, D])
    norm_mix_d = din("norm_mix", [1, D])
    w_in_d = din("w_in", [D, IN_DIM])
    conv_w_d = din("conv_w", [5, D])
    conv_b_d = din("conv_b", [1, D])
    dt_bias_d = din("dt_bias", [1, 16])
    a_log_d = din("a_log", [1, 16])
    d_skip_d = din("d_skip", [1, 8])
    ssd_norm_d = din("ssd_norm", [1, 512])
    q_norm_d = din("q_norm", [1, 768])
    w_uq_d = din("w_uq", [768, 768])
    kv_norm_d = din("kv_norm", [1, 256])
    w_ukv_d = din("w_ukv", [256, 1024])
    w_out_d = din("w_out", [D, D])
    norm_ffn2_d = din("norm_ffn2", [1, D])
    w_ffn2_in_d = din("w_ffn2_in", [D, 2 * DFF])
    w_ffn2_out_d = din("w_ffn2_out", [DFF, D])
    final_norm_d = din("final_norm", [1, D])
    consts_d = din("consts", [128, NCONST])
    rope_d = din("rope", [2, 32, L])

    out_d = nc.dram_tensor("out", [L, D], F32, kind="ExternalOutput").ap()

    def dscr(name, shape, dt):
        kind = "ExternalOutput" if (debug and name in debug) else "Internal"
        return nc.dram_tensor(name, list(shape), dt, kind=kind).ap()

    mrow_d = dscr("mrow", [2, 9 * D], F32)
    x1_d = dscr("x1", [T, D], F32)

    with contextlib.ExitStack() as es:
        tk = TK(nc, es)

        def sb(st, name, shape, dt=F32, nb=1):
            return SB(st.enter_context(nc.sbuf_tensor(name, list(shape), dt)), nb)

        def ps(st, name, shape, dt=F32, nb=1):
            return SB(st.enter_context(nc.psum_tensor(name, list(shape), dt)), nb)

        dram_bufs = {}

        def db(name):
            if name not in dram_bufs:
                dram_bufs[name] = Buf(name)
            return dram_bufs[name]

        cst = sb(es, "cst", [128, NCONST])
        idb = sb(es, "idb", [128, 128], BF16)
        cols = sb(es, "cols", [128, 8, 32])
        modp = sb(es, "modp", [128, 3, 2, 2, 8])
        tk.dma(cst.t[:], consts_d, writes=[cst.b])
        tk.op("dve", lambda e: e.tensor_copy(out=idb.t[:], in_=cst.t[:, C_ID:C_ID + 128]), [cst.b], [idb.b])

        with contextlib.ExitStack() as p0, nc.named_scope("p0"):
            stage = sb(p0, "stage", [32, D])
            wa = sb(p0, "wa", [128, 4, 8, 512], F32, nb=4)
            mrow = sb(p0, "mrow_sb", [2, 9 * D])
            bada = sb(p0, "bada", [1, 9 * D])
            pm = ps(p0, "pm", [2, 2, 512], F32, nb=2)
            pt = ps(p0, "pt", [128, 8, 32], F32)
            pc = ps(p0, "pc", [128, 72, 2], F32)
            mcol = sb(p0, "mcol", [128, 72, 2])

            tk.op("pool", lambda e: e.memset(stage.t[:], 0.0), [], [stage.b])
            tk.dma(stage.t[0:2, :], cc_d, writes=[stage.b])
            rows = [(2, norm_ffn1_d, D), (3, norm_mix_d, D), (4, norm_ffn2_d, D), (10, conv_b_d, D),
                    (11, q_norm_d, 768), (12, kv_norm_d, 256), (13, ssd_norm_d, 512)]
            for r, src, n in rows:
                tk.dma(stage.t[r:r + 1, 0:n], src, writes=[stage.b])
            tk.dma(stage.t[5:10, :], conv_w_d, writes=[stage.b])
            tk.dma(bada.t[:], b_ada_d, writes=[bada.b])
            tk.op("act", lambda e: e.activation(out=stage.t[0:2, :], in_=stage.t[0:2, :], func=AF.Silu),
                  [stage.b], [stage.b])
            for kc in range(8):
                tk.op("pe", lambda e, kc=kc: e.transpose(out=pt.t[:, kc, :], in_=stage.t[0:32, kc * 128:(kc + 1) * 128],
                                                         identity=cst.t[0:32, C_ID:C_ID + 32]),
                      [stage.b, cst.b], [pt.b])
            tk.op("dve", lambda e: e.tensor_copy(out=cols.t[:], in_=pt.t[:]), [pt.b], [cols.b])

            war = w_ada_d.rearrange("(kc k) n -> k kc n", k=128)
            for nb in range(18):
                s = nb % 4
                tk.dma(wa.t[:, s], war[:, :, nb * 512:(nb + 1) * 512], writes=[wa.bufs[s]])
                pb = nb % 2
                for kc in range(8):
                    tk.op("pe", lambda e, kc=kc, s=s, pb=pb: e.matmul(pm.t[:, pb, :], lhsT=cols.t[:, kc, 0:2],
                                                                     rhs=wa.t[:, s, kc, :], start=(kc == 0), stop=False),
                          [cols.b, wa.bufs[s]], [pm.bufs[pb]])
                tk.op("pe", lambda e, pb=pb, nb=nb: e.matmul(pm.t[:, pb, :], lhsT=cst.t[0:1, C_ONE:C_ONE + 2],
                                                            rhs=bada.t[0:1, nb * 512:(nb + 1) * 512], start=False, stop=True),
                      [cst.b, bada.b], [pm.bufs[pb]])
                tk.op("dve", lambda e, pb=pb, nb=nb: e.tensor_copy(out=mrow.t[:, nb * 512:(nb + 1) * 512], in_=pm.t[:, pb, :]),
                      [pm.bufs[pb]], [mrow.b])
            tk.dma(mrow_d, mrow.t[:], reads=[mrow.b], writes=[db("mrow")])
            for j in range(72):
                tk.op("pe", lambda e, j=j: e.transpose(out=pc.t[:, j, :], in_=mrow.t[0:2, j * 128:(j + 1) * 128],
                                                       identity=cst.t[0:2, C_ID:C_ID + 2]),
                      [mrow.b, cst.b], [pc.b])
            tk.op("dve", lambda e: e.tensor_copy(out=mcol.t[:], in_=pc.t[:]), [pc.b], [mcol.b])
            for s_ in range(3):
                for b_ in range(2):
                    tk.op("dve", lambda e, s_=s_, b_=b_: e.scalar_tensor_tensor(
                        out=modp.t[:, s_, b_, 0, :], in0=mcol.t[:, (3 * s_ + 1) * 8:(3 * s_ + 2) * 8, b_], scalar=1.0,
                        in1=cols.t[:, :, 2 + s_], op0=ALU.add, op1=ALU.mult), [mcol.b, cols.b], [modp.b])
                    tk.op("dve", lambda e, s_=s_, b_=b_: e.tensor_copy(
                        out=modp.t[:, s_, b_, 1, :], in_=mcol.t[:, (3 * s_) * 8:(3 * s_ + 1) * 8, b_]), [mcol.b], [modp.b])
            tk.barrier()

        def ffn_phase(tag, w_in_dram, w_out_dram, mset, gate_idx, blocks, final):
            with contextlib.ExitStack() as ph, nc.named_scope(tag):
                wi = sb(ph, tag + "wi", [128, 8, 2 * DFF], BF16)
                wo = sb(ph, tag + "wo", [128, NKF, D], BF16)
                gbs = sb(ph, tag + "gb", [128, 2, D], F32, nb=2)
                fnb = sb(ph, tag + "fnb", [128, D], F32) if final else None
                bs = set(b_ for (_, _, b_, _, _) in blocks)
                for b_ in sorted(bs):
                    src = mrow_d[b_:b_ + 1, gate_idx * D:(gate_idx + 1) * D]
                    tk.dma(gbs.t[:, b_, :], bass.AP(src.tensor, src.offset, [[0, 128], [1, D]]),
                           reads=[db("mrow")], writes=[gbs.bufs[b_]])
                    tk.op("pool", lambda e, b_=b_: e.tensor_scalar_mul(out=gbs.t[:, b_, :], in0=gbs.t[:, b_, :], scalar1=0.5),
                          [gbs.bufs[b_]], [gbs.bufs[b_]])
                if final:
                    tk.dma(fnb.t[:], bass.AP(final_norm_d.tensor, 0, [[0, 128], [1, D]]), writes=[fnb.b])
                with contextlib.ExitStack() as st:
                    stg = sb(st, tag + "stg", [128, 3, 2048], F32, nb=3)
                    pieces = []
                    wir = w_in_dram.rearrange("(kc k) n -> k kc n", k=128)
                    for kc in range(8):
                        for q4 in range(4):
                            pieces.append((wir[:, kc, q4 * 1408:(q4 + 1) * 1408], wi.t[:, kc, q4 * 1408:(q4 + 1) * 1408], 1408, wi.b))
                    wor = w_out_dram.rearrange("(kc k) n -> k kc n", k=128)
                    for kc in range(NKF):
                        pieces.append((wor[:, kc, :], wo.t[:, kc, :], 1024, wo.b))
                    cast_eng = ["dve", "pool", "act"]
                    for i, (src, dst, n, wb) in enumerate(pieces):
                        s = i % 3
                        tk.dma(stg.t[:, s, 0:n], src, writes=[stg.bufs[s]])
                        ce = cast_eng[i % 3]
                        if ce == "act":
                            tk.op("act", lambda e, s=s, n=n, dst=dst: e.copy(out=dst, in_=stg.t[:, s, 0:n]), [stg.bufs[s]], [wb])
                        else:
                            tk.op(ce, lambda e, s=s, n=n, dst=dst: e.tensor_copy(out=dst, in_=stg.t[:, s, 0:n]), [stg.bufs[s]], [wb])
                    tk.barrier()
                xb = sb(ph, tag + "xb", [128, 2, 2, D], F32, nb=2)
                xn = sb(ph, tag + "xn", [128, 2, 2, D], BF16, nb=2)
                hT = sb(ph, tag + "hT", [128, 2, 8, 256], BF16, nb=2)
                tmpm = sb(ph, tag + "tmpm", [128, 8, 128], F32)
                gsb = sb(ph, tag + "g", [128, NKF, 256], BF16)
                sg = sb(ph, tag + "sg", [128, 2, 512], F32, nb=2)
                tt = sb(ph, tag + "tt", [128, 2, 512], F32, nb=2)
                junk = sb(ph, tag + "junk", [128, D], BF16)
                ss = sb(ph, tag + "ss", [128, 2, 8], F32, nb=2)
                pG = ps(ph, tag + "pG", [128, 2, 512], F32, nb=2)
                pU = ps(ph, tag + "pU", [128, 2, 512], F32, nb=2)
                pY = ps(ph, tag + "pY", [128, 2, 512], F32, nb=2)
                pT = ps(ph, tag + "pT", [128, 2, 8, 128], BF16, nb=2)
                nblk = len(blocks)

                def load(i):
                    src, _, _, sbuf, _ = blocks[i]
                    s = i % 2
                    tk.dma(xb.t[:, s], src.rearrange("(t p) d -> p t d", p=128), reads=[sbuf], writes=[xb.bufs[s]])

                def norm(i):
                    s = i % 2
                    sbf = ss.bufs[s]
                    for t in range(2):
                        tk.op("act", lambda e, t=t: e.activation(out=junk.t[:], in_=xb.t[:, s, t, :], func=AF.Square,
                                                                  accum_out=ss.t[:, s, t:t + 1]), [xb.bufs[s]], [junk.b, sbf])
                    tk.op("dve", lambda e: e.tensor_scalar(out=ss.t[:, s, 2:4], in0=ss.t[:, s, 0:2], scalar1=1.0 / D, scalar2=EPS,
                                                           op0=ALU.mult, op1=ALU.add), [sbf], [sbf])
                    tk.op("act", lambda e: e.sqrt(out=ss.t[:, s, 4:6], in_=ss.t[:, s, 2:4]), [sbf], [sbf])
                    tk.op("dve", lambda e: e.reciprocal(out=ss.t[:, s, 6:8], in_=ss.t[:, s, 4:6]), [sbf], [sbf])
                    for t in range(2):
                        tk.op("dve", lambda e, t=t: e.tensor_scalar_mul(out=xn.t[:, s, t, :], in0=xb.t[:, s, t, :],
                                                                        scalar1=ss.t[:, s, 6 + t:7 + t]),
                              [xb.bufs[s], sbf], [xn.bufs[s]])

                def transp(i):
                    s = i % 2
                    b_ = blocks[i][2]
                    for t in range(2):
                        for kc in range(8):
                            tk.op("pe", lambda e, t=t, kc=kc: e.transpose(out=pT.t[:, t, kc, :], in_=xn.t[:, s, t, kc * 128:(kc + 1) * 128],
                                                                            identity=idb.t[:]), [xn.bufs[s], idb.b], [pT.bufs[t]])
                        G = modp.t[:, mset, b_, 0, :].unsqueeze(2).to_broadcast([128, 8, 128])
                        S = modp.t[:, mset, b_, 1, :].unsqueeze(2).to_broadcast([128, 8, 128])
                        tk.op("dve", lambda e, t=t, G=G: e.tensor_tensor(out=tmpm.t[:], in0=pT.t[:, t], in1=G, op=ALU.mult),
                              [pT.bufs[t], modp.b], [tmpm.b])
                        tk.op("pool", lambda e, t=t, S=S: e.tensor_tensor(out=hT.t[:, s, :, t * 128:(t + 1) * 128], in0=tmpm.t[:], in1=S, op=ALU.add),
                              [tmpm.b, modp.b], [hT.bufs[s]])

                def ffn_in(i):
                    s = i % 2
                    for fp in range(NKF // 2):
                        pb = fp % 2
                        for j in range(2):
                            fc = fp * 2 + j
                            for kc in range(8):
                                tk.op("pe", lambda e, j=j, fc=fc, kc=kc: e.matmul(pG.t[:, pb, j * 256:(j + 1) * 256], lhsT=wi.t[:, kc, fc * 128:(fc + 1) * 128],
                                                                                   rhs=hT.t[:, s, kc, :], start=(kc == 0), stop=(kc == 7)),
                                      [wi.b, hT.bufs[s]], [pG.bufs[pb]])
                            for kc in range(8):
                                tk.op("pe", lambda e, j=j, fc=fc, kc=kc: e.matmul(pU.t[:, pb, j * 256:(j + 1) * 256], lhsT=wi.t[:, kc, DFF + fc * 128:DFF + (fc + 1) * 128],
                                                                                   rhs=hT.t[:, s, kc, :], start=(kc == 0), stop=(kc == 7)),
                                      [wi.b, hT.bufs[s]], [pU.bufs[pb]])
                        tk.op("act", lambda e: e.activation(out=sg.t[:, pb, :], in_=pG.t[:, pb, :], func=AF.Silu), [pG.bufs[pb]], [sg.bufs[pb]])
                        tk.op("dve", lambda e, fp=fp: e.tensor_tensor(out=gsb.t[:, fp * 2:fp * 2 + 2, :], in0=sg.t[:, pb, :].rearrange("p (a b) -> p a b", a=2),
                                                                      in1=pU.t[:, pb, :].rearrange("p (a b) -> p a b", a=2), op=ALU.mult),
                              [sg.bufs[pb], pU.bufs[pb]], [gsb.b])

                def ffn_out(i):
                    s = i % 2
                    _, dst, b_, _, dbuf = blocks[i]
                    for t in range(2):
                        for h in range(2):
                            for kc in range(NKF):
                                tk.op("pe", lambda e, t=t, h=h, kc=kc: e.matmul(pY.t[:, h, :], lhsT=gsb.t[:, kc, t * 128:(t + 1) * 128],
                                                                                 rhs=wo.t[:, kc, h * 512:(h + 1) * 512], start=(kc == 0), stop=(kc == NKF - 1)),
                                      [gsb.b, wo.b], [pY.bufs[h]])
                            tk.op("dve", lambda e, h=h: e.tensor_tensor(out=tt.t[:, h, :], in0=pY.t[:, h, :], in1=gbs.t[:, b_, h * 512:(h + 1) * 512], op=ALU.mult),
                                  [pY.bufs[h], gbs.bufs[b_]], [tt.bufs[h]])
                            tk.op("pool", lambda e, t=t, h=h: e.tensor_tensor(out=xb.t[:, s, t, h * 512:(h + 1) * 512], in0=xb.t[:, s, t, h * 512:(h + 1) * 512],
                                                                               in1=tt.t[:, h, :], op=ALU.add), [tt.bufs[h], xb.bufs[s]], [xb.bufs[s]])
                    if final:
                        sbf = ss.bufs[s]
                        for t in range(2):
                            tk.op("act", lambda e, t=t: e.activation(out=junk.t[:], in_=xb.t[:, s, t, :], func=AF.Square,
                                                                      accum_out=ss.t[:, s, t:t + 1]), [xb.bufs[s]], [junk.b, sbf])
                        tk.op("dve", lambda e: e.tensor_scalar(out=ss.t[:, s, 2:4], in0=ss.t[:, s, 0:2], scalar1=1.0 / D, scalar2=EPS,
                                                               op0=ALU.mult, op1=ALU.add), [sbf], [sbf])
                        tk.op("act", lambda e: e.sqrt(out=ss.t[:, s, 4:6], in_=ss.t[:, s, 2:4]), [sbf], [sbf])
                        tk.op("dve", lambda e: e.reciprocal(out=ss.t[:, s, 6:8], in_=ss.t[:, s, 4:6]), [sbf], [sbf])
                        for t in range(2):
                            tk.op("dve", lambda e, t=t: e.scalar_tensor_tensor(out=xb.t[:, s, t, :], in0=xb.t[:, s, t, :], scalar=ss.t[:, s, 6 + t:7 + t],
                                                                               in1=fnb.t[:], op0=ALU.mult, op1=ALU.mult),
                                  [xb.bufs[s], sbf, fnb.b], [xb.bufs[s]])
                    tk.dma(dst.rearrange("(t p) d -> p t d", p=128), xb.t[:, s], reads=[xb.bufs[s]], writes=[dbuf])

                load(0)
                if nblk > 1:
                    load(1)
                norm(0)
                transp(0)
                for i in range(nblk):
                    ffn_in(i)
                    if i + 1 < nblk:
                        norm(i + 1)
                        transp(i + 1)
                    ffn_out(i)
                    if i + 2 < nblk:
                        load(i + 2)
                tk.barrier()

        blocks1 = [(ctx_d, x1_d[0:256, :], 1, Buf(), db("x1"))]
        for i in range(L // 256):
            blocks1.append((x_d[i * 256:(i + 1) * 256, :], x1_d[CL + i * 256:CL + (i + 1) * 256, :], 0, Buf(), db("x1")))
        if debug and "only_ctx" in debug:
            blocks1 = blocks1[:3]
        ffn_phase("f1", w_ffn1_in_d, w_ffn1_out_d, 0, 2, blocks1, False)


        def phase1b():
            with contextlib.ExitStack() as ph, nc.named_scope("p1b"):
                wp = sb(ph, "wp", [128, 8, 2624], BF16)
                wq = sb(ph, "wq", [128, 6, 8, 96], BF16)
                wqs = sb(ph, "wqs", [128, 6, 8, 96], BF16)
                wkv = sb(ph, "wkv", [128, 2, 1024], BF16)
                wkp = sb(ph, "wkp", [128, 8, 2, 96], BF16)
                dtb = sb(ph, "dtb", [128, 16], F32)
                with contextlib.ExitStack() as st:
                    stg = sb(st, "stg1", [128, 3, IN_DIM], F32, nb=3)
                    kst = sb(st, "kst", [128, 8, 2, 32], F32)
                    wir = w_in_d.rearrange("(kc k) n -> k kc n", k=128)
                    ce = ["dve", "pool"]
                    n = 0
                    for kc in range(8):
                        s = n % 3
                        tk.dma(stg.t[:, s, :], wir[:, kc, :], writes=[stg.bufs[s]])
                        tk.op(ce[n % 2], lambda e, s=s, kc=kc: e.tensor_copy(out=wp.t[:, kc, 0:1552], in_=stg.t[:, s, 0:1552]), [stg.bufs[s]], [wp.b])
                        tk.op(ce[(n + 1) % 2], lambda e, s=s, kc=kc: e.tensor_copy(out=wp.t[:, kc, 1600:2624], in_=stg.t[:, s, 1552:2576]), [stg.bufs[s]], [wp.b])
                        n += 1
                    tk.op("pool", lambda e: e.memset(wkp.t[:], 0.0), [], [wkp.b])
                    tk.dma(kst.t[:, :, 0, :], wir[:, :, 2576:2608], writes=[kst.b])
                    for half in range(2):
                        o = 2576 + 16 * half
                        tk.dma(kst.t[:, :, 1, 16 * half:16 * half + 8], wir[:, :, o + 8:o + 16], writes=[kst.b])
                        tk.dma(kst.t[:, :, 1, 16 * half + 8:16 * half + 16], wir[:, :, o:o + 8], writes=[kst.b])
                    tk.op("dve", lambda e: e.tensor_copy(out=wkp.t[:, :, :, 64:96], in_=kst.t[:]), [kst.b], [wkp.b])
                    wur = w_uq_d.rearrange("(kc k) n -> k kc n", k=128)
                    for kc in range(6):
                        s = n % 3
                        tk.dma(stg.t[:, s, 0:768], wur[:, kc, :], writes=[stg.bufs[s]])
                        tk.op("dve", lambda e, s=s, kc=kc: e.tensor_scalar_mul(out=wq.t[:, kc].rearrange("p h d -> p (h d)"), in0=stg.t[:, s, 0:768],
                                                                               scalar1=cols.t[:, kc, 11:12]), [stg.bufs[s], cols.b], [wq.b])
                        n += 1
                    tk.op("pool", lambda e: e.tensor_copy(out=wqs.t[:], in_=wq.t[:]), [wq.b], [wqs.b])
                    for half in range(2):
                        o = 64 + 16 * half
                        tk.op("dve", lambda e, o=o: e.tensor_copy(out=wqs.t[:, :, :, o:o + 8], in_=wq.t[:, :, :, o + 8:o + 16]), [wq.b], [wqs.b])
                        tk.op("dve", lambda e, o=o: e.tensor_copy(out=wqs.t[:, :, :, o + 8:o + 16], in_=wq.t[:, :, :, o:o + 8]), [wq.b], [wqs.b])
                    wkr = w_ukv_d.rearrange("(kc k) n -> k kc n", k=128)
                    for kc in range(2):
                        s = n % 3
                        tk.dma(stg.t[:, s, 0:1024], wkr[:, kc, :], writes=[stg.bufs[s]])
                        src = stg.t[:, s, 0:1024].rearrange("p (h two d) -> p two h d", h=8, two=2)
                        tk.op("dve", lambda e, src=src, kc=kc: e.tensor_scalar_mul(out=wkv.t[:, kc, :].rearrange("p (two h d) -> p two h d", two=2, h=8), in0=src,
                                                                                  scalar1=cols.t[:, kc, 12:13]), [stg.bufs[s], cols.b], [wkv.b])
                        n += 1
                    tk.dma(dtb.t[:], bass.AP(dt_bias_d.tensor, 0, [[0, 128], [1, 16]]), writes=[dtb.b])
                    tk.barrier()
                if debug and 'stop0' in debug:
                    return
                NB = 512
                xb = sb(ph, "bxb", [128, 2, 4, D], F32, nb=2)
                xn = sb(ph, "bxn", [128, 4, D], BF16)
                hT = sb(ph, "bhT", [128, 8, NB], BF16)
                tmpm = sb(ph, "btmpm", [128, 8, 128], F32)
                junk = sb(ph, "bjunk", [128, D], BF16)
                ss = sb(ph, "bss", [128, 16], F32)
                zt = sb(ph, "bzt", [128, 4, 512], BF16)
                xo = sb(ph, "bxo", [128, 8, NB], BF16)
                cq = sb(ph, "bcq", [128, 6, NB], BF16)
                cqn = sb(ph, "bcqn", [128, 6, NB], BF16)
                ckv = sb(ph, "bckv", [128, 2, NB], BF16)
                ckvn = sb(ph, "bckvn", [128, 2, NB], BF16)
                sq = sb(ph, "bsq", [128, 8, NB], BF16)
                rb = sb(ph, "brb", [128, 2, 2, NB], F32, nb=2)
                qo = sb(ph, "bqo", [96, 8, NB], BF16)
                ko = sb(ph, "bko", [128, 4, NB], BF16)
                kpo = sb(ph, "bkpo", [96, NB], BF16)
                vo = sb(ph, "bvo", [128, 4, 512], BF16)
                rp = sb(ph, "brp", [96, 2, NB], F32)
                rt = sb(ph, "brt", [96, 2, 2, NB], F32, nb=2)
                dtt = sb(ph, "bdtt", [128, 4, 16], F32)
                pA = ps(ph, "pA", [128, 4, 512], F32, nb=4)
                pT = ps(ph, "pT1", [128, 8, 128], BF16)
                pS = ps(ph, "pS", [128, 512], F32)
                pD = ps(ph, "pD", [128, 4, 16], F32)
                pai = [0]

                def nxt():
                    pai[0] = (pai[0] + 1) % 4
                    return pai[0]

                blocks = [(x1_d[0:CL, :], CL, 1, 0)] + [(x1_d[CL + i * NB:CL + (i + 1) * NB, :], NB, 0, CL + i * NB) for i in range(L // NB)]
                if debug and "only_ctx" in debug:
                    blocks = blocks[:2]
                def load_b(bi):
                    src_, N_, _, _ = blocks[bi]
                    tk.dma(xb.t[:, bi % 2, 0:N_ // 128], src_.rearrange("(t p) d -> p t d", p=128), reads=[db("x1")], writes=[xb.bufs[bi % 2]])

                load_b(0)
                for bi, (src, N, b_, tok0) in enumerate(blocks):
                    nt = N // 128
                    s = bi % 2
                    if bi + 1 < len(blocks):
                        load_b(bi + 1)
                    if b_ == 0:
                        tk.dma(rp.t[64:96, :, 0:N], rope_d[:, :, tok0 - CL:tok0 - CL + N].rearrange("c d n -> d c n"), writes=[rp.b])
                    for t in range(nt):
                        tk.op("act", lambda e, t=t: e.activation(out=junk.t[:], in_=xb.t[:, s, t, :], func=AF.Square, accum_out=ss.t[:, t:t + 1]),
                              [xb.bufs[s]], [junk.b, ss.b])
                    tk.op("dve", lambda e: e.tensor_scalar(out=ss.t[:, 4:8], in0=ss.t[:, 0:4], scalar1=1.0 / D, scalar2=EPS, op0=ALU.mult, op1=ALU.add), [ss.b], [ss.b])
                    tk.op("act", lambda e: e.sqrt(out=ss.t[:, 8:12], in_=ss.t[:, 4:8]), [ss.b], [ss.b])
                    tk.op("dve", lambda e: e.reciprocal(out=ss.t[:, 12:16], in_=ss.t[:, 8:12]), [ss.b], [ss.b])
                    G = modp.t[:, 1, b_, 0, :].unsqueeze(2).to_broadcast([128, 8, 128])
                    S = modp.t[:, 1, b_, 1, :].unsqueeze(2).to_broadcast([128, 8, 128])
                    for t in range(nt):
                        tk.op("dve", lambda e, t=t: e.tensor_scalar_mul(out=xn.t[:, t, :], in0=xb.t[:, s, t, :], scalar1=ss.t[:, 12 + t:13 + t]),
                              [xb.bufs[s], ss.b], [xn.b])
                        for kc in range(8):
                            tk.op("pe", lambda e, t=t, kc=kc: e.transpose(out=pT.t[:, kc, :], in_=xn.t[:, t, kc * 128:(kc + 1) * 128], identity=idb.t[:]),
                                  [xn.b, idb.b], [pT.b])
                        tk.op("dve", lambda e: e.tensor_tensor(out=tmpm.t[:], in0=pT.t[:], in1=G, op=ALU.mult), [pT.b, modp.b], [tmpm.b])
                        tk.op("pool", lambda e, t=t: e.tensor_tensor(out=hT.t[:, :, t * 128:(t + 1) * 128], in0=tmpm.t[:], in1=S, op=ALU.add),
                              [tmpm.b, modp.b], [hT.b])

                    def fm(lhs_fn, nk, rhs_fn, M, rd):
                        a = nxt()
                        for kc in range(nk):
                            tk.op("pe", lambda e, kc=kc: e.matmul(pA.t[0:M, a, 0:N], lhsT=lhs_fn(kc), rhs=rhs_fn(kc), start=(kc == 0), stop=(kc == nk - 1)),
                                  rd, [pA.bufs[a]])
                        return a

                    if debug and 'stop1' in debug:
                        continue
                    if b_ == 0:
                        for t in range(nt):
                            a = fm(lambda kc, t=t: hT.t[:, kc, t * 128:(t + 1) * 128], 8, lambda kc: wp.t[:, kc, 0:512], 128, [hT.b, wp.b])
                            tk.op("act", lambda e, t=t, a=a: e.copy(out=zt.t[:, t, :], in_=pA.t[:, a, :]), [pA.bufs[a]], [zt.b])
                        r0 = tok0 - CL
                        tk.dma(z_d[r0:r0 + N, :].rearrange("(t p) d -> p t d", p=128), zt.t[:, 0:nt], reads=[zt.b], writes=[db("z")])
                    for cc in range(8):
                        a = fm(lambda kc, cc=cc: wp.t[:, kc, 512 + cc * 128:512 + (cc + 1) * 128], 8, lambda kc: hT.t[:, kc, 0:N], 128, [hT.b, wp.b])
                        tk.op("act" if cc % 2 else "dve",
                              (lambda e, cc=cc, a=a: e.copy(out=xo.t[:, cc, 0:N], in_=pA.t[:, a, 0:N])) if cc % 2 else
                              (lambda e, cc=cc, a=a: e.tensor_copy(out=xo.t[:, cc, 0:N], in_=pA.t[:, a, 0:N])), [pA.bufs[a]], [xo.b])
                    tk.dma(xbcT_d[:, :, tok0:tok0 + N].rearrange("c p n -> p c n"), xo.t[:, :, 0:N], reads=[xo.b], writes=[db("xbcT")])
                    if debug and 'stop2' in debug:
                        continue
                    for t in range(nt):
                        for kc in range(8):
                            tk.op("pe", lambda e, t=t, kc=kc: e.matmul(pD.t[:, t, :], lhsT=hT.t[:, kc, t * 128:(t + 1) * 128], rhs=wp.t[:, kc, 1536:1552],
                                                                       start=(kc == 0), stop=(kc == 7)), [hT.b, wp.b], [pD.b])
                    tg = tok0 // 128
                    tk.op("dve", lambda e: e.tensor_tensor(out=dtt.t[:, 0:nt, :], in0=pD.t[:, 0:nt, :], in1=dtb.t[:].unsqueeze(1).to_broadcast([128, nt, 16]), op=ALU.add),
                          [pD.b, dtb.b], [dtt.b])
                    tk.op("act", lambda e: e.activation(out=dtt.t[:, 0:nt, :], in_=dtt.t[:, 0:nt, :], func=AF.Exp), [dtt.b], [dtt.b])
                    tk.op("act", lambda e: e.activation(out=dtv.t[:, tg:tg + nt, :], in_=dtt.t[:, 0:nt, :], func=AF.Ln, bias=1.0), [dtt.b], [dtv.b])
                    if debug and 'stop3' in debug:
                        continue
                    for (dst, dstn, nk, c0, dim, ri) in ((cq, cqn, 6, 1600, 768.0, 0), (ckv, ckvn, 2, 2368, 256.0, 1)):
                        for c_ in range(nk):
                            a = fm(lambda kc, c_=c_: wp.t[:, kc, c0 + c_ * 128:c0 + (c_ + 1) * 128], 8, lambda kc: hT.t[:, kc, 0:N], 128, [hT.b, wp.b])
                            tk.op("dve", lambda e, c_=c_, a=a: e.tensor_copy(out=dst.t[:, c_, 0:N], in_=pA.t[:, a, 0:N]), [pA.bufs[a]], [dst.b])
                            tk.op("pool", lambda e, c_=c_: e.tensor_tensor(out=sq.t[:, c_, 0:N], in0=dst.t[:, c_, 0:N], in1=dst.t[:, c_, 0:N], op=ALU.mult), [dst.b], [sq.b])
                        if debug and 'stop3a' in debug:
                            continue
                        for c_ in range(nk):
                            tk.op("pe", lambda e, c_=c_: e.matmul(pS.t[:, 0:N], lhsT=onb.t[:], rhs=sq.t[:, c_, 0:N], start=(c_ == 0), stop=(c_ == nk - 1)),
                                  [onb.b, sq.b], [pS.b])
                        if debug and 'stop3b' in debug:
                            continue
                        tk.op("dve", lambda e: e.tensor_scalar(out=rb.t[:, ri, 0, 0:N], in0=pS.t[:, 0:N], scalar1=1.0 / dim, scalar2=EPS, op0=ALU.mult, op1=ALU.add),
                              [pS.b], [rb.bufs[ri]])
                        tk.op("act", lambda e: e.sqrt(out=rb.t[:, ri, 0, 0:N], in_=rb.t[:, ri, 0, 0:N]), [rb.bufs[ri]], [rb.bufs[ri]])
                        tk.op("dve", lambda e: e.reciprocal(out=rb.t[:, ri, 1, 0:N], in_=rb.t[:, ri, 0, 0:N]), [rb.bufs[ri]], [rb.bufs[ri]])
                        tk.op("dve", lambda e: e.tensor_tensor(out=dstn.t[:, :, 0:N], in0=dst.t[:, :, 0:N],
                                                                in1=rb.t[:, ri, 1, 0:N].unsqueeze(1).to_broadcast([128, nk, N]), op=ALU.mult),
                              [dst.b, rb.bufs[ri]], [dstn.b])
                    if debug and 'stop4' in debug:
                        continue
                    a1 = fm(lambda kc: wkp.t[:, kc, 0, :], 8, lambda kc: hT.t[:, kc, 0:N], 96, [hT.b, wkp.b])
                    if b_ == 0:
                        a2 = fm(lambda kc: wkp.t[:, kc, 1, :], 8, lambda kc: hT.t[:, kc, 0:N], 96, [hT.b, wkp.b])
                        tk.op("dve", lambda e: e.tensor_tensor(out=rt.t[64:96, 0, 0, 0:N], in0=pA.t[64:96, a1, 0:N], in1=rp.t[64:96, 0, 0:N], op=ALU.mult),
                              [pA.bufs[a1], rp.b], [rt.bufs[0]])
                        tk.op("dve", lambda e: e.tensor_tensor(out=rt.t[64:96, 0, 1, 0:N], in0=pA.t[64:96, a2, 0:N], in1=rp.t[64:96, 1, 0:N], op=ALU.mult),
                              [pA.bufs[a2], rp.b], [rt.bufs[0]])
                        tk.op("pool", lambda e: e.tensor_tensor(out=kpo.t[64:96, 0:N], in0=rt.t[64:96, 0, 0, 0:N], in1=rt.t[64:96, 0, 1, 0:N], op=ALU.add),
                              [rt.bufs[0]], [kpo.b])
                    else:
                        tk.op("dve", lambda e: e.tensor_copy(out=kpo.t[64:96, 0:N], in_=pA.t[64:96, a1, 0:N]), [pA.bufs[a1]], [kpo.b])
                    for h in range(8):
                        tk.dma(kT_d[h, 64:96, tok0:tok0 + N], kpo.t[64:96, 0:N], reads=[kpo.b], writes=[db("kT")])
                    if debug and 'stop5' in debug:
                        continue
                    for hp in range(4):
                        a = fm(lambda kc, hp=hp: wkv.t[:, kc, hp * 128:(hp + 1) * 128], 2, lambda kc: ckvn.t[:, kc, 0:N], 128, [ckvn.b, wkv.b])
                        tk.op("act", lambda e, hp=hp, a=a: e.copy(out=ko.t[:, hp, 0:N], in_=pA.t[:, a, 0:N]), [pA.bufs[a]], [ko.b])
                    for h in range(8):
                        tk.dma(kT_d[h, 0:64, tok0:tok0 + N], ko.t[(h % 2) * 64:(h % 2) * 64 + 64, h // 2, 0:N], reads=[ko.b], writes=[db("kT")])
                    for t in range(nt):
                        a = nxt()
                        for kc in range(2):
                            tk.op("pe", lambda e, t=t, kc=kc, a=a: e.matmul(pA.t[:, a, :], lhsT=ckvn.t[:, kc, t * 128:(t + 1) * 128], rhs=wkv.t[:, kc, 512:1024],
                                                                            start=(kc == 0), stop=(kc == 1)), [ckvn.b, wkv.b], [pA.bufs[a]])
                        tk.op("dve", lambda e, t=t, a=a: e.tensor_copy(out=vo.t[:, t, :], in_=pA.t[:, a, :]), [pA.bufs[a]], [vo.b])
                    tk.dma(V_d[tok0:tok0 + N, :].rearrange("(t p) d -> p t d", p=128), vo.t[:, 0:nt], reads=[vo.b], writes=[db("V")])
                    if b_ == 0:
                        for h in range(8):
                            a1 = fm(lambda kc, h=h: wq.t[:, kc, h, :], 6, lambda kc: cqn.t[:, kc, 0:N], 96, [cqn.b, wq.b])
                            a2 = fm(lambda kc, h=h: wqs.t[:, kc, h, :], 6, lambda kc: cqn.t[:, kc, 0:N], 96, [cqn.b, wqs.b])
                            tk.op("dve", lambda e, h=h, a1=a1: e.tensor_copy(out=qo.t[0:64, h, 0:N], in_=pA.t[0:64, a1, 0:N]), [pA.bufs[a1]], [qo.b])
                            r = h % 2
                            tk.op("dve", lambda e, a1=a1, r=r: e.tensor_tensor(out=rt.t[64:96, r, 0, 0:N], in0=pA.t[64:96, a1, 0:N], in1=rp.t[64:96, 0, 0:N], op=ALU.mult),
                                  [pA.bufs[a1], rp.b], [rt.bufs[r]])
                            tk.op("dve", lambda e, a2=a2, r=r: e.tensor_tensor(out=rt.t[64:96, r, 1, 0:N], in0=pA.t[64:96, a2, 0:N], in1=rp.t[64:96, 1, 0:N], op=ALU.mult),
                                  [pA.bufs[a2], rp.b], [rt.bufs[r]])
                            tk.op("pool", lambda e, h=h, r=r: e.tensor_tensor(out=qo.t[64:96, h, 0:N], in0=rt.t[64:96, r, 0, 0:N], in1=rt.t[64:96, r, 1, 0:N], op=ALU.add),
                                  [rt.bufs[r]], [qo.b])
                        r0 = tok0 - CL
                        tk.dma(qT_d[:, :, r0:r0 + N].rearrange("h d n -> d h n"), qo.t[:, :, 0:N], reads=[qo.b], writes=[db("qT")])
                tk.barrier()

        z_d = dscr("z", [L, 512], BF16)
        xbcT_d = dscr("xbcT", [8, 128, T], BF16)
        kT_d = dscr("kT", [8, 96, T], BF16)
        qT_d = dscr("qT", [8, 96, L], BF16)
        V_d = dscr("V", [T, 512], BF16)
        mid = contextlib.ExitStack()
        dtv = sb(mid, "dtv", [128, NT, 16], F32)
        onb = sb(mid, "onb", [128, 128], BF16)
        tk.op("dve", lambda e: e.tensor_copy(out=onb.t[:], in_=cst.t[:, C_ONE:C_ONE + 128]), [cst.b], [onb.b])
        dtv_d = dscr("dtv_dbg", [128, NT, 16], F32)
        phase1b()
        if debug and "dtv_dbg" in debug:
            tk.dma(dtv_d, dtv.t[:], reads=[dtv.b], writes=[db("dtvd")])


        mixT_d = dscr("mixT", [8, 128, L], BF16)

        def phase2b():
            with contextlib.ExitStack() as ph, nc.named_scope("p2b"):
                kT = sb(ph, "akT", [128, 2, T], BF16, nb=2)
                qT = sb(ph, "aqT", [128, 2, L], BF16, nb=2)
                Vh = sb(ph, "aV", [128, 2, NT, 128], BF16, nb=2)
                P = sb(ph, "aP", [128, 2, 2, 512], BF16, nb=2)
                rc = sb(ph, "arc", [65, 2, 512], F32, nb=2)
                bc = sb(ph, "abc", [64, 2, 512], F32, nb=2)
                on = sb(ph, "aon", [64, 2, 512], BF16, nb=2)
                pS_ = ps(ph, "aS", [128, 2, 2, 512], F32, nb=2)
                pO = ps(ph, "aO", [128, 2, 512], F32, nb=2)
                pB = ps(ph, "aB", [64, 512], F32)
                for s in range(2):
                    tk.op("pool", lambda e, s=s: e.memset(Vh.t[:, s], 0.0), [], [Vh.bufs[s]])
                    tk.op("pool", lambda e, s=s: e.memset(Vh.t[:, s, :, 64:65], 1.0), [], [Vh.bufs[s]])
                    tk.op("dve", lambda e, s=s: e.memset(kT.t[:, s, :], 0.0), [], [kT.bufs[s]])
                    tk.op("dve", lambda e, s=s: e.memset(qT.t[:, s, :], 0.0), [], [qT.bufs[s]])
                Vr = V_d.rearrange("(t p) (h d) -> p t h d", p=128, h=8)
                nh = 8
                nqb = 1 if (debug and "only_ctx" in debug) else 8
                for h in range(nh):
                    s = h % 2
                    tk.dma(kT.t[0:96, s, :], kT_d[h], reads=[db("kT")], writes=[kT.bufs[s]])
                    tk.dma(qT.t[0:96, s, :], qT_d[h], reads=[db("qT")], writes=[qT.bufs[s]])
                    tk.dma(Vh.t[:, s, :, 0:64], Vr[:, :, h, :], reads=[db("V")], writes=[Vh.bufs[s]])
                    for qb in range(nqb):
                        o = (h * 8 + qb) % 2

                        def S(kp):
                            sl = kp % 2
                            for j in range(2):
                                kt = kp * 2 + j
                                tk.op("pe", lambda e, j=j, kt=kt: e.matmul(pS_.t[:, sl, j, :], lhsT=kT.t[:, s, kt * 128:(kt + 1) * 128],
                                                                           rhs=qT.t[:, s, qb * 512:(qb + 1) * 512], start=True, stop=True),
                                      [kT.bufs[s], qT.bufs[s]], [pS_.bufs[sl]])
                        S(0)
                        for kp in range(NT // 2):
                            sl = kp % 2
                            if kp + 1 < NT // 2:
                                S(kp + 1)
                            tk.op("act", lambda e: e.activation(out=P.t[:, sl], in_=pS_.t[:, sl], func=AF.Exp, scale=SCALE), [pS_.bufs[sl]], [P.bufs[sl]])
                            for j in range(2):
                                kt = kp * 2 + j
                                tk.op("pe", lambda e, j=j, kt=kt: e.matmul(pO.t[:, o, :], lhsT=Vh.t[:, s, kt, :], rhs=P.t[:, sl, j, :],
                                                                           start=(kt == 0), stop=(kt == NT - 1)), [Vh.bufs[s], P.bufs[sl]], [pO.bufs[o]])
                        tk.op("dve", lambda e: e.reciprocal(out=rc.t[64:65, o, :], in_=pO.t[64:65, o, :]), [pO.bufs[o]], [rc.bufs[o]])
                        tk.op("pe", lambda e: e.matmul(pB.t[:], lhsT=cst.t[64:65, C_ONE:C_ONE + 64], rhs=rc.t[64:65, o, :], start=True, stop=True),
                              [cst.b, rc.bufs[o]], [pB.b])
                        tk.op("act", lambda e: e.copy(out=bc.t[:, o, :], in_=pB.t[:]), [pB.b], [bc.bufs[o]])
                        tk.op("dve", lambda e: e.tensor_tensor(out=on.t[:, o, :], in0=pO.t[0:64, o, :], in1=bc.t[:, o, :], op=ALU.mult),
                              [pO.bufs[o], bc.bufs[o]], [on.bufs[o]])
                        tk.dma(mixT_d[4 + h // 2, (h % 2) * 64:(h % 2) * 64 + 64, qb * 512:(qb + 1) * 512], on.t[:, o, :], reads=[on.bufs[o]], writes=[db("mixT")])
                tk.barrier()

        def phase2c():
            with contextlib.ExitStack() as ph, nc.named_scope("p2c"):
                wo = sb(ph, "cwo", [128, 8, D], BF16)
                stg = sb(ph, "cstg", [128, 2, D], F32, nb=2)
                gb = sb(ph, "cgb", [128, D], F32)
                mx = sb(ph, "cmx", [128, 2, 8, 256], BF16, nb=2)
                xb = sb(ph, "cxb", [128, 2, 2, D], F32, nb=2)
                tt = sb(ph, "ctt", [128, 2, 512], F32, nb=2)
                pY = ps(ph, "cpY", [128, 2, 512], F32, nb=2)
                wor = w_out_d.rearrange("(kc k) n -> k kc n", k=128)
                for kc in range(8):
                    s = kc % 2
                    tk.dma(stg.t[:, s, :], wor[:, kc, :], writes=[stg.bufs[s]])
                    if kc < 4:
                        tk.op("dve", lambda e, s=s, kc=kc: e.tensor_scalar_mul(out=wo.t[:, kc, :], in0=stg.t[:, s, :], scalar1=cols.t[:, kc, 13:14]),
                              [stg.bufs[s], cols.b], [wo.b])
                    else:
                        tk.op("dve", lambda e, s=s, kc=kc: e.tensor_copy(out=wo.t[:, kc, :], in_=stg.t[:, s, :]), [stg.bufs[s]], [wo.b])
                src = mrow_d[0:1, 5 * D:6 * D]
                tk.dma(gb.t[:], bass.AP(src.tensor, src.offset, [[0, 128], [1, D]]), reads=[db("mrow")], writes=[gb.b])
                nblk = 2 if (debug and "only_ctx" in debug) else L // 256
                def rows_of(bi):
                    return x1_d[CL + bi * 256:CL + (bi + 1) * 256, :].rearrange("(t p) d -> p t d", p=128)

                def load_c(bi):
                    s = bi % 2
                    tk.dma(mx.t[:, s], mixT_d[:, :, bi * 256:(bi + 1) * 256].rearrange("c p n -> p c n"), reads=[db("mixT")], writes=[mx.bufs[s]])
                    tk.dma(xb.t[:, s], rows_of(bi), reads=[Buf()], writes=[xb.bufs[s]])

                load_c(0)
                for bi in range(nblk):
                    s = bi % 2
                    rows = rows_of(bi)
                    if bi + 1 < nblk:
                        load_c(bi + 1)
                    for t in range(2):
                        for hf in range(2):
                            for kc in range(8):
                                tk.op("pe", lambda e, t=t, hf=hf, kc=kc: e.matmul(pY.t[:, hf, :], lhsT=mx.t[:, s, kc, t * 128:(t + 1) * 128],
                                                                                  rhs=wo.t[:, kc, hf * 512:(hf + 1) * 512], start=(kc == 0), stop=(kc == 7)),
                                      [mx.bufs[s], wo.b], [pY.bufs[hf]])
                            tk.op("dve", lambda e, hf=hf: e.tensor_tensor(out=tt.t[:, hf, :], in0=pY.t[:, hf, :], in1=gb.t[:, hf * 512:(hf + 1) * 512], op=ALU.mult),
                                  [pY.bufs[hf], gb.b], [tt.bufs[hf]])
                            tk.op("pool", lambda e, t=t, hf=hf: e.tensor_tensor(out=xb.t[:, s, t, hf * 512:(hf + 1) * 512], in0=xb.t[:, s, t, hf * 512:(hf + 1) * 512],
                                                                                in1=tt.t[:, hf, :], op=ALU.add), [tt.bufs[hf], xb.bufs[s]], [xb.bufs[s]])
                    tk.dma(rows, xb.t[:, s], reads=[xb.bufs[s]], writes=[db("x1")])
                tk.barrier()


        yf_d = dscr("yf", [L, 512], F32)
        hf_d = dscr("hf_dbg", [2, 128, 512], F32)
        ecw_d = dscr("ecw_dbg", [128, 40], F32)
        xdtd_d = dscr("xdtd_dbg", [128, 512], BF16)
        xs_dbg = dscr("xs_dbg", [128, NT, 512], BF16)
        bmt_dbg = dscr("bmt_dbg", [128, NT, 256], BF16)

        def phase2a():
            with contextlib.ExitStack() as ph, nc.named_scope("p2a"):
                xs = sb(ph, "sxs", [128, NT, 512], BF16)
                bmt = sb(ph, "sbmt", [128, NT, 256], BF16)
                bmT = sb(ph, "sbmT", [128, 2, T], BF16)
                cmT = sb(ph, "scmT", [128, 2, T], BF16)
                aall = sb(ph, "saall", [128, NT, 16], F32)
                anb = sb(ph, "sanb", [128, 16], F32)
                dsk = sb(ph, "sdsk", [128, 8], F32)
                tk.dma(anb.t[:], bass.AP(a_log_d.tensor, 0, [[0, 128], [1, 16]]), writes=[anb.b])
                tk.dma(dsk.t[:], bass.AP(d_skip_d.tensor, 0, [[0, 128], [1, 8]]), writes=[dsk.b])
                tk.op("act", lambda e: e.activation(out=anb.t[:], in_=anb.t[:], func=AF.Exp), [anb.b], [anb.b])
                tk.op("dve", lambda e: e.tensor_scalar_mul(out=anb.t[:], in0=anb.t[:], scalar1=-1.0), [anb.b], [anb.b])
                tk.op("dve", lambda e: e.tensor_tensor(out=aall.t[:], in0=dtv.t[:], in1=anb.t[:].unsqueeze(1).to_broadcast([128, NT, 16]), op=ALU.mult),
                      [dtv.b, anb.b], [aall.b])
                if debug and 'cv0' in debug:
                    tk.barrier()
                    return
                with contextlib.ExitStack() as cv:
                    xr = sb(cv, "sxr", [128, 2, T], BF16, nb=2)
                    acc = sb(cv, "sacc", [128, T], F32)
                    co = sb(cv, "sco", [128, 2, T], BF16, nb=2)
                    pT = ps(cv, "spT", [128, 2, 8, 128], BF16, nb=2)
                    ng = 0
                    for cc in range(8):
                        s = cc % 2
                        tk.dma(xr.t[:, s, :], xbcT_d[cc], reads=[db("xbcT")], writes=[xr.bufs[s]])
                        tk.op("dve", lambda e, cc=cc: e.tensor_scalar(out=acc.t[:], in0=xr.t[:, s, :], scalar1=cols.t[:, cc, 7:8], scalar2=cols.t[:, cc, 10:11],
                                                                      op0=ALU.mult, op1=ALU.add), [xr.bufs[s], cols.b], [acc.b])
                        if debug and 'cv1' in debug:
                            continue
                        for (lo, hi) in ((0, CL), (CL, T)):
                            for j in (0, 1, 3, 4):
                                sh = j - 2
                                o0 = lo + max(0, -sh)
                                o1 = hi - max(0, sh)
                                tk.op("dve", lambda e, cc=cc, j=j, o0=o0, o1=o1, sh=sh: e.scalar_tensor_tensor(
                                    out=acc.t[:, o0:o1], in0=xr.t[:, s, o0 + sh:o1 + sh], scalar=cols.t[:, cc, 5 + j:6 + j], in1=acc.t[:, o0:o1],
                                    op0=ALU.mult, op1=ALU.add), [xr.bufs[s], cols.b, acc.b], [acc.b])
                        if debug and 'cv2' in debug:
                            continue
                        if cc < 4:
                            dest, dbuf = co.t[:, s, :], co.bufs[s]
                        elif cc < 6:
                            dest, dbuf = bmT.t[:, cc - 4, :], bmT.b
                        else:
                            dest, dbuf = cmT.t[:, cc - 6, :], cmT.b
                        tk.op("act", lambda e, dest=dest: e.activation(out=dest, in_=acc.t[:], func=AF.Silu), [acc.b], [dbuf])
                        if debug and 'cv3' in debug:
                            continue
                        if cc < 6:
                            for tg in range(0, NT, 4):
                                n = min(4, NT - tg)
                                sl = ng % 2
                                ng += 1
                                for k in range(n):
                                    t = tg + k
                                    tk.op("pe", lambda e, k=k, t=t, dest=dest: e.transpose(out=pT.t[:, sl, k, :], in_=dest[:, t * 128:(t + 1) * 128], identity=idb.t[:]),
                                          [dbuf, idb.b], [pT.bufs[sl]])
                                if cc < 4:
                                    tk.op("dve" if ng % 2 else "pool" if False else "dve", lambda e, cc=cc, tg=tg, n=n: e.tensor_copy(out=xs.t[:, tg:tg + n, cc * 128:(cc + 1) * 128], in_=pT.t[:, sl, 0:n, :]),
                                          [pT.bufs[sl]], [xs.b])
                                else:
                                    tk.op("dve", lambda e, cc=cc, tg=tg, n=n: e.tensor_copy(out=bmt.t[:, tg:tg + n, (cc - 4) * 128:(cc - 3) * 128], in_=pT.t[:, sl, 0:n, :]),
                                          [pT.bufs[sl]], [bmt.b])
                    tk.barrier()
                if debug and 'xs_dbg' in debug:
                    tk.dma(xs_dbg, xs.t[:], reads=[xs.b], writes=[db('xsd')])
                    tk.dma(bmt_dbg, bmt.t[:], reads=[bmt.b], writes=[db('bmd')])
                if debug and 'conv_only' in debug:
                    return
                with contextlib.ExitStack() as sw:
                    hst = sb(sw, "shst", [128, 512], F32)
                    hbf = sb(sw, "shbf", [128, 2, 512], BF16, nb=2)
                    ec = sb(sw, "sec", [128, 2, 24], F32, nb=2)
                    aexp = sb(sw, "saexp", [128, 8, 128], F32)
                    eseg = sb(sw, "seseg", [128, 8, 128], F32)
                    Mt = sb(sw, "sM", [128, 2, 8, 128], BF16, nb=2)
                    cbm = sb(sw, "scbm", [128, 2, 128], F32)
                    w8 = sb(sw, "sw8", [128, 8], F32)
                    xdt = sb(sw, "sxdt", [128, 2, 8, 64], BF16, nb=2)
                    xdtd = sb(sw, "sxdtd", [128, 2, 8, 64], BF16, nb=2)
                    t1 = sb(sw, "st1", [128, 8, 64], F32)
                    yv = sb(sw, "syv", [128, 2, 512], F32, nb=2)
                    yfi = sb(sw, "syfi", [128, 2, 512], F32, nb=2)
                    zt = sb(sw, "szt", [128, 2, 512], BF16, nb=2)
                    sz = sb(sw, "ssz", [128, 512], F32)
                    sq2 = sb(sw, "ssq2", [128, 512], F32)
                    ss = sb(sw, "sss", [128, 8], F32)
                    ob = sb(sw, "sob", [128, 512], BF16)
                    obT = sb(sw, "sobT", [128, 2, 4, 128], BF16, nb=2)
                    pC = ps(sw, "spC", [128, 512], F32)
                    pSeg = ps(sw, "spSeg", [128, 2, 512], F32)
                    pCB = ps(sw, "spCB", [128, 4, 128], F32)
                    pY = ps(sw, "spY", [128, 512], F32)
                    pYo = ps(sw, "spYo", [128, 512], F32)
                    pSt = ps(sw, "spSt", [128, 512], F32)
                    pTo = ps(sw, "spTo", [128, 8, 128], BF16)
                    nx = 4 if (debug and "only_ctx" in debug) else NT - 2
                    for d in range(2):
                        if d == 0:
                            order = list(range(2 + nx))
                            m_incl, m_excl, m_seg = C_LE, C_GT, C_GT
                        else:
                            order = [1, 0] + list(range(2 + nx - 1, 1, -1))
                            m_incl, m_excl, m_seg = C_GE, C_LT, C_LT
                        tk.op("dve", lambda e: e.memset(hst.t[:], 0.0), [], [hst.b])
                        tk.op("dve", lambda e: e.memset(hbf.t[:, 0, :], 0.0), [], [hbf.bufs[0]])

                        def front(i):
                            c = order[i]
                            s_ = i % 2
                            a = aall.t[:, c, d * 8:(d + 1) * 8]
                            dtc = dtv.t[:, c, d * 8:(d + 1) * 8]
                            for q_, msk in enumerate((m_incl, m_excl, C_ONE)):
                                tk.op("pe", lambda e, q_=q_, msk=msk: e.matmul(pC.t[:, q_ * 8:(q_ + 1) * 8], lhsT=cst.t[:, msk:msk + 128], rhs=a, start=True, stop=True),
                                      [cst.b, aall.b], [pC.b])
                            ecs = ec.t[:, s_, :]
                            tk.op("act", lambda e: e.activation(out=ecs, in_=pC.t[:, 0:24], func=AF.Exp), [pC.b], [ec.bufs[s_]])
                            tk.op("dve", lambda e: e.tensor_tensor(out=w8.t[:], in0=dtc, in1=ecs[:, 8:16], op=ALU.mult), [dtv.b, ec.bufs[s_]], [w8.b])
                            xsc = xs.t[:, c, :].rearrange("p (h d) -> p h d", h=8)
                            tk.op("pool", lambda e: e.tensor_tensor(out=xdtd.t[:, s_], in0=xsc, in1=w8.t[:].unsqueeze(2).to_broadcast([128, 8, 64]), op=ALU.mult),
                                  [xs.b, w8.b], [xdtd.bufs[s_]])
                            if c < 2:
                                return
                            tk.op("dve", lambda e: e.tensor_tensor(out=aexp.t[:], in0=cst.t[:, m_incl:m_incl + 128].unsqueeze(1).to_broadcast([128, 8, 128]),
                                                                   in1=a.unsqueeze(2).to_broadcast([128, 8, 128]), op=ALU.mult), [cst.b, aall.b], [aexp.b])
                            for hh in range(2):
                                tk.op("pe", lambda e, hh=hh: e.matmul(pSeg.t[:, hh, :], lhsT=cst.t[:, m_seg:m_seg + 128],
                                                                      rhs=aexp.t[:, hh * 4:(hh + 1) * 4, :].rearrange("p h i -> p (h i)"), start=True, stop=True),
                                      [cst.b, aexp.b], [pSeg.b])
                            tk.op("act", lambda e: e.activation(out=eseg.t[:].rearrange("p h i -> p (h i)"), in_=pSeg.t[:].rearrange("p a n -> p (a n)"), func=AF.Exp),
                                  [pSeg.b], [eseg.b])
                            for g in range(2):
                                tk.op("pe", lambda e, g=g: e.matmul(pCB.t[:, g, :], lhsT=bmT.t[:, g, c * 128:(c + 1) * 128], rhs=cmT.t[:, g, c * 128:(c + 1) * 128],
                                                                    start=True, stop=True), [bmT.b, cmT.b], [pCB.b])
                            tk.op("dve", lambda e: e.tensor_tensor(out=cbm.t[:], in0=pCB.t[:, 0:2, :], in1=cst.t[:, m_incl:m_incl + 128].unsqueeze(1).to_broadcast([128, 2, 128]), op=ALU.mult),
                                  [pCB.b, cst.b], [cbm.b])
                            tk.op("dve", lambda e: e.tensor_tensor(out=Mt.t[:, s_].rearrange("p (g h) i -> p g h i", g=2), in0=eseg.t[:].rearrange("p (g h) i -> p g h i", g=2),
                                                                   in1=cbm.t[:].unsqueeze(2).to_broadcast([128, 2, 4, 128]), op=ALU.mult), [eseg.b, cbm.b], [Mt.bufs[s_]])
                            tk.op("pool", lambda e: e.tensor_tensor(out=xdt.t[:, s_], in0=xsc, in1=dtc.unsqueeze(2).to_broadcast([128, 8, 64]), op=ALU.mult),
                                  [xs.b, dtv.b], [xdt.bufs[s_]])

                        def back(i):
                            c = order[i]
                            s_ = i % 2
                            ecs = ec.t[:, s_, :]
                            xsc = xs.t[:, c, :].rearrange("p (h d) -> p h d", h=8)
                            for g in range(2):
                                tk.op("pe", lambda e, g=g: e.matmul(pSt.t[:, g * 256:(g + 1) * 256], lhsT=bmt.t[:, c, g * 128:(g + 1) * 128],
                                                                    rhs=xdtd.t[:, s_, g * 4:(g + 1) * 4, :].rearrange("p h d -> p (h d)"), start=True, stop=True),
                                      [bmt.b, xdtd.bufs[s_]], [pSt.b])
                            tk.op("dve", lambda e: e.tensor_tensor(out=hst.t[:].rearrange("p (h d) -> p h d", h=8), in0=hst.t[:].rearrange("p (h d) -> p h d", h=8),
                                                                   in1=ecs[:, 16:24].unsqueeze(2).to_broadcast([128, 8, 64]), op=ALU.mult), [hst.b, ec.bufs[s_]], [hst.b])
                            tk.op("dve", lambda e: e.tensor_tensor(out=hst.t[:], in0=hst.t[:], in1=pSt.t[:], op=ALU.add), [hst.b, pSt.b], [hst.b])
                            tk.op("pool", lambda e: e.tensor_copy(out=hbf.t[:, 1 - s_, :], in_=hst.t[:]), [hst.b], [hbf.bufs[1 - s_]])
                            if c < 2:
                                return
                            for h in range(8):
                                tk.op("pe", lambda e, h=h: e.matmul(pY.t[:, h * 64:(h + 1) * 64], lhsT=Mt.t[:, s_, h, :], rhs=xdt.t[:, s_, h, :], start=True, stop=True),
                                      [Mt.bufs[s_], xdt.bufs[s_]], [pY.b])
                            for g in range(2):
                                tk.op("pe", lambda e, g=g: e.matmul(pYo.t[:, g * 256:(g + 1) * 256], lhsT=cmT.t[:, g, c * 128:(c + 1) * 128], rhs=hbf.t[:, s_, g * 256:(g + 1) * 256],
                                                                    start=True, stop=True), [cmT.b, hbf.bufs[s_]], [pYo.b])
                            ys = i % 2
                            tk.op("dve", lambda e: e.tensor_tensor(out=t1.t[:], in0=pYo.t[:].rearrange("p (h d) -> p h d", h=8),
                                                                   in1=ecs[:, 0:8].unsqueeze(2).to_broadcast([128, 8, 64]), op=ALU.mult), [pYo.b, ec.bufs[s_]], [t1.b])
                            tk.op("dve", lambda e: e.tensor_tensor(out=yv.t[:, ys, :], in0=t1.t[:].rearrange("p h d -> p (h d)"), in1=pY.t[:], op=ALU.add),
                                  [t1.b, pY.b], [yv.bufs[ys]])
                            r0 = (c - 2) * 128
                            if d == 0:
                                tk.dma(yf_d[r0:r0 + 128, :], yv.t[:, ys, :], reads=[yv.bufs[ys]], writes=[db("yf")])
                                return
                            tk.dma(yfi.t[:, ys, :], yf_d[r0:r0 + 128, :], reads=[db("yf")], writes=[yfi.bufs[ys]])
                            tk.dma(zt.t[:, ys, :], z_d[r0:r0 + 128, :], reads=[db("z")], writes=[zt.bufs[ys]])
                            tk.op("pool", lambda e: e.tensor_tensor(out=yv.t[:, ys, :], in0=yv.t[:, ys, :], in1=yfi.t[:, ys, :], op=ALU.add),
                                  [yv.bufs[ys], yfi.bufs[ys]], [yv.bufs[ys]])
                            tk.op("dve", lambda e: e.tensor_tensor(out=t1.t[:], in0=xsc, in1=dsk.t[:].unsqueeze(2).to_broadcast([128, 8, 64]), op=ALU.mult),
                                  [xs.b, dsk.b], [t1.b])
                            tk.op("pool", lambda e: e.tensor_tensor(out=yv.t[:, ys, :], in0=yv.t[:, ys, :], in1=t1.t[:].rearrange("p h d -> p (h d)"), op=ALU.add),
                                  [yv.bufs[ys], t1.b], [yv.bufs[ys]])
                            tk.op("act", lambda e: e.activation(out=sz.t[:], in_=zt.t[:, ys, :], func=AF.Silu), [zt.bufs[ys]], [sz.b])
                            tk.op("dve", lambda e: e.tensor_tensor(out=yv.t[:, ys, :], in0=yv.t[:, ys, :], in1=sz.t[:], op=ALU.mult), [yv.bufs[ys], sz.b], [yv.bufs[ys]])
                            tk.op("pool", lambda e: e.tensor_tensor(out=sq2.t[:], in0=yv.t[:, ys, :], in1=yv.t[:, ys, :], op=ALU.mult), [yv.bufs[ys]], [sq2.b])
                            tk.op("dve", lambda e: e.reduce_sum(out=ss.t[:, 0:1], in_=sq2.t[:], axis=mybir.AxisListType.X), [sq2.b], [ss.b])
                            tk.op("dve", lambda e: e.tensor_scalar(out=ss.t[:, 1:2], in0=ss.t[:, 0:1], scalar1=1.0 / 512, scalar2=EPS, op0=ALU.mult, op1=ALU.add), [ss.b], [ss.b])
                            tk.op("act", lambda e: e.sqrt(out=ss.t[:, 2:3], in_=ss.t[:, 1:2]), [ss.b], [ss.b])
                            tk.op("dve", lambda e: e.reciprocal(out=ss.t[:, 3:4], in_=ss.t[:, 2:3]), [ss.b], [ss.b])
                            tk.op("dve", lambda e: e.tensor_scalar_mul(out=ob.t[:], in0=yv.t[:, ys, :], scalar1=ss.t[:, 3:4]), [yv.bufs[ys], ss.b], [ob.b])
                            for k in range(4):
                                tk.op("pe", lambda e, k=k: e.transpose(out=pTo.t[:, k, :], in_=ob.t[:, k * 128:(k + 1) * 128], identity=idb.t[:]), [ob.b, idb.b], [pTo.b])
                            tk.op("act", lambda e: e.copy(out=obT.t[:, ys], in_=pTo.t[:, 0:4, :]), [pTo.b], [obT.bufs[ys]])
                            tk.dma(mixT_d[0:4, :, r0:r0 + 128].rearrange("c p n -> p c n"), obT.t[:, ys], reads=[obT.bufs[ys]], writes=[db("mixT")])

                        front(0)
                        for i in range(len(order)):
                            if i + 1 < len(order):
                                front(i + 1)
                            back(i)
                    tk.barrier()

        if ENABLE_SSD and not (debug and "skip_ssd" in debug):
            phase2a()
        phase2b()
        phase2c()
        mid.close()
        blocks3 = []
        for i in range(L // 256):
            blocks3.append((x1_d[CL + i * 256:CL + (i + 1) * 256, :], out_d[i * 256:(i + 1) * 256, :], 0, db("x1"), db("out")))
        if debug and "only_ctx" in debug:
            blocks3 = blocks3[:2]
        ffn_phase("f2", w_ffn2_in_d, w_ffn2_out_d, 2, 8, blocks3, True)

        tk.barrier(("sp",))
    return nc


_CONSTS = None


def _consts():
    global _CONSTS
    if _CONSTS is None:
        c = np.zeros((128, NCONST), np.float32)
        k = np.arange(128)[:, None]
        i = np.arange(128)[None, :]
        c[:, C_ID:C_ID + 128] = (k == i)
        c[:, C_LE:C_LE + 128] = (k <= i)
        c[:, C_GT:C_GT + 128] = (k > i)
        c[:, C_GE:C_GE + 128] = (k >= i)
        c[:, C_LT:C_LT + 128] = (k < i)
        c[:, C_ONE:C_ONE + 128] = 1.0
        c[0, C_SEL:C_SEL + 128] = 1.0
        c[1, C_SEL + 128:C_SEL + 256] = 1.0
        half = 16
        inv = 10000.0 ** (-np.arange(0, half, 2, dtype=np.float32) / half)
        pos = np.arange(L)
        row = (pos // 64).astype(np.float32)
        colp = (pos % 64).astype(np.float32)
        rope = np.zeros((2, 32, L), np.float32)
        for base, p in ((0, row), (16, colp)):
            ang = p[None, :] * inv[:, None]
            cs, sn = np.cos(ang), np.sin(ang)
            rope[0, base:base + 8] = cs
            rope[0, base + 8:base + 16] = cs
            rope[1, base:base + 8] = -sn
            rope[1, base + 8:base + 16] = sn
        _CONSTS = (c, rope)
    return _CONSTS


def make_in_maps(inputs, cores=range(8)):
    f = lambda a: np.ascontiguousarray(np.asarray(a, dtype=np.float32))
    c, rope = _consts()
    shared = {
        "w_ada": f(inputs["w_ada"][0]), "b_ada": f(inputs["b_ada"][0]).reshape(1, -1),
        "norm_ffn1": f(inputs["norm_ffn1"][0]).reshape(1, -1), "w_ffn1_in": f(inputs["w_ffn1_in"][0]),
        "w_ffn1_out": f(inputs["w_ffn1_out"][0]), "norm_mix": f(inputs["norm_mix"][0]).reshape(1, -1),
        "w_in": f(inputs["w_in"][0]), "conv_w": f(inputs["conv_w"][0]), "conv_b": f(inputs["conv_b"][0]).reshape(1, -1),
        "dt_bias": f(inputs["dt_bias"][0]).reshape(1, 16), "a_log": f(inputs["a_log"][0]).reshape(1, 16),
        "d_skip": f(inputs["d_skip"][0]).reshape(1, 8), "ssd_norm": f(inputs["ssd_norm"][0]).reshape(1, -1),
        "q_norm": f(inputs["q_norm"][0]).reshape(1, -1), "w_uq": f(inputs["w_uq"][0]),
        "kv_norm": f(inputs["kv_norm"][0]).reshape(1, -1), "w_ukv": f(inputs["w_ukv"][0]), "w_out": f(inputs["w_out"][0]),
        "norm_ffn2": f(inputs["norm_ffn2"][0]).reshape(1, -1), "w_ffn2_in": f(inputs["w_ffn2_in"][0]),
        "w_ffn2_out": f(inputs["w_ffn2_out"][0]), "final_norm": f(inputs["final_norm"]).reshape(1, -1),
        "consts": c, "rope": rope,
    }
    maps = []
    for b in cores:
        m = dict(shared)
        m["x"] = f(inputs["x"][b])
        m["ctx"] = f(inputs["ctx"][b])
        m["cc"] = np.stack([f(inputs["c"][b]), f(inputs["c_ctx"])], 0)
        maps.append(m)
    return maps


def kernel(**inputs):
    nc = build()
    maps = make_in_maps(inputs)
    res = run_bass_kernel_spmd(nc, maps, core_ids=list(range(8)))
    return np.stack([np.asarray(r["out"], dtype=np.float32) for r in res.results], 0)
```

```python
import contextlib
import numpy as np
import ml_dtypes
import concourse.bass as bass
import concourse.mybir as mybir
from concourse.bass_utils import run_bass_kernel_spmd

F32 = mybir.dt.float32
BF16 = mybir.dt.bfloat16
AF = mybir.ActivationFunctionType
ALU = mybir.AluOpType

D = 1024
L = 4096
CL = 256
T = L + CL
NT = T // 128
DFF = 2816
NKF = DFF // 128
EPS = 1e-6
IN_DIM = 2608
SCALE = 96 ** -0.5
ENABLE_SSD = True

C_ID = 0
C_LE = 128
C_GT = 256
C_GE = 384
C_LT = 512
C_ONE = 640
C_SEL = 768
NCONST = 1024


def _merge(d, s):
    for k, v in s.items():
        if d.get(k, 0) < v:
            d[k] = v


class Buf:
    __slots__ = ("name", "w", "r")

    def __init__(self, name=""):
        self.name = name
        self.w = {}
        self.r = {}


class TK:
    def __init__(self, nc, es, n_dma=14):
        self.nc = nc
        self.eng = {"pe": nc.tensor, "act": nc.scalar, "dve": nc.vector, "pool": nc.gpsimd, "sp": nc.sync}
        self.sem = {}
        self.cnt = {}
        for n in ("pe", "act", "dve", "pool"):
            self.sem[n] = es.enter_context(nc.semaphore("s_" + n))
            self.cnt[n] = 0
        self.nd = n_dma
        for i in range(n_dma):
            self.sem[f"d{i}"] = es.enter_context(nc.semaphore(f"s_d{i}"))
            self.cnt[f"d{i}"] = 0
        self.drr = 0
        self.seen = {n: {} for n in self.eng}

    def _wait(self, e, deps):
        eng = self.eng[e]
        seen = self.seen[e]
        for k, v in deps.items():
            if k == e:
                continue
            if seen.get(k, 0) < v:
                eng.wait_ge(self.sem[k], v)
                seen[k] = v

    def op(self, e, fn, reads=(), writes=()):
        deps = {}
        for b in reads:
            _merge(deps, b.w)
        own = deps.get(e, 0)
        if e != "pe" and own > 0 and self.cnt[e] - own < 2 and self.seen[e].get(e, 0) < own:
            self.eng[e].wait_ge(self.sem[e], own)
            self.seen[e][e] = own
        for b in writes:
            _merge(deps, b.r)
            _merge(deps, b.w)
        self._wait(e, deps)
        ins = fn(self.eng[e])
        self.cnt[e] += 1
        ins.then_inc(self.sem[e], 1)
        c = self.cnt[e]
        for b in reads:
            b.r[e] = c
        for b in writes:
            b.w[e] = c
        return ins

    def dma(self, out, in_, reads=(), writes=(), q="sp", **kw):
        k = self.drr
        self.drr = (self.drr + 1) % self.nd
        key = f"d{k}"
        deps = {}
        for b in reads:
            _merge(deps, b.w)
        for b in writes:
            _merge(deps, b.r)
            _merge(deps, b.w)
        if self.cnt[key] > 0:
            deps[key] = max(deps.get(key, 0), self.cnt[key])
        self._wait(q, deps)
        ins = self.eng[q].dma_start(out=out, in_=in_, **kw)
        self.cnt[key] += 16
        ins.then_inc(self.sem[key], 16)
        c = self.cnt[key]
        for b in reads:
            b.r[key] = c
        for b in writes:
            b.w[key] = c
        return ins

    def barrier(self, engines=("pe", "act", "dve", "pool", "sp")):
        for e in engines:
            deps = {k: v for k, v in self.cnt.items() if v > 0}
            self._wait(e, deps)


class SB:
    def __init__(self, t, nb=1):
        self.t = t
        self.bufs = [Buf() for _ in range(nb)]

    @property
    def b(self):
        return self.bufs[0]


def build(debug=None):
    nc = bass.Bass("TRN2", target_bir_lowering=False)

    def din(name, shape, dt=F32):
        return nc.dram_tensor(name, list(shape), dt, kind="ExternalInput").ap()

    x_d = din("x", [L, D])
    ctx_d = din("ctx", [CL, D])
    cc_d = din("cc", [2, D])
    w_ada_d = din("w_ada", [D, 9 * D])
    b_ada_d = din("b_ada", [1, 9 * D])
    norm_ffn1_d = din("norm_ffn1", [1, D])
    w_ffn1_in_d = din("w_ffn1_in", [D, 2 * DFF])
    w_ffn1_out_d = din("w_ffn1_out", [DFF, D])
    norm_mix_d = din("norm_mix", [1, D])
    w_in_d = din("w_in", [D, IN_DIM])
    conv_w_d = din("conv_w", [5, D])
    conv_b_d = din("conv_b", [1, D])
    dt_bias_d = din("dt_bias", [1, 16])
    a_log_d = din("a_log", [1, 16])
    d_skip_d = din("d_skip", [1, 8])
    ssd_norm_d = din("ssd_norm", [1, 512])
    q_norm_d = din("q_norm", [1, 768])
    w_uq_d = din("w_uq", [768, 768])
    kv_norm_d = din("kv_norm", [1, 256])
    w_ukv_d = din("w_ukv", [256, 1024])
    w_out_d = din("w_out", [D, D])
    norm_ffn2_d = din("norm_ffn2", [1, D])
    w_ffn2_in_d = din("w_ffn2_in", [D, 2 * DFF])
    w_ffn2_out_d = din("w_ffn2_out", [DFF, D])
    final_norm_d = din("final_norm", [1, D])
    consts_d = din("consts", [128, NCONST])
    rope_d = din("rope", [2, 32, L])

    out_d = nc.dram_tensor("out", [L, D], F32, kind="ExternalOutput").ap()

    def dscr(name, shape, dt):
        kind = "ExternalOutput" if (debug and name in debug) else "Internal"
        return nc.dram_tensor(name, list(shape), dt, kind=kind).ap()

    mrow_d = dscr("mrow", [2, 9 * D], F32)
    x1_d = dscr("x1", [T, D], F32)

    def scope(name):
        return nc.named_scope(name) if (debug and "scopes" in debug) else contextlib.nullcontext()

    with contextlib.ExitStack() as es:
        tk = TK(nc, es)

        def sb(st, name, shape, dt=F32, nb=1):
            return SB(st.enter_context(nc.sbuf_tensor(name, list(shape), dt)), nb)

        def ps(st, name, shape, dt=F32, nb=1):
            return SB(st.enter_context(nc.psum_tensor(name, list(shape), dt)), nb)

        dram_bufs = {}

        def db(name):
            if name not in dram_bufs:
                dram_bufs[name] = Buf(name)
            return dram_bufs[name]

        cst = sb(es, "cst", [128, NCONST])
        idb = sb(es, "idb", [128, 128], BF16)
        cols = sb(es, "cols", [128, 8, 32])
        modp = sb(es, "modp", [128, 3, 2, 2, 8])
        tk.dma(cst.t[:], consts_d, writes=[cst.b])
        tk.op("dve", lambda e: e.tensor_copy(out=idb.t[:], in_=cst.t[:, C_ID:C_ID + 128]), [cst.b], [idb.b])

        with contextlib.ExitStack() as p0, scope("p0"):
            stage = sb(p0, "stage", [32, D])
            wa = sb(p0, "wa", [128, 4, 8, 512], F32, nb=4)
            mrow = sb(p0, "mrow_sb", [2, 9 * D])
            bada = sb(p0, "bada", [1, 9 * D])
            pm = ps(p0, "pm", [2, 2, 512], F32, nb=2)
            pt = ps(p0, "pt", [128, 8, 32], F32)
            pc = ps(p0, "pc", [128, 72, 2], F32)
            mcol = sb(p0, "mcol", [128, 72, 2])

            tk.op("pool", lambda e: e.memset(stage.t[:], 0.0), [], [stage.b])
            tk.dma(stage.t[0:2, :], cc_d, writes=[stage.b])
            rows = [(2, norm_ffn1_d, D), (3, norm_mix_d, D), (4, norm_ffn2_d, D), (10, conv_b_d, D),
                    (11, q_norm_d, 768), (12, kv_norm_d, 256), (13, ssd_norm_d, 512)]
            for r, src, n in rows:
                tk.dma(stage.t[r:r + 1, 0:n], src, writes=[stage.b])
            tk.dma(stage.t[5:10, :], conv_w_d, writes=[stage.b])
            tk.dma(bada.t[:], b_ada_d, writes=[bada.b])
            tk.op("act", lambda e: e.activation(out=stage.t[0:2, :], in_=stage.t[0:2, :], func=AF.Silu),
                  [stage.b], [stage.b])
            for kc in range(8):
                tk.op("pe", lambda e, kc=kc: e.transpose(out=pt.t[:, kc, :], in_=stage.t[0:32, kc * 128:(kc + 1) * 128],
                                                         identity=cst.t[0:32, C_ID:C_ID + 32]),
                      [stage.b, cst.b], [pt.b])
            tk.op("dve", lambda e: e.tensor_copy(out=cols.t[:], in_=pt.t[:]), [pt.b], [cols.b])

            war = w_ada_d.rearrange("(kc k) n -> k kc n", k=128)
            for nb in range(18):
                s = nb % 4
                tk.dma(wa.t[:, s], war[:, :, nb * 512:(nb + 1) * 512], writes=[wa.bufs[s]])
                pb = nb % 2
                for kc in range(8):
                    tk.op("pe", lambda e, kc=kc, s=s, pb=pb: e.matmul(pm.t[:, pb, :], lhsT=cols.t[:, kc, 0:2],
                                                                     rhs=wa.t[:, s, kc, :], start=(kc == 0), stop=False),
                          [cols.b, wa.bufs[s]], [pm.bufs[pb]])
                tk.op("pe", lambda e, pb=pb, nb=nb: e.matmul(pm.t[:, pb, :], lhsT=cst.t[0:1, C_ONE:C_ONE + 2],
                                                            rhs=bada.t[0:1, nb * 512:(nb + 1) * 512], start=False, stop=True),
                      [cst.b, bada.b], [pm.bufs[pb]])
                tk.op("dve", lambda e, pb=pb, nb=nb: e.tensor_copy(out=mrow.t[:, nb * 512:(nb + 1) * 512], in_=pm.t[:, pb, :]),
                      [pm.bufs[pb]], [mrow.b])
            tk.dma(mrow_d, mrow.t[:], reads=[mrow.b], writes=[db("mrow")])
            for j in range(72):
                tk.op("pe", lambda e, j=j: e.transpose(out=pc.t[:, j, :], in_=mrow.t[0:2, j * 128:(j + 1) * 128],
                                                       identity=cst.t[0:2, C_ID:C_ID + 2]),
                      [mrow.b, cst.b], [pc.b])
            tk.op("dve", lambda e: e.tensor_copy(out=mcol.t[:], in_=pc.t[:]), [pc.b], [mcol.b])
            for s_ in range(3):
                for b_ in range(2):
                    tk.op("dve", lambda e, s_=s_, b_=b_: e.scalar_tensor_tensor(
                        out=modp.t[:, s_, b_, 0, :], in0=mcol.t[:, (3 * s_ + 1) * 8:(3 * s_ + 2) * 8, b_], scalar=1.0,
                        in1=cols.t[:, :, 2 + s_], op0=ALU.add, op1=ALU.mult), [mcol.b, cols.b], [modp.b])
                    tk.op("dve", lambda e, s_=s_, b_=b_: e.tensor_copy(
                        out=modp.t[:, s_, b_, 1, :], in_=mcol.t[:, (3 * s_) * 8:(3 * s_ + 1) * 8, b_]), [mcol.b], [modp.b])
            tk.barrier()

        def ffn_phase(tag, w_in_dram, w_out_dram, mset, gate_idx, blocks, final):
            with contextlib.ExitStack() as ph, scope(tag):
                wi = sb(ph, tag + "wi", [128, 8, 2 * DFF], BF16)
                wo = sb(ph, tag + "wo", [128, NKF, D], BF16)
                gbs = sb(ph, tag + "gb", [128, 2, D], F32, nb=2)
                fnb = sb(ph, tag + "fnb", [128, D], F32) if final else None
                bs = set(b_ for (_, _, b_, _, _) in blocks)
                for b_ in sorted(bs):
                    src = mrow_d[b_:b_ + 1, gate_idx * D:(gate_idx + 1) * D]
                    tk.dma(gbs.t[:, b_, :], bass.AP(src.tensor, src.offset, [[0, 128], [1, D]]),
                           reads=[db("mrow")], writes=[gbs.bufs[b_]])
                    tk.op("pool", lambda e, b_=b_: e.tensor_scalar_mul(out=gbs.t[:, b_, :], in0=gbs.t[:, b_, :], scalar1=0.5),
                          [gbs.bufs[b_]], [gbs.bufs[b_]])
                if final:
                    tk.dma(fnb.t[:], bass.AP(final_norm_d.tensor, 0, [[0, 128], [1, D]]), writes=[fnb.b])
                with contextlib.ExitStack() as st:
                    stg = sb(st, tag + "stg", [128, 3, 2048], F32, nb=3)
                    pieces = []
                    wir = w_in_dram.rearrange("(kc k) n -> k kc n", k=128)
                    for kc in range(8):
                        for q4 in range(4):
                            pieces.append((wir[:, kc, q4 * 1408:(q4 + 1) * 1408], wi.t[:, kc, q4 * 1408:(q4 + 1) * 1408], 1408, wi.b))
                    wor = w_out_dram.rearrange("(kc k) n -> k kc n", k=128)
                    for kc in range(NKF):
                        pieces.append((wor[:, kc, :], wo.t[:, kc, :], 1024, wo.b))
                    cast_eng = ["dve", "pool", "act"]
                    for i, (src, dst, n, wb) in enumerate(pieces):
                        s = i % 3
                        tk.dma(stg.t[:, s, 0:n], src, writes=[stg.bufs[s]])
                        ce = cast_eng[i % 3]
                        if ce == "act":
                            tk.op("act", lambda e, s=s, n=n, dst=dst: e.copy(out=dst, in_=stg.t[:, s, 0:n]), [stg.bufs[s]], [wb])
                        else:
                            tk.op(ce, lambda e, s=s, n=n, dst=dst: e.tensor_copy(out=dst, in_=stg.t[:, s, 0:n]), [stg.bufs[s]], [wb])
                    tk.barrier()
                xb = sb(ph, tag + "xb", [128, 2, 2, D], F32, nb=2)
                xn = sb(ph, tag + "xn", [128, 2, 2, D], BF16, nb=2)
                hT = sb(ph, tag + "hT", [128, 2, 8, 256], BF16, nb=2)
                tmpm = sb(ph, tag + "tmpm", [128, 8, 128], F32)
                gsb = sb(ph, tag + "g", [128, NKF, 256], BF16)
                sg = sb(ph, tag + "sg", [128, 2, 512], F32, nb=2)
                tt = sb(ph, tag + "tt", [128, 2, 512], F32, nb=2)
                junk = sb(ph, tag + "junk", [128, D], BF16)
                ss = sb(ph, tag + "ss", [128, 2, 8], F32, nb=2)
                pG = ps(ph, tag + "pG", [128, 2, 512], F32, nb=2)
                pU = ps(ph, tag + "pU", [128, 2, 512], F32, nb=2)
                pY = ps(ph, tag + "pY", [128, 2, 512], F32, nb=2)
                pT = ps(ph, tag + "pT", [128, 2, 8, 128], BF16, nb=2)
                nblk = len(blocks)

                def load(i):
                    src, _, _, sbuf, _ = blocks[i]
                    s = i % 2
                    tk.dma(xb.t[:, s], src.rearrange("(t p) d -> p t d", p=128), reads=[sbuf], writes=[xb.bufs[s]])

                def norm(i):
                    s = i % 2
                    sbf = ss.bufs[s]
                    for t in range(2):
                        tk.op("act", lambda e, t=t: e.activation(out=junk.t[:], in_=xb.t[:, s, t, :], func=AF.Square,
                                                                  accum_out=ss.t[:, s, t:t + 1]), [xb.bufs[s]], [junk.b, sbf])
                    tk.op("dve", lambda e: e.tensor_scalar(out=ss.t[:, s, 2:4], in0=ss.t[:, s, 0:2], scalar1=1.0 / D, scalar2=EPS,
                                                           op0=ALU.mult, op1=ALU.add), [sbf], [sbf])
                    tk.op("act", lambda e: e.sqrt(out=ss.t[:, s, 4:6], in_=ss.t[:, s, 2:4]), [sbf], [sbf])
                    tk.op("dve", lambda e: e.reciprocal(out=ss.t[:, s, 6:8], in_=ss.t[:, s, 4:6]), [sbf], [sbf])
                    for t in range(2):
                        tk.op("dve", lambda e, t=t: e.tensor_scalar_mul(out=xn.t[:, s, t, :], in0=xb.t[:, s, t, :],
                                                                        scalar1=ss.t[:, s, 6 + t:7 + t]),
                              [xb.bufs[s], sbf], [xn.bufs[s]])

                def transp(i):
                    s = i % 2
                    b_ = blocks[i][2]
                    for t in range(2):
                        for kc in range(8):
                            tk.op("pe", lambda e, t=t, kc=kc: e.transpose(out=pT.t[:, t, kc, :], in_=xn.t[:, s, t, kc * 128:(kc + 1) * 128],
                                                                            identity=idb.t[:]), [xn.bufs[s], idb.b], [pT.bufs[t]])
                        G = modp.t[:, mset, b_, 0, :].unsqueeze(2).to_broadcast([128, 8, 128])
                        S = modp.t[:, mset, b_, 1, :].unsqueeze(2).to_broadcast([128, 8, 128])
                        tk.op("dve", lambda e, t=t, G=G: e.tensor_tensor(out=tmpm.t[:], in0=pT.t[:, t], in1=G, op=ALU.mult),
                              [pT.bufs[t], modp.b], [tmpm.b])
                        tk.op("pool", lambda e, t=t, S=S: e.tensor_tensor(out=hT.t[:, s, :, t * 128:(t + 1) * 128], in0=tmpm.t[:], in1=S, op=ALU.add),
                              [tmpm.b, modp.b], [hT.bufs[s]])

                def ffn_in(i):
                    s = i % 2
                    for fp in range(NKF // 2):
                        pb = fp % 2
                        for j in range(2):
                            fc = fp * 2 + j
                            for kc in range(8):
                                tk.op("pe", lambda e, j=j, fc=fc, kc=kc: e.matmul(pG.t[:, pb, j * 256:(j + 1) * 256], lhsT=wi.t[:, kc, fc * 128:(fc + 1) * 128],
                                                                                   rhs=hT.t[:, s, kc, :], start=(kc == 0), stop=(kc == 7)),
                                      [wi.b, hT.bufs[s]], [pG.bufs[pb]])
                            for kc in range(8):
                                tk.op("pe", lambda e, j=j, fc=fc, kc=kc: e.matmul(pU.t[:, pb, j * 256:(j + 1) * 256], lhsT=wi.t[:, kc, DFF + fc * 128:DFF + (fc + 1) * 128],
                                                                                   rhs=hT.t[:, s, kc, :], start=(kc == 0), stop=(kc == 7)),
                                      [wi.b, hT.bufs[s]], [pU.bufs[pb]])
                        tk.op("act", lambda e: e.activation(out=sg.t[:, pb, :], in_=pG.t[:, pb, :], func=AF.Silu), [pG.bufs[pb]], [sg.bufs[pb]])
                        tk.op("dve", lambda e, fp=fp: e.tensor_tensor(out=gsb.t[:, fp * 2:fp * 2 + 2, :], in0=sg.t[:, pb, :].rearrange("p (a b) -> p a b", a=2),
                                                                      in1=pU.t[:, pb, :].rearrange("p (a b) -> p a b", a=2), op=ALU.mult),
                              [sg.bufs[pb], pU.bufs[pb]], [gsb.b])

                def ffn_out(i):
                    s = i % 2
                    _, dst, b_, _, dbuf = blocks[i]
                    for t in range(2):
                        for h in range(2):
                            for kc in range(NKF):
                                tk.op("pe", lambda e, t=t, h=h, kc=kc: e.matmul(pY.t[:, h, :], lhsT=gsb.t[:, kc, t * 128:(t + 1) * 128],
                                                                                 rhs=wo.t[:, kc, h * 512:(h + 1) * 512], start=(kc == 0), stop=(kc == NKF - 1)),
                                      [gsb.b, wo.b], [pY.bufs[h]])
                            tk.op("dve", lambda e, h=h: e.tensor_tensor(out=tt.t[:, h, :], in0=pY.t[:, h, :], in1=gbs.t[:, b_, h * 512:(h + 1) * 512], op=ALU.mult),
                                  [pY.bufs[h], gbs.bufs[b_]], [tt.bufs[h]])
                            tk.op("pool", lambda e, t=t, h=h: e.tensor_tensor(out=xb.t[:, s, t, h * 512:(h + 1) * 512], in0=xb.t[:, s, t, h * 512:(h + 1) * 512],
                                                                               in1=tt.t[:, h, :], op=ALU.add), [tt.bufs[h], xb.bufs[s]], [xb.bufs[s]])
                    if final:
                        sbf = ss.bufs[s]
                        for t in range(2):
                            tk.op("act", lambda e, t=t: e.activation(out=junk.t[:], in_=xb.t[:, s, t, :], func=AF.Square,
                                                                      accum_out=ss.t[:, s, t:t + 1]), [xb.bufs[s]], [junk.b, sbf])
                        tk.op("dve", lambda e: e.tensor_scalar(out=ss.t[:, s, 2:4], in0=ss.t[:, s, 0:2], scalar1=1.0 / D, scalar2=EPS,
                                                               op0=ALU.mult, op1=ALU.add), [sbf], [sbf])
                        tk.op("act", lambda e: e.sqrt(out=ss.t[:, s, 4:6], in_=ss.t[:, s, 2:4]), [sbf], [sbf])
                        tk.op("dve", lambda e: e.reciprocal(out=ss.t[:, s, 6:8], in_=ss.t[:, s, 4:6]), [sbf], [sbf])
                        for t in range(2):
                            tk.op("dve", lambda e, t=t: e.scalar_tensor_tensor(out=xb.t[:, s, t, :], in0=xb.t[:, s, t, :], scalar=ss.t[:, s, 6 + t:7 + t],
                                                                               in1=fnb.t[:], op0=ALU.mult, op1=ALU.mult),
                                  [xb.bufs[s], sbf, fnb.b], [xb.bufs[s]])
                    tk.dma(dst.rearrange("(t p) d -> p t d", p=128), xb.t[:, s], reads=[xb.bufs[s]], writes=[dbuf])

                load(0)
                if nblk > 1:
                    load(1)
                norm(0)
                transp(0)
                for i in range(nblk):
                    ffn_in(i)
                    if i + 1 < nblk:
                        norm(i + 1)
                        transp(i + 1)
                    ffn_out(i)
                    if i + 2 < nblk:
                        load(i + 2)
                tk.barrier()

        blocks1 = [(ctx_d, x1_d[0:256, :], 1, Buf(), db("x1"))]
        for i in range(L // 256):
            blocks1.append((x_d[i * 256:(i + 1) * 256, :], x1_d[CL + i * 256:CL + (i + 1) * 256, :], 0, Buf(), db("x1")))
        if debug and "only_ctx" in debug:
            blocks1 = blocks1[:3]
        ffn_phase("f1", w_ffn1_in_d, w_ffn1_out_d, 0, 2, blocks1, False)


        def phase1b():
            with contextlib.ExitStack() as ph, scope("p1b"):
                wp = sb(ph, "wp", [128, 8, 2624], BF16)
                wq = sb(ph, "wq", [128, 6, 8, 96], BF16)
                wqs = sb(ph, "wqs", [128, 6, 8, 96], BF16)
                wkv = sb(ph, "wkv", [128, 2, 1024], BF16)
                wkp = sb(ph, "wkp", [128, 8, 2, 96], BF16)
                dtb = sb(ph, "dtb", [128, 16], F32)
                with contextlib.ExitStack() as st:
                    stg = sb(st, "stg1", [128, 3, IN_DIM], F32, nb=3)
                    kst = sb(st, "kst", [128, 8, 2, 32], F32)
                    wir = w_in_d.rearrange("(kc k) n -> k kc n", k=128)
                    ce = ["dve", "pool"]
                    n = 0
                    for kc in range(8):
                        s = n % 3
                        tk.dma(stg.t[:, s, :], wir[:, kc, :], writes=[stg.bufs[s]])
                        tk.op(ce[n % 2], lambda e, s=s, kc=kc: e.tensor_copy(out=wp.t[:, kc, 0:1552], in_=stg.t[:, s, 0:1552]), [stg.bufs[s]], [wp.b])
                        tk.op(ce[(n + 1) % 2], lambda e, s=s, kc=kc: e.tensor_copy(out=wp.t[:, kc, 1600:2624], in_=stg.t[:, s, 1552:2576]), [stg.bufs[s]], [wp.b])
                        n += 1
                    tk.op("pool", lambda e: e.memset(wkp.t[:], 0.0), [], [wkp.b])
                    tk.dma(kst.t[:, :, 0, :], wir[:, :, 2576:2608], writes=[kst.b])
                    for half in range(2):
                        o = 2576 + 16 * half
                        tk.dma(kst.t[:, :, 1, 16 * half:16 * half + 8], wir[:, :, o + 8:o + 16], writes=[kst.b])
                        tk.dma(kst.t[:, :, 1, 16 * half + 8:16 * half + 16], wir[:, :, o:o + 8], writes=[kst.b])
                    tk.op("dve", lambda e: e.tensor_copy(out=wkp.t[:, :, :, 64:96], in_=kst.t[:]), [kst.b], [wkp.b])
                    wur = w_uq_d.rearrange("(kc k) n -> k kc n", k=128)
                    for kc in range(6):
                        s = n % 3
                        tk.dma(stg.t[:, s, 0:768], wur[:, kc, :], writes=[stg.bufs[s]])
                        tk.op("dve", lambda e, s=s, kc=kc: e.tensor_scalar_mul(out=wq.t[:, kc].rearrange("p h d -> p (h d)"), in0=stg.t[:, s, 0:768],
                                                                               scalar1=cols.t[:, kc, 11:12]), [stg.bufs[s], cols.b], [wq.b])
                        n += 1
                    tk.op("pool", lambda e: e.tensor_copy(out=wqs.t[:], in_=wq.t[:]), [wq.b], [wqs.b])
                    for half in range(2):
                        o = 64 + 16 * half
                        tk.op("dve", lambda e, o=o: e.tensor_copy(out=wqs.t[:, :, :, o:o + 8], in_=wq.t[:, :, :, o + 8:o + 16]), [wq.b], [wqs.b])
                        tk.op("dve", lambda e, o=o: e.tensor_copy(out=wqs.t[:, :, :, o + 8:o + 16], in_=wq.t[:, :, :, o:o + 8]), [wq.b], [wqs.b])
                    wkr = w_ukv_d.rearrange("(kc k) n -> k kc n", k=128)
                    for kc in range(2):
                        s = n % 3
                        tk.dma(stg.t[:, s, 0:1024], wkr[:, kc, :], writes=[stg.bufs[s]])
                        src = stg.t[:, s, 0:1024].rearrange("p (h two d) -> p two h d", h=8, two=2)
                        tk.op("dve", lambda e, src=src, kc=kc: e.tensor_scalar_mul(out=wkv.t[:, kc, :].rearrange("p (two h d) -> p two h d", two=2, h=8), in0=src,
                                                                                  scalar1=cols.t[:, kc, 12:13]), [stg.bufs[s], cols.b], [wkv.b])
                        n += 1
                    tk.dma(dtb.t[:], bass.AP(dt_bias_d.tensor, 0, [[0, 128], [1, 16]]), writes=[dtb.b])
                    tk.barrier()
                if debug and 'stop0' in debug:
                    return
                NB = 512
                xb = sb(ph, "bxb", [128, 2, 4, D], F32, nb=2)
                xn = sb(ph, "bxn", [128, 4, D], BF16)
                hT = sb(ph, "bhT", [128, 8, NB], BF16)
                tmpm = sb(ph, "btmpm", [128, 8, 128], F32)
                junk = sb(ph, "bjunk", [128, D], BF16)
                ss = sb(ph, "bss", [128, 16], F32)
                zt = sb(ph, "bzt", [128, 4, 512], BF16)
                xo = sb(ph, "bxo", [128, 8, NB], BF16)
                cq = sb(ph, "bcq", [128, 6, NB], BF16)
                cqn = sb(ph, "bcqn", [128, 6, NB], BF16)
                ckv = sb(ph, "bckv", [128, 2, NB], BF16)
                ckvn = sb(ph, "bckvn", [128, 2, NB], BF16)
                sq = sb(ph, "bsq", [128, 8, NB], BF16)
                rb = sb(ph, "brb", [128, 2, 2, NB], F32, nb=2)
                qo = sb(ph, "bqo", [96, 8, NB], BF16)
                ko = sb(ph, "bko", [128, 4, NB], BF16)
                kpo = sb(ph, "bkpo", [96, NB], BF16)
                vo = sb(ph, "bvo", [128, 4, 512], BF16)
                rp = sb(ph, "brp", [96, 2, NB], F32)
                rt = sb(ph, "brt", [96, 2, 2, NB], F32, nb=2)
                dtt = sb(ph, "bdtt", [128, 4, 16], F32)
                pA = ps(ph, "pA", [128, 4, 512], F32, nb=4)
                pT = ps(ph, "pT1", [128, 8, 128], BF16)
                pS = ps(ph, "pS", [128, 512], F32)
                pD = ps(ph, "pD", [128, 4, 16], F32)
                pai = [0]

                def nxt():
                    pai[0] = (pai[0] + 1) % 4
                    return pai[0]

                blocks = [(x1_d[0:CL, :], CL, 1, 0)] + [(x1_d[CL + i * NB:CL + (i + 1) * NB, :], NB, 0, CL + i * NB) for i in range(L // NB)]
                if debug and "only_ctx" in debug:
                    blocks = blocks[:2]
                def load_b(bi):
                    src_, N_, _, _ = blocks[bi]
                    tk.dma(xb.t[:, bi % 2, 0:N_ // 128], src_.rearrange("(t p) d -> p t d", p=128), reads=[db("x1")], writes=[xb.bufs[bi % 2]])

                load_b(0)
                for bi, (src, N, b_, tok0) in enumerate(blocks):
                    nt = N // 128
                    s = bi % 2
                    if bi + 1 < len(blocks):
                        load_b(bi + 1)
                    if b_ == 0:
                        tk.dma(rp.t[64:96, :, 0:N], rope_d[:, :, tok0 - CL:tok0 - CL + N].rearrange("c d n -> d c n"), writes=[rp.b])
                    for t in range(nt):
                        tk.op("act", lambda e, t=t: e.activation(out=junk.t[:], in_=xb.t[:, s, t, :], func=AF.Square, accum_out=ss.t[:, t:t + 1]),
                              [xb.bufs[s]], [junk.b, ss.b])
                    tk.op("dve", lambda e: e.tensor_scalar(out=ss.t[:, 4:8], in0=ss.t[:, 0:4], scalar1=1.0 / D, scalar2=EPS, op0=ALU.mult, op1=ALU.add), [ss.b], [ss.b])
                    tk.op("act", lambda e: e.sqrt(out=ss.t[:, 8:12], in_=ss.t[:, 4:8]), [ss.b], [ss.b])
                    tk.op("dve", lambda e: e.reciprocal(out=ss.t[:, 12:16], in_=ss.t[:, 8:12]), [ss.b], [ss.b])
                    G = modp.t[:, 1, b_, 0, :].unsqueeze(2).to_broadcast([128, 8, 128])
                    S = modp.t[:, 1, b_, 1, :].unsqueeze(2).to_broadcast([128, 8, 128])
                    for t in range(nt):
                        tk.op("dve", lambda e, t=t: e.tensor_scalar_mul(out=xn.t[:, t, :], in0=xb.t[:, s, t, :], scalar1=ss.t[:, 12 + t:13 + t]),
                              [xb.bufs[s], ss.b], [xn.b])
                        for kc in range(8):
                            tk.op("pe", lambda e, t=t, kc=kc: e.transpose(out=pT.t[:, kc, :], in_=xn.t[:, t, kc * 128:(kc + 1) * 128], identity=idb.t[:]),
                                  [xn.b, idb.b], [pT.b])
                        tk.op("dve", lambda e: e.tensor_tensor(out=tmpm.t[:], in0=pT.t[:], in1=G, op=ALU.mult), [pT.b, modp.b], [tmpm.b])
                        tk.op("pool", lambda e, t=t: e.tensor_tensor(out=hT.t[:, :, t * 128:(t + 1) * 128], in0=tmpm.t[:], in1=S, op=ALU.add),
                              [tmpm.b, modp.b], [hT.b])

                    def fm(lhs_fn, nk, rhs_fn, M, rd):
                        a = nxt()
                        for kc in range(nk):
                            tk.op("pe", lambda e, kc=kc: e.matmul(pA.t[0:M, a, 0:N], lhsT=lhs_fn(kc), rhs=rhs_fn(kc), start=(kc == 0), stop=(kc == nk - 1)),
                                  rd, [pA.bufs[a]])
                        return a

                    if debug and 'stop1' in debug:
                        continue
                    if b_ == 0:
                        for t in range(nt):
                            a = fm(lambda kc, t=t: hT.t[:, kc, t * 128:(t + 1) * 128], 8, lambda kc: wp.t[:, kc, 0:512], 128, [hT.b, wp.b])
                            tk.op("act", lambda e, t=t, a=a: e.copy(out=zt.t[:, t, :], in_=pA.t[:, a, :]), [pA.bufs[a]], [zt.b])
                        r0 = tok0 - CL
                        tk.dma(z_d[r0:r0 + N, :].rearrange("(t p) d -> p t d", p=128), zt.t[:, 0:nt], reads=[zt.b], writes=[db("z")])
                    for cc in range(8):
                        a = fm(lambda kc, cc=cc: wp.t[:, kc, 512 + cc * 128:512 + (cc + 1) * 128], 8, lambda kc: hT.t[:, kc, 0:N], 128, [hT.b, wp.b])
                        tk.op("act" if cc % 2 else "dve",
                              (lambda e, cc=cc, a=a: e.copy(out=xo.t[:, cc, 0:N], in_=pA.t[:, a, 0:N])) if cc % 2 else
                              (lambda e, cc=cc, a=a: e.tensor_copy(out=xo.t[:, cc, 0:N], in_=pA.t[:, a, 0:N])), [pA.bufs[a]], [xo.b])
                    tk.dma(xbcT_d[:, :, tok0:tok0 + N].rearrange("c p n -> p c n"), xo.t[:, :, 0:N], reads=[xo.b], writes=[db("xbcT")])
                    if debug and 'stop2' in debug:
                        continue
                    for t in range(nt):
                        for kc in range(8):
                            tk.op("pe", lambda e, t=t, kc=kc: e.matmul(pD.t[:, t, :], lhsT=hT.t[:, kc, t * 128:(t + 1) * 128], rhs=wp.t[:, kc, 1536:1552],
                                                                       start=(kc == 0), stop=(kc == 7)), [hT.b, wp.b], [pD.b])
                    tg = tok0 // 128
                    tk.op("dve", lambda e: e.tensor_tensor(out=dtt.t[:, 0:nt, :], in0=pD.t[:, 0:nt, :], in1=dtb.t[:].unsqueeze(1).to_broadcast([128, nt, 16]), op=ALU.add),
                          [pD.b, dtb.b], [dtt.b])
                    tk.op("act", lambda e: e.activation(out=dtt.t[:, 0:nt, :], in_=dtt.t[:, 0:nt, :], func=AF.Exp), [dtt.b], [dtt.b])
                    tk.op("act", lambda e: e.activation(out=dtv.t[:, tg:tg + nt, :], in_=dtt.t[:, 0:nt, :], func=AF.Ln, bias=1.0), [dtt.b], [dtv.b])
                    if debug and 'stop3' in debug:
                        continue
                    for (dst, dstn, nk, c0, dim, ri) in ((cq, cqn, 6, 1600, 768.0, 0), (ckv, ckvn, 2, 2368, 256.0, 1)):
                        for c_ in range(nk):
                            a = fm(lambda kc, c_=c_: wp.t[:, kc, c0 + c_ * 128:c0 + (c_ + 1) * 128], 8, lambda kc: hT.t[:, kc, 0:N], 128, [hT.b, wp.b])
                            tk.op("dve", lambda e, c_=c_, a=a: e.tensor_copy(out=dst.t[:, c_, 0:N], in_=pA.t[:, a, 0:N]), [pA.bufs[a]], [dst.b])
                            tk.op("pool", lambda e, c_=c_: e.tensor_tensor(out=sq.t[:, c_, 0:N], in0=dst.t[:, c_, 0:N], in1=dst.t[:, c_, 0:N], op=ALU.mult), [dst.b], [sq.b])
                        if debug and 'stop3a' in debug:
                            continue
                        for c_ in range(nk):
                            tk.op("pe", lambda e, c_=c_: e.matmul(pS.t[:, 0:N], lhsT=onb.t[:], rhs=sq.t[:, c_, 0:N], start=(c_ == 0), stop=(c_ == nk - 1)),
                                  [onb.b, sq.b], [pS.b])
                        if debug and 'stop3b' in debug:
                            continue
                        tk.op("dve", lambda e: e.tensor_scalar(out=rb.t[:, ri, 0, 0:N], in0=pS.t[:, 0:N], scalar1=1.0 / dim, scalar2=EPS, op0=ALU.mult, op1=ALU.add),
                              [pS.b], [rb.bufs[ri]])
                        tk.op("act", lambda e: e.sqrt(out=rb.t[:, ri, 0, 0:N], in_=rb.t[:, ri, 0, 0:N]), [rb.bufs[ri]], [rb.bufs[ri]])
                        tk.op("dve", lambda e: e.reciprocal(out=rb.t[:, ri, 1, 0:N], in_=rb.t[:, ri, 0, 0:N]), [rb.bufs[ri]], [rb.bufs[ri]])
                        tk.op("dve", lambda e: e.tensor_tensor(out=dstn.t[:, :, 0:N], in0=dst.t[:, :, 0:N],
                                                                in1=rb.t[:, ri, 1, 0:N].unsqueeze(1).to_broadcast([128, nk, N]), op=ALU.mult),
                              [dst.b, rb.bufs[ri]], [dstn.b])
                    if debug and 'stop4' in debug:
                        continue
                    a1 = fm(lambda kc: wkp.t[:, kc, 0, :], 8, lambda kc: hT.t[:, kc, 0:N], 96, [hT.b, wkp.b])
                    if b_ == 0:
                        a2 = fm(lambda kc: wkp.t[:, kc, 1, :], 8, lambda kc: hT.t[:, kc, 0:N], 96, [hT.b, wkp.b])
                        tk.op("dve", lambda e: e.tensor_tensor(out=rt.t[64:96, 0, 0, 0:N], in0=pA.t[64:96, a1, 0:N], in1=rp.t[64:96, 0, 0:N], op=ALU.mult),
                              [pA.bufs[a1], rp.b], [rt.bufs[0]])
                        tk.op("dve", lambda e: e.tensor_tensor(out=rt.t[64:96, 0, 1, 0:N], in0=pA.t[64:96, a2, 0:N], in1=rp.t[64:96, 1, 0:N], op=ALU.mult),
                              [pA.bufs[a2], rp.b], [rt.bufs[0]])
                        tk.op("pool", lambda e: e.tensor_tensor(out=kpo.t[64:96, 0:N], in0=rt.t[64:96, 0, 0, 0:N], in1=rt.t[64:96, 0, 1, 0:N], op=ALU.add),
                              [rt.bufs[0]], [kpo.b])
                    else:
                        tk.op("dve", lambda e: e.tensor_copy(out=kpo.t[64:96, 0:N], in_=pA.t[64:96, a1, 0:N]), [pA.bufs[a1]], [kpo.b])
                    for h in range(8):
                        tk.dma(kT_d[h, 64:96, tok0:tok0 + N], kpo.t[64:96, 0:N], reads=[kpo.b], writes=[db("kT")])
                    if debug and 'stop5' in debug:
                        continue
                    for hp in range(4):
                        a = fm(lambda kc, hp=hp: wkv.t[:, kc, hp * 128:(hp + 1) * 128], 2, lambda kc: ckvn.t[:, kc, 0:N], 128, [ckvn.b, wkv.b])
                        tk.op("act", lambda e, hp=hp, a=a: e.copy(out=ko.t[:, hp, 0:N], in_=pA.t[:, a, 0:N]), [pA.bufs[a]], [ko.b])
                    for h in range(8):
                        tk.dma(kT_d[h, 0:64, tok0:tok0 + N], ko.t[(h % 2) * 64:(h % 2) * 64 + 64, h // 2, 0:N], reads=[ko.b], writes=[db("kT")])
                    for t in range(nt):
                        a = nxt()
                        for kc in range(2):
                            tk.op("pe", lambda e, t=t, kc=kc, a=a: e.matmul(pA.t[:, a, :], lhsT=ckvn.t[:, kc, t * 128:(t + 1) * 128], rhs=wkv.t[:, kc, 512:1024],
                                                                            start=(kc == 0), stop=(kc == 1)), [ckvn.b, wkv.b], [pA.bufs[a]])
                        tk.op("dve", lambda e, t=t, a=a: e.tensor_copy(out=vo.t[:, t, :], in_=pA.t[:, a, :]), [pA.bufs[a]], [vo.b])
                    tk.dma(V_d[tok0:tok0 + N, :].rearrange("(t p) d -> p t d", p=128), vo.t[:, 0:nt], reads=[vo.b], writes=[db("V")])
                    if b_ == 0:
                        for h in range(8):
                            a1 = fm(lambda kc, h=h: wq.t[:, kc, h, :], 6, lambda kc: cqn.t[:, kc, 0:N], 96, [cqn.b, wq.b])
                            a2 = fm(lambda kc, h=h: wqs.t[:, kc, h, :], 6, lambda kc: cqn.t[:, kc, 0:N], 96, [cqn.b, wqs.b])
                            tk.op("dve", lambda e, h=h, a1=a1: e.tensor_copy(out=qo.t[0:64, h, 0:N], in_=pA.t[0:64, a1, 0:N]), [pA.bufs[a1]], [qo.b])
                            r = h % 2
                            tk.op("dve", lambda e, a1=a1, r=r: e.tensor_tensor(out=rt.t[64:96, r, 0, 0:N], in0=pA.t[64:96, a1, 0:N], in1=rp.t[64:96, 0, 0:N], op=ALU.mult),
                                  [pA.bufs[a1], rp.b], [rt.bufs[r]])
                            tk.op("dve", lambda e, a2=a2, r=r: e.tensor_tensor(out=rt.t[64:96, r, 1, 0:N], in0=pA.t[64:96, a2, 0:N], in1=rp.t[64:96, 1, 0:N], op=ALU.mult),
                                  [pA.bufs[a2], rp.b], [rt.bufs[r]])
                            tk.op("pool", lambda e, h=h, r=r: e.tensor_tensor(out=qo.t[64:96, h, 0:N], in0=rt.t[64:96, r, 0, 0:N], in1=rt.t[64:96, r, 1, 0:N], op=ALU.add),
                                  [rt.bufs[r]], [qo.b])
                        r0 = tok0 - CL
                        tk.dma(qT_d[:, :, r0:r0 + N].rearrange("h d n -> d h n"), qo.t[:, :, 0:N], reads=[qo.b], writes=[db("qT")])
                tk.barrier()

        z_d = dscr("z", [L, 512], BF16)
        xbcT_d = dscr("xbcT", [8, 128, T], BF16)
        kT_d = dscr("kT", [8, 96, T], BF16)
        qT_d = dscr("qT", [8, 96, L], BF16)
        V_d = dscr("V", [T, 512], BF16)
        mid = contextlib.ExitStack()
        dtv = sb(mid, "dtv", [128, NT, 16], F32)
        onb = sb(mid, "onb", [128, 128], BF16)
        tk.op("dve", lambda e: e.tensor_copy(out=onb.t[:], in_=cst.t[:, C_ONE:C_ONE + 128]), [cst.b], [onb.b])
        dtv_d = dscr("dtv_dbg", [128, NT, 16], F32)
        phase1b()
        if debug and "dtv_dbg" in debug:
            tk.dma(dtv_d, dtv.t[:], reads=[dtv.b], writes=[db("dtvd")])


        mixT_d = dscr("mixT", [8, 128, L], BF16)

        def phase2b():
            with contextlib.ExitStack() as ph, scope("p2b"):
                kT = sb(ph, "akT", [128, 2, T], BF16, nb=2)
                qT = sb(ph, "aqT", [128, 2, L], BF16, nb=2)
                Vh = sb(ph, "aV", [128, 2, NT, 128], BF16, nb=2)
                P = sb(ph, "aP", [128, 3, 2, 512], BF16, nb=3)
                oc = sb(ph, "aoc", [65, 2, 512], F32, nb=2)
                rc = sb(ph, "arc", [65, 2, 512], F32, nb=2)
                bc = sb(ph, "abc", [64, 2, 512], F32, nb=2)
                on = sb(ph, "aon", [64, 2, 512], BF16, nb=2)
                pS_ = ps(ph, "aS", [128, 3, 2, 512], F32, nb=3)
                pO = ps(ph, "aO", [128, 512], F32)
                pB = ps(ph, "aB", [64, 512], F32)
                for s in range(2):
                    tk.op("pool", lambda e, s=s: e.memset(Vh.t[:, s], 0.0), [], [Vh.bufs[s]])
                    tk.op("pool", lambda e, s=s: e.memset(Vh.t[:, s, :, 64:65], 1.0), [], [Vh.bufs[s]])
                    tk.op("dve", lambda e, s=s: e.memset(kT.t[:, s, :], 0.0), [], [kT.bufs[s]])
                    tk.op("dve", lambda e, s=s: e.memset(qT.t[:, s, :], 0.0), [], [qT.bufs[s]])
                Vr = V_d.rearrange("(t p) (h d) -> p t h d", p=128, h=8)
                nh = 8
                nqb = 1 if (debug and "only_ctx" in debug) else 8
                NP = NT // 2

                def load_h(h):
                    s = h % 2
                    tk.dma(kT.t[0:96, s, :], kT_d[h], reads=[db("kT")], writes=[kT.bufs[s]])
                    tk.dma(qT.t[0:96, s, :], qT_d[h], reads=[db("qT")], writes=[qT.bufs[s]])
                    tk.dma(Vh.t[:, s, :, 0:64], Vr[:, :, h, :], reads=[db("V")], writes=[Vh.bufs[s]])

                units = [(h, qb) for h in range(nh) for qb in range(nqb)]
                gi = [0]

                def S(h, qb, kp, g):
                    s = h % 2
                    sl = g % 3
                    for j in range(2):
                        kt = kp * 2 + j
                        tk.op("pe", lambda e, j=j, kt=kt: e.matmul(pS_.t[:, sl, j, :], lhsT=kT.t[:, s, kt * 128:(kt + 1) * 128],
                                                                   rhs=qT.t[:, s, qb * 512:(qb + 1) * 512], start=True, stop=True),
                              [kT.bufs[s], qT.bufs[s]], [pS_.bufs[sl]])

                items = [(u, kp) for u in range(len(units)) for kp in range(NP)]
                load_h(0)
                for n_ahead in range(min(2, len(items))):
                    u, kp = items[n_ahead]
                    S(units[u][0], units[u][1], kp, n_ahead)
                for n, (u, kp) in enumerate(items):
                    h, qb = units[u]
                    s = h % 2
                    o = u % 2
                    if kp == 0 and qb == 0 and h + 1 < nh:
                        load_h(h + 1)
                    if n + 2 < len(items):
                        u2, kp2 = items[n + 2]
                        S(units[u2][0], units[u2][1], kp2, n + 2)
                    sl = n % 3
                    tk.op("act", lambda e: e.activation(out=P.t[:, sl], in_=pS_.t[:, sl], func=AF.Exp, scale=SCALE), [pS_.bufs[sl]], [P.bufs[sl]])
                    for j in range(2):
                        kt = kp * 2 + j
                        tk.op("pe", lambda e, j=j, kt=kt: e.matmul(pO.t[:], lhsT=Vh.t[:, s, kt, :], rhs=P.t[:, sl, j, :],
                                                                   start=(kt == 0), stop=(kt == NT - 1)), [Vh.bufs[s], P.bufs[sl]], [pO.b])
                    if kp == NP - 1:
                        tk.op("dve", lambda e: e.tensor_copy(out=oc.t[:, o, :], in_=pO.t[0:65, :]), [pO.b], [oc.bufs[o]])
                        tk.op("dve", lambda e: e.reciprocal(out=rc.t[64:65, o, :], in_=oc.t[64:65, o, :]), [oc.bufs[o]], [rc.bufs[o]])
                        tk.op("pe", lambda e: e.matmul(pB.t[:], lhsT=cst.t[64:65, C_ONE:C_ONE + 64], rhs=rc.t[64:65, o, :], start=True, stop=True),
                              [cst.b, rc.bufs[o]], [pB.b])
                        tk.op("act", lambda e: e.copy(out=bc.t[:, o, :], in_=pB.t[:]), [pB.b], [bc.bufs[o]])
                        tk.op("pool", lambda e: e.tensor_tensor(out=on.t[:, o, :], in0=oc.t[0:64, o, :], in1=bc.t[:, o, :], op=ALU.mult),
                              [oc.bufs[o], bc.bufs[o]], [on.bufs[o]])
                        tk.dma(mixT_d[4 + h // 2, (h % 2) * 64:(h % 2) * 64 + 64, qb * 512:(qb + 1) * 512], on.t[:, o, :], reads=[on.bufs[o]], writes=[db("mixT")])
                tk.barrier()

        def phase2c():
            with contextlib.ExitStack() as ph, scope("p2c"):
                wo = sb(ph, "cwo", [128, 8, D], BF16)
                stg = sb(ph, "cstg", [128, 2, D], F32, nb=2)
                gb = sb(ph, "cgb", [128, D], F32)
                mx = sb(ph, "cmx", [128, 2, 8, 256], BF16, nb=2)
                xb = sb(ph, "cxb", [128, 2, 2, D], F32, nb=2)
                tt = sb(ph, "ctt", [128, 2, 512], F32, nb=2)
                pY = ps(ph, "cpY", [128, 2, 512], F32, nb=2)
                wor = w_out_d.rearrange("(kc k) n -> k kc n", k=128)
                for kc in range(8):
                    s = kc % 2
                    tk.dma(stg.t[:, s, :], wor[:, kc, :], writes=[stg.bufs[s]])
                    if kc < 4:
                        tk.op("dve", lambda e, s=s, kc=kc: e.tensor_scalar_mul(out=wo.t[:, kc, :], in0=stg.t[:, s, :], scalar1=cols.t[:, kc, 13:14]),
                              [stg.bufs[s], cols.b], [wo.b])
                    else:
                        tk.op("dve", lambda e, s=s, kc=kc: e.tensor_copy(out=wo.t[:, kc, :], in_=stg.t[:, s, :]), [stg.bufs[s]], [wo.b])
                src = mrow_d[0:1, 5 * D:6 * D]
                tk.dma(gb.t[:], bass.AP(src.tensor, src.offset, [[0, 128], [1, D]]), reads=[db("mrow")], writes=[gb.b])
                nblk = 2 if (debug and "only_ctx" in debug) else L // 256
                def rows_of(bi):
                    return x1_d[CL + bi * 256:CL + (bi + 1) * 256, :].rearrange("(t p) d -> p t d", p=128)

                def load_c(bi):
                    s = bi % 2
                    tk.dma(mx.t[:, s], mixT_d[:, :, bi * 256:(bi + 1) * 256].rearrange("c p n -> p c n"), reads=[db("mixT")], writes=[mx.bufs[s]])
                    tk.dma(xb.t[:, s], rows_of(bi), reads=[Buf()], writes=[xb.bufs[s]])

                load_c(0)
                for bi in range(nblk):
                    s = bi % 2
                    rows = rows_of(bi)
                    if bi + 1 < nblk:
                        load_c(bi + 1)
                    for t in range(2):
                        for hf in range(2):
                            for kc in range(8):
                                tk.op("pe", lambda e, t=t, hf=hf, kc=kc: e.matmul(pY.t[:, hf, :], lhsT=mx.t[:, s, kc, t * 128:(t + 1) * 128],
                                                                                  rhs=wo.t[:, kc, hf * 512:(hf + 1) * 512], start=(kc == 0), stop=(kc == 7)),
                                      [mx.bufs[s], wo.b], [pY.bufs[hf]])
                            tk.op("dve", lambda e, hf=hf: e.tensor_tensor(out=tt.t[:, hf, :], in0=pY.t[:, hf, :], in1=gb.t[:, hf * 512:(hf + 1) * 512], op=ALU.mult),
                                  [pY.bufs[hf], gb.b], [tt.bufs[hf]])
                            tk.op("pool", lambda e, t=t, hf=hf: e.tensor_tensor(out=xb.t[:, s, t, hf * 512:(hf + 1) * 512], in0=xb.t[:, s, t, hf * 512:(hf + 1) * 512],
                                                                                in1=tt.t[:, hf, :], op=ALU.add), [tt.bufs[hf], xb.bufs[s]], [xb.bufs[s]])
                    tk.dma(rows, xb.t[:, s], reads=[xb.bufs[s]], writes=[db("x1")])
                tk.barrier()


        yf_d = dscr("yf", [L, 512], F32)
        hf_d = dscr("hf_dbg", [2, 128, 512], F32)
        ecw_d = dscr("ecw_dbg", [128, 40], F32)
        xdtd_d = dscr("xdtd_dbg", [128, 512], BF16)
        xs_dbg = dscr("xs_dbg", [128, NT, 512], BF16)
        bmt_dbg = dscr("bmt_dbg", [128, NT, 256], BF16)

        def phase2a():
            with contextlib.ExitStack() as ph, scope("p2a"):
                xs = sb(ph, "sxs", [128, NT, 512], BF16)
                bmt = sb(ph, "sbmt", [128, NT, 256], BF16)
                bmT = sb(ph, "sbmT", [128, 2, T], BF16)
                cmT = sb(ph, "scmT", [128, 2, T], BF16)
                aall = sb(ph, "saall", [128, NT, 16], F32)
                anb = sb(ph, "sanb", [128, 16], F32)
                dsk = sb(ph, "sdsk", [128, 8], F32)
                tk.dma(anb.t[:], bass.AP(a_log_d.tensor, 0, [[0, 128], [1, 16]]), writes=[anb.b])
                tk.dma(dsk.t[:], bass.AP(d_skip_d.tensor, 0, [[0, 128], [1, 8]]), writes=[dsk.b])
                tk.op("act", lambda e: e.activation(out=anb.t[:], in_=anb.t[:], func=AF.Exp), [anb.b], [anb.b])
                tk.op("dve", lambda e: e.tensor_scalar_mul(out=anb.t[:], in0=anb.t[:], scalar1=-1.0), [anb.b], [anb.b])
                tk.op("dve", lambda e: e.tensor_tensor(out=aall.t[:], in0=dtv.t[:], in1=anb.t[:].unsqueeze(1).to_broadcast([128, NT, 16]), op=ALU.mult),
                      [dtv.b, anb.b], [aall.b])
                if debug and 'cv0' in debug:
                    tk.barrier()
                    return
                with contextlib.ExitStack() as cv:
                    xr = sb(cv, "sxr", [128, 2, T], BF16, nb=2)
                    acc = sb(cv, "sacc", [128, T], F32)
                    co = sb(cv, "sco", [128, 2, T], BF16, nb=2)
                    pT = ps(cv, "spT", [128, 2, 8, 128], BF16, nb=2)
                    ng = 0
                    for cc in range(8):
                        s = cc % 2
                        tk.dma(xr.t[:, s, :], xbcT_d[cc], reads=[db("xbcT")], writes=[xr.bufs[s]])
                        tk.op("dve", lambda e, cc=cc: e.tensor_scalar(out=acc.t[:], in0=xr.t[:, s, :], scalar1=cols.t[:, cc, 7:8], scalar2=cols.t[:, cc, 10:11],
                                                                      op0=ALU.mult, op1=ALU.add), [xr.bufs[s], cols.b], [acc.b])
                        if debug and 'cv1' in debug:
                            continue
                        for (lo, hi) in ((0, CL), (CL, T)):
                            for j in (0, 1, 3, 4):
                                sh = j - 2
                                o0 = lo + max(0, -sh)
                                o1 = hi - max(0, sh)
                                tk.op("dve", lambda e, cc=cc, j=j, o0=o0, o1=o1, sh=sh: e.scalar_tensor_tensor(
                                    out=acc.t[:, o0:o1], in0=xr.t[:, s, o0 + sh:o1 + sh], scalar=cols.t[:, cc, 5 + j:6 + j], in1=acc.t[:, o0:o1],
                                    op0=ALU.mult, op1=ALU.add), [xr.bufs[s], cols.b, acc.b], [acc.b])
                        if debug and 'cv2' in debug:
                            continue
                        if cc < 4:
                            dest, dbuf = co.t[:, s, :], co.bufs[s]
                        elif cc < 6:
                            dest, dbuf = bmT.t[:, cc - 4, :], bmT.b
                        else:
                            dest, dbuf = cmT.t[:, cc - 6, :], cmT.b
                        tk.op("act", lambda e, dest=dest: e.activation(out=dest, in_=acc.t[:], func=AF.Silu), [acc.b], [dbuf])
                        if debug and 'cv3' in debug:
                            continue
                        if cc < 6:
                            for tg in range(0, NT, 4):
                                n = min(4, NT - tg)
                                sl = ng % 2
                                ng += 1
                                for k in range(n):
                                    t = tg + k
                                    tk.op("pe", lambda e, k=k, t=t, dest=dest: e.transpose(out=pT.t[:, sl, k, :], in_=dest[:, t * 128:(t + 1) * 128], identity=idb.t[:]),
                                          [dbuf, idb.b], [pT.bufs[sl]])
                                if cc < 4:
                                    tk.op("dve" if ng % 2 else "pool" if False else "dve", lambda e, cc=cc, tg=tg, n=n: e.tensor_copy(out=xs.t[:, tg:tg + n, cc * 128:(cc + 1) * 128], in_=pT.t[:, sl, 0:n, :]),
                                          [pT.bufs[sl]], [xs.b])
                                else:
                                    tk.op("dve", lambda e, cc=cc, tg=tg, n=n: e.tensor_copy(out=bmt.t[:, tg:tg + n, (cc - 4) * 128:(cc - 3) * 128], in_=pT.t[:, sl, 0:n, :]),
                                          [pT.bufs[sl]], [bmt.b])
                    tk.barrier()
                if debug and 'xs_dbg' in debug:
                    tk.dma(xs_dbg, xs.t[:], reads=[xs.b], writes=[db('xsd')])
                    tk.dma(bmt_dbg, bmt.t[:], reads=[bmt.b], writes=[db('bmd')])
                if debug and 'conv_only' in debug:
                    return
                with contextlib.ExitStack() as sw:
                    hst = sb(sw, "shst", [128, 512], F32)
                    hbf = sb(sw, "shbf", [128, 2, 512], BF16, nb=2)
                    ec = sb(sw, "sec", [128, 2, 24], F32, nb=2)
                    aexp = sb(sw, "saexp", [128, 8, 128], F32)
                    eseg = sb(sw, "seseg", [128, 8, 128], F32)
                    Mt = sb(sw, "sM", [128, 2, 8, 128], BF16, nb=2)
                    cbm = sb(sw, "scbm", [128, 2, 128], F32)
                    w8 = sb(sw, "sw8", [128, 8], F32)
                    xdt = sb(sw, "sxdt", [128, 2, 8, 64], BF16, nb=2)
                    xdtd = sb(sw, "sxdtd", [128, 2, 8, 64], BF16, nb=2)
                    t1 = sb(sw, "st1", [128, 8, 64], F32)
                    yv = sb(sw, "syv", [128, 2, 512], F32, nb=2)
                    yfi = sb(sw, "syfi", [128, 2, 512], F32, nb=2)
                    zt = sb(sw, "szt", [128, 2, 512], BF16, nb=2)
                    sz = sb(sw, "ssz", [128, 512], F32)
                    sq2 = sb(sw, "ssq2", [128, 512], F32)
                    ss = sb(sw, "sss", [128, 8], F32)
                    ob = sb(sw, "sob", [128, 512], BF16)
                    obT = sb(sw, "sobT", [128, 2, 4, 128], BF16, nb=2)
                    pC = ps(sw, "spC", [128, 512], F32)
                    pSeg = ps(sw, "spSeg", [128, 2, 512], F32)
                    pCB = ps(sw, "spCB", [128, 4, 128], F32)
                    pY = ps(sw, "spY", [128, 512], F32)
                    pYo = ps(sw, "spYo", [128, 512], F32)
                    pSt = ps(sw, "spSt", [128, 512], F32)
                    pTo = ps(sw, "spTo", [128, 8, 128], BF16)
                    nx = 4 if (debug and "only_ctx" in debug) else NT - 2
                    for d in range(2):
                        if d == 0:
                            order = list(range(2 + nx))
                            m_incl, m_excl, m_seg = C_LE, C_GT, C_GT
                        else:
                            order = [1, 0] + list(range(2 + nx - 1, 1, -1))
                            m_incl, m_excl, m_seg = C_GE, C_LT, C_LT
                        tk.op("dve", lambda e: e.memset(hst.t[:], 0.0), [], [hst.b])
                        tk.op("dve", lambda e: e.memset(hbf.t[:, 0, :], 0.0), [], [hbf.bufs[0]])

                        def front(i):
                            c = order[i]
                            s_ = i % 2
                            a = aall.t[:, c, d * 8:(d + 1) * 8]
                            dtc = dtv.t[:, c, d * 8:(d + 1) * 8]
                            for q_, msk in enumerate((m_incl, m_excl, C_ONE)):
                                tk.op("pe", lambda e, q_=q_, msk=msk: e.matmul(pC.t[:, q_ * 8:(q_ + 1) * 8], lhsT=cst.t[:, msk:msk + 128], rhs=a, start=True, stop=True),
                                      [cst.b, aall.b], [pC.b])
                            ecs = ec.t[:, s_, :]
                            tk.op("act", lambda e: e.activation(out=ecs, in_=pC.t[:, 0:24], func=AF.Exp), [pC.b], [ec.bufs[s_]])
                            tk.op("dve", lambda e: e.tensor_tensor(out=w8.t[:], in0=dtc, in1=ecs[:, 8:16], op=ALU.mult), [dtv.b, ec.bufs[s_]], [w8.b])
                            xsc = xs.t[:, c, :].rearrange("p (h d) -> p h d", h=8)
                            tk.op("pool", lambda e: e.tensor_tensor(out=xdtd.t[:, s_], in0=xsc, in1=w8.t[:].unsqueeze(2).to_broadcast([128, 8, 64]), op=ALU.mult),
                                  [xs.b, w8.b], [xdtd.bufs[s_]])
                            if c < 2:
                                return
                            tk.op("dve", lambda e: e.tensor_tensor(out=aexp.t[:], in0=cst.t[:, m_incl:m_incl + 128].unsqueeze(1).to_broadcast([128, 8, 128]),
                                                                   in1=a.unsqueeze(2).to_broadcast([128, 8, 128]), op=ALU.mult), [cst.b, aall.b], [aexp.b])
                            for hh in range(2):
                                tk.op("pe", lambda e, hh=hh: e.matmul(pSeg.t[:, hh, :], lhsT=cst.t[:, m_seg:m_seg + 128],
                                                                      rhs=aexp.t[:, hh * 4:(hh + 1) * 4, :].rearrange("p h i -> p (h i)"), start=True, stop=True),
                                      [cst.b, aexp.b], [pSeg.b])
                            tk.op("act", lambda e: e.activation(out=eseg.t[:].rearrange("p h i -> p (h i)"), in_=pSeg.t[:].rearrange("p a n -> p (a n)"), func=AF.Exp),
                                  [pSeg.b], [eseg.b])
                            for g in range(2):
                                tk.op("pe", lambda e, g=g: e.matmul(pCB.t[:, g, :], lhsT=bmT.t[:, g, c * 128:(c + 1) * 128], rhs=cmT.t[:, g, c * 128:(c + 1) * 128],
                                                                    start=True, stop=True), [bmT.b, cmT.b], [pCB.b])
                            tk.op("dve", lambda e: e.tensor_tensor(out=cbm.t[:], in0=pCB.t[:, 0:2, :], in1=cst.t[:, m_incl:m_incl + 128].unsqueeze(1).to_broadcast([128, 2, 128]), op=ALU.mult),
                                  [pCB.b, cst.b], [cbm.b])
                            tk.op("dve", lambda e: e.tensor_tensor(out=Mt.t[:, s_].rearrange("p (g h) i -> p g h i", g=2), in0=eseg.t[:].rearrange("p (g h) i -> p g h i", g=2),
                                                                   in1=cbm.t[:].unsqueeze(2).to_broadcast([128, 2, 4, 128]), op=ALU.mult), [eseg.b, cbm.b], [Mt.bufs[s_]])
                            tk.op("pool", lambda e: e.tensor_tensor(out=xdt.t[:, s_], in0=xsc, in1=dtc.unsqueeze(2).to_broadcast([128, 8, 64]), op=ALU.mult),
                                  [xs.b, dtv.b], [xdt.bufs[s_]])

                        def back(i):
                            c = order[i]
                            s_ = i % 2
                            ecs = ec.t[:, s_, :]
                            xsc = xs.t[:, c, :].rearrange("p (h d) -> p h d", h=8)
                            for g in range(2):
                                tk.op("pe", lambda e, g=g: e.matmul(pSt.t[:, g * 256:(g + 1) * 256], lhsT=bmt.t[:, c, g * 128:(g + 1) * 128],
                                                                    rhs=xdtd.t[:, s_, g * 4:(g + 1) * 4, :].rearrange("p h d -> p (h d)"), start=True, stop=True),
                                      [bmt.b, xdtd.bufs[s_]], [pSt.b])
                            tk.op("dve", lambda e: e.tensor_tensor(out=hst.t[:].rearrange("p (h d) -> p h d", h=8), in0=hst.t[:].rearrange("p (h d) -> p h d", h=8),
                                                                   in1=ecs[:, 16:24].unsqueeze(2).to_broadcast([128, 8, 64]), op=ALU.mult), [hst.b, ec.bufs[s_]], [hst.b])
                            tk.op("dve", lambda e: e.tensor_tensor(out=hst.t[:], in0=hst.t[:], in1=pSt.t[:], op=ALU.add), [hst.b, pSt.b], [hst.b])
                            tk.op("pool", lambda e: e.tensor_copy(out=hbf.t[:, 1 - s_, :], in_=hst.t[:]), [hst.b], [hbf.bufs[1 - s_]])
                            if c < 2:
                                return
                            for h in range(8):
                                tk.op("pe", lambda e, h=h: e.matmul(pY.t[:, h * 64:(h + 1) * 64], lhsT=Mt.t[:, s_, h, :], rhs=xdt.t[:, s_, h, :], start=True, stop=True),
                                      [Mt.bufs[s_], xdt.bufs[s_]], [pY.b])
                            for g in range(2):
                                tk.op("pe", lambda e, g=g: e.matmul(pYo.t[:, g * 256:(g + 1) * 256], lhsT=cmT.t[:, g, c * 128:(c + 1) * 128], rhs=hbf.t[:, s_, g * 256:(g + 1) * 256],
                                                                    start=True, stop=True), [cmT.b, hbf.bufs[s_]], [pYo.b])
                            ys = i % 2
                            tk.op("dve", lambda e: e.tensor_tensor(out=t1.t[:], in0=pYo.t[:].rearrange("p (h d) -> p h d", h=8),
                                                                   in1=ecs[:, 0:8].unsqueeze(2).to_broadcast([128, 8, 64]), op=ALU.mult), [pYo.b, ec.bufs[s_]], [t1.b])
                            tk.op("dve", lambda e: e.tensor_tensor(out=yv.t[:, ys, :], in0=t1.t[:].rearrange("p h d -> p (h d)"), in1=pY.t[:], op=ALU.add),
                                  [t1.b, pY.b], [yv.bufs[ys]])
                            r0 = (c - 2) * 128
                            if d == 0:
                                tk.dma(yf_d[r0:r0 + 128, :], yv.t[:, ys, :], reads=[yv.bufs[ys]], writes=[db("yf")])
                                return
                            tk.dma(yfi.t[:, ys, :], yf_d[r0:r0 + 128, :], reads=[db("yf")], writes=[yfi.bufs[ys]])
                            tk.dma(zt.t[:, ys, :], z_d[r0:r0 + 128, :], reads=[db("z")], writes=[zt.bufs[ys]])
                            tk.op("pool", lambda e: e.tensor_tensor(out=yv.t[:, ys, :], in0=yv.t[:, ys, :], in1=yfi.t[:, ys, :], op=ALU.add),
                                  [yv.bufs[ys], yfi.bufs[ys]], [yv.bufs[ys]])
                            tk.op("dve", lambda e: e.tensor_tensor(out=t1.t[:], in0=xsc, in1=dsk.t[:].unsqueeze(2).to_broadcast([128, 8, 64]), op=ALU.mult),
                                  [xs.b, dsk.b], [t1.b])
                            tk.op("pool", lambda e: e.tensor_tensor(out=yv.t[:, ys, :], in0=yv.t[:, ys, :], in1=t1.t[:].rearrange("p h d -> p (h d)"), op=ALU.add),
                                  [yv.bufs[ys], t1.b], [yv.bufs[ys]])
                            tk.op("act", lambda e: e.activation(out=sz.t[:], in_=zt.t[:, ys, :], func=AF.Silu), [zt.bufs[ys]], [sz.b])
                            tk.op("dve", lambda e: e.tensor_tensor(out=yv.t[:, ys, :], in0=yv.t[:, ys, :], in1=sz.t[:], op=ALU.mult), [yv.bufs[ys], sz.b], [yv.bufs[ys]])
                            tk.op("pool", lambda e: e.tensor_tensor(out=sq2.t[:], in0=yv.t[:, ys, :], in1=yv.t[:, ys, :], op=ALU.mult), [yv.bufs[ys]], [sq2.b])
                            tk.op("dve", lambda e: e.reduce_sum(out=ss.t[:, 0:1], in_=sq2.t[:], axis=mybir.AxisListType.X), [sq2.b], [ss.b])
                            tk.op("dve", lambda e: e.tensor_scalar(out=ss.t[:, 1:2], in0=ss.t[:, 0:1], scalar1=1.0 / 512, scalar2=EPS, op0=ALU.mult, op1=ALU.add), [ss.b], [ss.b])
                            tk.op("act", lambda e: e.sqrt(out=ss.t[:, 2:3], in_=ss.t[:, 1:2]), [ss.b], [ss.b])
                            tk.op("dve", lambda e: e.reciprocal(out=ss.t[:, 3:4], in_=ss.t[:, 2:3]), [ss.b], [ss.b])
                            tk.op("dve", lambda e: e.tensor_scalar_mul(out=ob.t[:], in0=yv.t[:, ys, :], scalar1=ss.t[:, 3:4]), [yv.bufs[ys], ss.b], [ob.b])
                            for k in range(4):
                                tk.op("pe", lambda e, k=k: e.transpose(out=pTo.t[:, k, :], in_=ob.t[:, k * 128:(k + 1) * 128], identity=idb.t[:]), [ob.b, idb.b], [pTo.b])
                            tk.op("act", lambda e: e.copy(out=obT.t[:, ys], in_=pTo.t[:, 0:4, :]), [pTo.b], [obT.bufs[ys]])
                            tk.dma(mixT_d[0:4, :, r0:r0 + 128].rearrange("c p n -> p c n"), obT.t[:, ys], reads=[obT.bufs[ys]], writes=[db("mixT")])

                        front(0)
                        for i in range(len(order)):
                            if i + 1 < len(order):
                                front(i + 1)
                            back(i)
                    tk.barrier()

        if ENABLE_SSD and not (debug and "skip_ssd" in debug):
            phase2a()
        phase2b()
        phase2c()
        mid.close()
        blocks3 = []
        for i in range(L // 256):
            blocks3.append((x1_d[CL + i * 256:CL + (i + 1) * 256, :], out_d[i * 256:(i + 1) * 256, :], 0, db("x1"), db("out")))
        if debug and "only_ctx" in debug:
            blocks3 = blocks3[:2]
        ffn_phase("f2", w_ffn2_in_d, w_ffn2_out_d, 2, 8, blocks3, True)

        tk.barrier(("sp",))
    return nc


_CONSTS = None


def _consts():
    global _CONSTS
    if _CONSTS is None:
        c = np.zeros((128, NCONST), np.float32)
        k = np.arange(128)[:, None]
        i = np.arange(128)[None, :]
        c[:, C_ID:C_ID + 128] = (k == i)
        c[:, C_LE:C_LE + 128] = (k <= i)
        c[:, C_GT:C_GT + 128] = (k > i)
        c[:, C_GE:C_GE + 128] = (k >= i)
        c[:, C_LT:C_LT + 128] = (k < i)
        c[:, C_ONE:C_ONE + 128] = 1.0
        c[0, C_SEL:C_SEL + 128] = 1.0
        c[1, C_SEL + 128:C_SEL + 256] = 1.0
        half = 16
        inv = 10000.0 ** (-np.arange(0, half, 2, dtype=np.float32) / half)
        pos = np.arange(L)
        row = (pos // 64).astype(np.float32)
        colp = (pos % 64).astype(np.float32)
        rope = np.zeros((2, 32, L), np.float32)
        for base, p in ((0, row), (16, colp)):
            ang = p[None, :] * inv[:, None]
            cs, sn = np.cos(ang), np.sin(ang)
            rope[0, base:base + 8] = cs
            rope[0, base + 8:base + 16] = cs
            rope[1, base:base + 8] = -sn
            rope[1, base + 8:base + 16] = sn
        _CONSTS = (c, rope)
    return _CONSTS


def make_in_maps(inputs, cores=range(8)):
    f = lambda a: np.ascontiguousarray(np.asarray(a, dtype=np.float32))
    c, rope = _consts()
    shared = {
        "w_ada": f(inputs["w_ada"][0]), "b_ada": f(inputs["b_ada"][0]).reshape(1, -1),
        "norm_ffn1": f(inputs["norm_ffn1"][0]).reshape(1, -1), "w_ffn1_in": f(inputs["w_ffn1_in"][0]),
        "w_ffn1_out": f(inputs["w_ffn1_out"][0]), "norm_mix": f(inputs["norm_mix"][0]).reshape(1, -1),
        "w_in": f(inputs["w_in"][0]), "conv_w": f(inputs["conv_w"][0]), "conv_b": f(inputs["conv_b"][0]).reshape(1, -1),
        "dt_bias": f(inputs["dt_bias"][0]).reshape(1, 16), "a_log": f(inputs["a_log"][0]).reshape(1, 16),
        "d_skip": f(inputs["d_skip"][0]).reshape(1, 8), "ssd_norm": f(inputs["ssd_norm"][0]).reshape(1, -1),
        "q_norm": f(inputs["q_norm"][0]).reshape(1, -1), "w_uq": f(inputs["w_uq"][0]),
        "kv_norm": f(inputs["kv_norm"][0]).reshape(1, -1), "w_ukv": f(inputs["w_ukv"][0]), "w_out": f(inputs["w_out"][0]),
        "norm_ffn2": f(inputs["norm_ffn2"][0]).reshape(1, -1), "w_ffn2_in": f(inputs["w_ffn2_in"][0]),
        "w_ffn2_out": f(inputs["w_ffn2_out"][0]), "final_norm": f(inputs["final_norm"]).reshape(1, -1),
        "consts": c, "rope": rope,
    }
    maps = []
    for b in cores:
        m = dict(shared)
        m["x"] = f(inputs["x"][b])
        m["ctx"] = f(inputs["ctx"][b])
        m["cc"] = np.stack([f(inputs["c"][b]), f(inputs["c_ctx"])], 0)
        maps.append(m)
    return maps


def kernel(**inputs):
    nc = build()
    maps = make_in_maps(inputs)
    res = run_bass_kernel_spmd(nc, maps, core_ids=list(range(8)))
    return np.stack([np.asarray(r["out"], dtype=np.float32) for r in res.results], 0)
```

```python
import contextlib
import numpy as np
import ml_dtypes
import concourse.bass as bass
import concourse.mybir as mybir
from concourse.bass_utils import run_bass_kernel_spmd

F32 = mybir.dt.float32
BF16 = mybir.dt.bfloat16
AF = mybir.ActivationFunctionType
ALU = mybir.AluOpType

D = 1024
L = 4096
CL = 256
T = L + CL
NT = T // 128
DFF = 2816
NKF = DFF // 128
EPS = 1e-6
IN_DIM = 2608
SCALE = 96 ** -0.5
ENABLE_SSD = True

C_ID = 0
C_LE = 128
C_GT = 256
C_GE = 384
C_LT = 512
C_ONE = 640
C_SEL = 768
NCONST = 1024


def _merge(d, s):
    for k, v in s.items():
        if d.get(k, 0) < v:
            d[k] = v


class Buf:
    __slots__ = ("name", "w", "r")

    def __init__(self, name=""):
        self.name = name
        self.w = {}
        self.r = {}


class TK:
    def __init__(self, nc, es, n_dma=14):
        self.nc = nc
        self.eng = {"pe": nc.tensor, "act": nc.scalar, "dve": nc.vector, "pool": nc.gpsimd, "sp": nc.sync}
        self.sem = {}
        self.cnt = {}
        for n in ("pe", "act", "dve", "pool"):
            self.sem[n] = es.enter_context(nc.semaphore("s_" + n))
            self.cnt[n] = 0
        self.nd = n_dma
        for i in range(n_dma):
            self.sem[f"d{i}"] = es.enter_context(nc.semaphore(f"s_d{i}"))
            self.cnt[f"d{i}"] = 0
        self.drr = 0
        self.seen = {n: {} for n in self.eng}

    def _wait(self, e, deps):
        eng = self.eng[e]
        seen = self.seen[e]
        for k, v in deps.items():
            if k == e:
                continue
            if seen.get(k, 0) < v:
                eng.wait_ge(self.sem[k], v)
                seen[k] = v

    def op(self, e, fn, reads=(), writes=()):
        deps = {}
        for b in reads:
            _merge(deps, b.w)
        own = deps.get(e, 0)
        if e != "pe" and own > 0 and self.cnt[e] - own < 2 and self.seen[e].get(e, 0) < own:
            self.eng[e].wait_ge(self.sem[e], own)
            self.seen[e][e] = own
        for b in writes:
            _merge(deps, b.r)
            _merge(deps, b.w)
        self._wait(e, deps)
        ins = fn(self.eng[e])
        self.cnt[e] += 1
        ins.then_inc(self.sem[e], 1)
        c = self.cnt[e]
        for b in reads:
            b.r[e] = c
        for b in writes:
            b.w[e] = c
        return ins

    def dma(self, out, in_, reads=(), writes=(), q="sp", **kw):
        k = self.drr
        self.drr = (self.drr + 1) % self.nd
        key = f"d{k}"
        deps = {}
        for b in reads:
            _merge(deps, b.w)
        for b in writes:
            _merge(deps, b.r)
            _merge(deps, b.w)
        if self.cnt[key] > 0:
            deps[key] = max(deps.get(key, 0), self.cnt[key])
        self._wait(q, deps)
        ins = self.eng[q].dma_start(out=out, in_=in_, **kw)
        self.cnt[key] += 16
        ins.then_inc(self.sem[key], 16)
        c = self.cnt[key]
        for b in reads:
            b.r[key] = c
        for b in writes:
            b.w[key] = c
        return ins

    def barrier(self, engines=("pe", "act", "dve", "pool", "sp")):
        for e in engines:
            deps = {k: v for k, v in self.cnt.items() if v > 0}
            self._wait(e, deps)


class SB:
    def __init__(self, t, nb=1):
        self.t = t
        self.bufs = [Buf() for _ in range(nb)]

    @property
    def b(self):
        return self.bufs[0]


def build(debug=None):
    nc = bass.Bass("TRN2", target_bir_lowering=False)

    def din(name, shape, dt=F32):
        return nc.dram_tensor(name, list(shape), dt, kind="ExternalInput").ap()

    x_d = din("x", [L, D])
    ctx_d = din("ctx", [CL, D])
    cc_d = din("cc", [2, D])
    w_ada_d = din("w_ada", [D, 9 * D])
    b_ada_d = din("b_ada", [1, 9 * D])
    norm_ffn1_d = din("norm_ffn1", [1, D])
    w_ffn1_in_d = din("w_ffn1_in", [D, 2 * DFF])
    w_ffn1_out_d = din("w_ffn1_out", [DFF, D])
    norm_mix_d = din("norm_mix", [1, D])
    w_in_d = din("w_in", [D, IN_DIM])
    conv_w_d = din("conv_w", [5, D])
    conv_b_d = din("conv_b", [1, D])
    dt_bias_d = din("dt_bias", [1, 16])
    a_log_d = din("a_log", [1, 16])
    d_skip_d = din("d_skip", [1, 8])
    ssd_norm_d = din("ssd_norm", [1, 512])
    q_norm_d = din("q_norm", [1, 768])
    w_uq_d = din("w_uq", [768, 768])
    kv_norm_d = din("kv_norm", [1, 256])
    w_ukv_d = din("w_ukv", [256, 1024])
    w_out_d = din("w_out", [D, D])
    norm_ffn2_d = din("norm_ffn2", [1, D])
    w_ffn2_in_d = din("w_ffn2_in", [D, 2 * DFF])
    w_ffn2_out_d = din("w_ffn2_out", [DFF, D])
    final_norm_d = din("final_norm", [1, D])
    consts_d = din("consts", [128, NCONST])
    rope_d = din("rope", [2, 32, L])

    out_d = nc.dram_tensor("out", [L, D], F32, kind="ExternalOutput").ap()

    def dscr(name, shape, dt):
        kind = "ExternalOutput" if (debug and name in debug) else "Internal"
        return nc.dram_tensor(name, list(shape), dt, kind=kind).ap()

    mrow_d = dscr("mrow", [2, 9 * D], F32)
    x1_d = dscr("x1", [T, D], F32)

    def scope(name):
        return nc.named_scope(name) if (debug and "scopes" in debug) else contextlib.nullcontext()

    with contextlib.ExitStack() as es:
        tk = TK(nc, es)

        def sb(st, name, shape, dt=F32, nb=1):
            return SB(st.enter_context(nc.sbuf_tensor(name, list(shape), dt)), nb)

        def ps(st, name, shape, dt=F32, nb=1):
            return SB(st.enter_context(nc.psum_tensor(name, list(shape), dt)), nb)

        dram_bufs = {}

        def db(name):
            if name not in dram_bufs:
                dram_bufs[name] = Buf(name)
            return dram_bufs[name]

        cst = sb(es, "cst", [128, NCONST])
        idb = sb(es, "idb", [128, 128], BF16)
        cols = sb(es, "cols", [128, 8, 32])
        modp = sb(es, "modp", [128, 3, 2, 2, 8])
        tk.dma(cst.t[:], consts_d, writes=[cst.b])
        tk.op("dve", lambda e: e.tensor_copy(out=idb.t[:], in_=cst.t[:, C_ID:C_ID + 128]), [cst.b], [idb.b])

        with contextlib.ExitStack() as p0, scope("p0"):
            stage = sb(p0, "stage", [32, D])
            wa = sb(p0, "wa", [128, 4, 8, 512], F32, nb=4)
            mrow = sb(p0, "mrow_sb", [2, 9 * D])
            bada = sb(p0, "bada", [1, 9 * D])
            pm = ps(p0, "pm", [2, 2, 512], F32, nb=2)
            pt = ps(p0, "pt", [128, 8, 32], F32)
            pc = ps(p0, "pc", [128, 72, 2], F32)
            mcol = sb(p0, "mcol", [128, 72, 2])

            tk.op("pool", lambda e: e.memset(stage.t[:], 0.0), [], [stage.b])
            tk.dma(stage.t[0:2, :], cc_d, writes=[stage.b])
            rows = [(2, norm_ffn1_d, D), (3, norm_mix_d, D), (4, norm_ffn2_d, D), (10, conv_b_d, D),
                    (11, q_norm_d, 768), (12, kv_norm_d, 256), (13, ssd_norm_d, 512)]
            for r, src, n in rows:
                tk.dma(stage.t[r:r + 1, 0:n], src, writes=[stage.b])
            tk.dma(stage.t[5:10, :], conv_w_d, writes=[stage.b])
            tk.dma(bada.t[:], b_ada_d, writes=[bada.b])
            tk.op("act", lambda e: e.activation(out=stage.t[0:2, :], in_=stage.t[0:2, :], func=AF.Silu),
                  [stage.b], [stage.b])
            for kc in range(8):
                tk.op("pe", lambda e, kc=kc: e.transpose(out=pt.t[:, kc, :], in_=stage.t[0:32, kc * 128:(kc + 1) * 128],
                                                         identity=cst.t[0:32, C_ID:C_ID + 32]),
                      [stage.b, cst.b], [pt.b])
            tk.op("dve", lambda e: e.tensor_copy(out=cols.t[:], in_=pt.t[:]), [pt.b], [cols.b])

            war = w_ada_d.rearrange("(kc k) n -> k kc n", k=128)
            for nb in range(18):
                s = nb % 4
                tk.dma(wa.t[:, s], war[:, :, nb * 512:(nb + 1) * 512], writes=[wa.bufs[s]])
                pb = nb % 2
                for kc in range(8):
                    tk.op("pe", lambda e, kc=kc, s=s, pb=pb: e.matmul(pm.t[:, pb, :], lhsT=cols.t[:, kc, 0:2],
                                                                     rhs=wa.t[:, s, kc, :], start=(kc == 0), stop=False),
                          [cols.b, wa.bufs[s]], [pm.bufs[pb]])
                tk.op("pe", lambda e, pb=pb, nb=nb: e.matmul(pm.t[:, pb, :], lhsT=cst.t[0:1, C_ONE:C_ONE + 2],
                                                            rhs=bada.t[0:1, nb * 512:(nb + 1) * 512], start=False, stop=True),
                      [cst.b, bada.b], [pm.bufs[pb]])
                tk.op("dve", lambda e, pb=pb, nb=nb: e.tensor_copy(out=mrow.t[:, nb * 512:(nb + 1) * 512], in_=pm.t[:, pb, :]),
                      [pm.bufs[pb]], [mrow.b])
            tk.dma(mrow_d, mrow.t[:], reads=[mrow.b], writes=[db("mrow")])
            for j in range(72):
                tk.op("pe", lambda e, j=j: e.transpose(out=pc.t[:, j, :], in_=mrow.t[0:2, j * 128:(j + 1) * 128],
                                                       identity=cst.t[0:2, C_ID:C_ID + 2]),
                      [mrow.b, cst.b], [pc.b])
            tk.op("dve", lambda e: e.tensor_copy(out=mcol.t[:], in_=pc.t[:]), [pc.b], [mcol.b])
            for s_ in range(3):
                for b_ in range(2):
                    tk.op("dve", lambda e, s_=s_, b_=b_: e.scalar_tensor_tensor(
                        out=modp.t[:, s_, b_, 0, :], in0=mcol.t[:, (3 * s_ + 1) * 8:(3 * s_ + 2) * 8, b_], scalar=1.0,
                        in1=cols.t[:, :, 2 + s_], op0=ALU.add, op1=ALU.mult), [mcol.b, cols.b], [modp.b])
                    tk.op("dve", lambda e, s_=s_, b_=b_: e.tensor_copy(
                        out=modp.t[:, s_, b_, 1, :], in_=mcol.t[:, (3 * s_) * 8:(3 * s_ + 1) * 8, b_]), [mcol.b], [modp.b])
            tk.barrier()

        def ffn_phase(tag, w_in_dram, w_out_dram, mset, gate_idx, blocks, final):
            with contextlib.ExitStack() as ph, scope(tag):
                wi = sb(ph, tag + "wi", [128, 8, 2 * DFF], BF16)
                wo = sb(ph, tag + "wo", [128, NKF, D], BF16)
                gbs = sb(ph, tag + "gb", [128, 2, D], F32, nb=2)
                fnb = sb(ph, tag + "fnb", [128, D], F32) if final else None
                bs = set(b_ for (_, _, b_, _, _) in blocks)
                for b_ in sorted(bs):
                    src = mrow_d[b_:b_ + 1, gate_idx * D:(gate_idx + 1) * D]
                    tk.dma(gbs.t[:, b_, :], bass.AP(src.tensor, src.offset, [[0, 128], [1, D]]),
                           reads=[db("mrow")], writes=[gbs.bufs[b_]])
                    tk.op("pool", lambda e, b_=b_: e.tensor_scalar_mul(out=gbs.t[:, b_, :], in0=gbs.t[:, b_, :], scalar1=0.5),
                          [gbs.bufs[b_]], [gbs.bufs[b_]])
                if final:
                    tk.dma(fnb.t[:], bass.AP(final_norm_d.tensor, 0, [[0, 128], [1, D]]), writes=[fnb.b])
                with contextlib.ExitStack() as st:
                    stg = sb(st, tag + "stg", [128, 3, 2048], F32, nb=3)
                    pieces = []
                    wir = w_in_dram.rearrange("(kc k) n -> k kc n", k=128)
                    for kc in range(8):
                        for q4 in range(4):
                            pieces.append((wir[:, kc, q4 * 1408:(q4 + 1) * 1408], wi.t[:, kc, q4 * 1408:(q4 + 1) * 1408], 1408, wi.b))
                    wor = w_out_dram.rearrange("(kc k) n -> k kc n", k=128)
                    for kc in range(NKF):
                        pieces.append((wor[:, kc, :], wo.t[:, kc, :], 1024, wo.b))
                    cast_eng = ["dve", "pool", "act"]
                    for i, (src, dst, n, wb) in enumerate(pieces):
                        s = i % 3
                        tk.dma(stg.t[:, s, 0:n], src, writes=[stg.bufs[s]])
                        ce = cast_eng[i % 3]
                        if ce == "act":
                            tk.op("act", lambda e, s=s, n=n, dst=dst: e.copy(out=dst, in_=stg.t[:, s, 0:n]), [stg.bufs[s]], [wb])
                        else:
                            tk.op(ce, lambda e, s=s, n=n, dst=dst: e.tensor_copy(out=dst, in_=stg.t[:, s, 0:n]), [stg.bufs[s]], [wb])
                    tk.barrier()
                xb = sb(ph, tag + "xb", [128, 2, 2, D], F32, nb=2)
                xn = sb(ph, tag + "xn", [128, 2, 2, D], BF16, nb=2)
                hT = sb(ph, tag + "hT", [128, 2, 8, 256], BF16, nb=2)
                tmpm = sb(ph, tag + "tmpm", [128, 8, 128], F32)
                gsb = sb(ph, tag + "g", [128, NKF, 256], BF16)
                sg = sb(ph, tag + "sg", [128, 2, 512], F32, nb=2)
                tt = sb(ph, tag + "tt", [128, 2, 512], F32, nb=2)
                junk = sb(ph, tag + "junk", [128, D], BF16)
                ss = sb(ph, tag + "ss", [128, 2, 8], F32, nb=2)
                pG = ps(ph, tag + "pG", [128, 2, 512], F32, nb=2)
                pU = ps(ph, tag + "pU", [128, 2, 512], F32, nb=2)
                pY = ps(ph, tag + "pY", [128, 2, 512], F32, nb=2)
                pT = ps(ph, tag + "pT", [128, 2, 8, 128], BF16, nb=2)
                nblk = len(blocks)

                def load(i):
                    src, _, _, sbuf, _ = blocks[i]
                    s = i % 2
                    tk.dma(xb.t[:, s], src.rearrange("(t p) d -> p t d", p=128), reads=[sbuf], writes=[xb.bufs[s]])

                def norm(i):
                    s = i % 2
                    sbf = ss.bufs[s]
                    for t in range(2):
                        tk.op("act", lambda e, t=t: e.activation(out=junk.t[:], in_=xb.t[:, s, t, :], func=AF.Square,
                                                                  accum_out=ss.t[:, s, t:t + 1]), [xb.bufs[s]], [junk.b, sbf])
                    tk.op("dve", lambda e: e.tensor_scalar(out=ss.t[:, s, 2:4], in0=ss.t[:, s, 0:2], scalar1=1.0 / D, scalar2=EPS,
                                                           op0=ALU.mult, op1=ALU.add), [sbf], [sbf])
                    tk.op("act", lambda e: e.sqrt(out=ss.t[:, s, 4:6], in_=ss.t[:, s, 2:4]), [sbf], [sbf])
                    tk.op("dve", lambda e: e.reciprocal(out=ss.t[:, s, 6:8], in_=ss.t[:, s, 4:6]), [sbf], [sbf])
                    for t in range(2):
                        tk.op("dve", lambda e, t=t: e.tensor_scalar_mul(out=xn.t[:, s, t, :], in0=xb.t[:, s, t, :],
                                                                        scalar1=ss.t[:, s, 6 + t:7 + t]),
                              [xb.bufs[s], sbf], [xn.bufs[s]])

                def transp(i):
                    s = i % 2
                    b_ = blocks[i][2]
                    for t in range(2):
                        for kc in range(8):
                            tk.op("pe", lambda e, t=t, kc=kc: e.transpose(out=pT.t[:, t, kc, :], in_=xn.t[:, s, t, kc * 128:(kc + 1) * 128],
                                                                            identity=idb.t[:]), [xn.bufs[s], idb.b], [pT.bufs[t]])
                        G = modp.t[:, mset, b_, 0, :].unsqueeze(2).to_broadcast([128, 8, 128])
                        S = modp.t[:, mset, b_, 1, :].unsqueeze(2).to_broadcast([128, 8, 128])
                        tk.op("dve", lambda e, t=t, G=G: e.tensor_tensor(out=tmpm.t[:], in0=pT.t[:, t], in1=G, op=ALU.mult),
                              [pT.bufs[t], modp.b], [tmpm.b])
                        tk.op("pool", lambda e, t=t, S=S: e.tensor_tensor(out=hT.t[:, s, :, t * 128:(t + 1) * 128], in0=tmpm.t[:], in1=S, op=ALU.add),
                              [tmpm.b, modp.b], [hT.bufs[s]])

                def ffn_in(i):
                    s = i % 2
                    for fp in range(NKF // 2):
                        pb = fp % 2
                        for j in range(2):
                            fc = fp * 2 + j
                            for kc in range(8):
                                tk.op("pe", lambda e, j=j, fc=fc, kc=kc: e.matmul(pG.t[:, pb, j * 256:(j + 1) * 256], lhsT=wi.t[:, kc, fc * 128:(fc + 1) * 128],
                                                                                   rhs=hT.t[:, s, kc, :], start=(kc == 0), stop=(kc == 7)),
                                      [wi.b, hT.bufs[s]], [pG.bufs[pb]])
                            for kc in range(8):
                                tk.op("pe", lambda e, j=j, fc=fc, kc=kc: e.matmul(pU.t[:, pb, j * 256:(j + 1) * 256], lhsT=wi.t[:, kc, DFF + fc * 128:DFF + (fc + 1) * 128],
                                                                                   rhs=hT.t[:, s, kc, :], start=(kc == 0), stop=(kc == 7)),
                                      [wi.b, hT.bufs[s]], [pU.bufs[pb]])
                        tk.op("act", lambda e: e.activation(out=sg.t[:, pb, :], in_=pG.t[:, pb, :], func=AF.Silu), [pG.bufs[pb]], [sg.bufs[pb]])
                        tk.op("dve", lambda e, fp=fp: e.tensor_tensor(out=gsb.t[:, fp * 2:fp * 2 + 2, :], in0=sg.t[:, pb, :].rearrange("p (a b) -> p a b", a=2),
                                                                      in1=pU.t[:, pb, :].rearrange("p (a b) -> p a b", a=2), op=ALU.mult),
                              [sg.bufs[pb], pU.bufs[pb]], [gsb.b])

                def ffn_out(i):
                    s = i % 2
                    _, dst, b_, _, dbuf = blocks[i]
                    for t in range(2):
                        for h in range(2):
                            for kc in range(NKF):
                                tk.op("pe", lambda e, t=t, h=h, kc=kc: e.matmul(pY.t[:, h, :], lhsT=gsb.t[:, kc, t * 128:(t + 1) * 128],
                                                                                 rhs=wo.t[:, kc, h * 512:(h + 1) * 512], start=(kc == 0), stop=(kc == NKF - 1)),
                                      [gsb.b, wo.b], [pY.bufs[h]])
                            tk.op("dve", lambda e, h=h: e.tensor_tensor(out=tt.t[:, h, :], in0=pY.t[:, h, :], in1=gbs.t[:, b_, h * 512:(h + 1) * 512], op=ALU.mult),
                                  [pY.bufs[h], gbs.bufs[b_]], [tt.bufs[h]])
                            tk.op("pool", lambda e, t=t, h=h: e.tensor_tensor(out=xb.t[:, s, t, h * 512:(h + 1) * 512], in0=xb.t[:, s, t, h * 512:(h + 1) * 512],
                                                                               in1=tt.t[:, h, :], op=ALU.add), [tt.bufs[h], xb.bufs[s]], [xb.bufs[s]])
                    if final:
                        sbf = ss.bufs[s]
                        for t in range(2):
                            tk.op("act", lambda e, t=t: e.activation(out=junk.t[:], in_=xb.t[:, s, t, :], func=AF.Square,
                                                                      accum_out=ss.t[:, s, t:t + 1]), [xb.bufs[s]], [junk.b, sbf])
                        tk.op("dve", lambda e: e.tensor_scalar(out=ss.t[:, s, 2:4], in0=ss.t[:, s, 0:2], scalar1=1.0 / D, scalar2=EPS,
                                                               op0=ALU.mult, op1=ALU.add), [sbf], [sbf])
                        tk.op("act", lambda e: e.sqrt(out=ss.t[:, s, 4:6], in_=ss.t[:, s, 2:4]), [sbf], [sbf])
                        tk.op("dve", lambda e: e.reciprocal(out=ss.t[:, s, 6:8], in_=ss.t[:, s, 4:6]), [sbf], [sbf])
                        for t in range(2):
                            tk.op("dve", lambda e, t=t: e.scalar_tensor_tensor(out=xb.t[:, s, t, :], in0=xb.t[:, s, t, :], scalar=ss.t[:, s, 6 + t:7 + t],
                                                                               in1=fnb.t[:], op0=ALU.mult, op1=ALU.mult),
                                  [xb.bufs[s], sbf, fnb.b], [xb.bufs[s]])
                    tk.dma(dst.rearrange("(t p) d -> p t d", p=128), xb.t[:, s], reads=[xb.bufs[s]], writes=[dbuf])

                load(0)
                if nblk > 1:
                    load(1)
                norm(0)
                transp(0)
                for i in range(nblk):
                    ffn_in(i)
                    if i + 1 < nblk:
                        norm(i + 1)
                        transp(i + 1)
                    ffn_out(i)
                    if i + 2 < nblk:
                        load(i + 2)
                tk.barrier()

        blocks1 = [(ctx_d, x1_d[0:256, :], 1, Buf(), db("x1"))]
        for i in range(L // 256):
            blocks1.append((x_d[i * 256:(i + 1) * 256, :], x1_d[CL + i * 256:CL + (i + 1) * 256, :], 0, Buf(), db("x1")))
        if debug and "only_ctx" in debug:
            blocks1 = blocks1[:3]
        ffn_phase("f1", w_ffn1_in_d, w_ffn1_out_d, 0, 2, blocks1, False)


        def phase1b():
            with contextlib.ExitStack() as ph, scope("p1b"):
                wp = sb(ph, "wp", [128, 8, 2624], BF16)
                wq = sb(ph, "wq", [128, 6, 8, 96], BF16)
                wqs = sb(ph, "wqs", [128, 6, 8, 96], BF16)
                wkv = sb(ph, "wkv", [128, 2, 1024], BF16)
                wkp = sb(ph, "wkp", [128, 8, 2, 96], BF16)
                dtb = sb(ph, "dtb", [128, 16], F32)
                with contextlib.ExitStack() as st:
                    stg = sb(st, "stg1", [128, 3, IN_DIM], F32, nb=3)
                    kst = sb(st, "kst", [128, 8, 2, 32], F32)
                    wir = w_in_d.rearrange("(kc k) n -> k kc n", k=128)
                    ce = ["dve", "pool"]
                    n = 0
                    for kc in range(8):
                        s = n % 3
                        tk.dma(stg.t[:, s, :], wir[:, kc, :], writes=[stg.bufs[s]])
                        tk.op(ce[n % 2], lambda e, s=s, kc=kc: e.tensor_copy(out=wp.t[:, kc, 0:1552], in_=stg.t[:, s, 0:1552]), [stg.bufs[s]], [wp.b])
                        tk.op(ce[(n + 1) % 2], lambda e, s=s, kc=kc: e.tensor_copy(out=wp.t[:, kc, 1600:2624], in_=stg.t[:, s, 1552:2576]), [stg.bufs[s]], [wp.b])
                        n += 1
                    tk.op("pool", lambda e: e.memset(wkp.t[:], 0.0), [], [wkp.b])
                    tk.dma(kst.t[:, :, 0, :], wir[:, :, 2576:2608], writes=[kst.b])
                    for half in range(2):
                        o = 2576 + 16 * half
                        tk.dma(kst.t[:, :, 1, 16 * half:16 * half + 8], wir[:, :, o + 8:o + 16], writes=[kst.b])
                        tk.dma(kst.t[:, :, 1, 16 * half + 8:16 * half + 16], wir[:, :, o:o + 8], writes=[kst.b])
                    tk.op("dve", lambda e: e.tensor_copy(out=wkp.t[:, :, :, 64:96], in_=kst.t[:]), [kst.b], [wkp.b])
                    wur = w_uq_d.rearrange("(kc k) n -> k kc n", k=128)
                    for kc in range(6):
                        s = n % 3
                        tk.dma(stg.t[:, s, 0:768], wur[:, kc, :], writes=[stg.bufs[s]])
                        tk.op("dve", lambda e, s=s, kc=kc: e.tensor_scalar_mul(out=wq.t[:, kc].rearrange("p h d -> p (h d)"), in0=stg.t[:, s, 0:768],
                                                                               scalar1=cols.t[:, kc, 11:12]), [stg.bufs[s], cols.b], [wq.b])
                        n += 1
                    tk.op("pool", lambda e: e.tensor_copy(out=wqs.t[:], in_=wq.t[:]), [wq.b], [wqs.b])
                    for half in range(2):
                        o = 64 + 16 * half
                        tk.op("dve", lambda e, o=o: e.tensor_copy(out=wqs.t[:, :, :, o:o + 8], in_=wq.t[:, :, :, o + 8:o + 16]), [wq.b], [wqs.b])
                        tk.op("dve", lambda e, o=o: e.tensor_copy(out=wqs.t[:, :, :, o + 8:o + 16], in_=wq.t[:, :, :, o:o + 8]), [wq.b], [wqs.b])
                    wkr = w_ukv_d.rearrange("(kc k) n -> k kc n", k=128)
                    for kc in range(2):
                        s = n % 3
                        tk.dma(stg.t[:, s, 0:1024], wkr[:, kc, :], writes=[stg.bufs[s]])
                        src = stg.t[:, s, 0:1024].rearrange("p (h two d) -> p two h d", h=8, two=2)
                        tk.op("dve", lambda e, src=src, kc=kc: e.tensor_scalar_mul(out=wkv.t[:, kc, :].rearrange("p (two h d) -> p two h d", two=2, h=8), in0=src,
                                                                                  scalar1=cols.t[:, kc, 12:13]), [stg.bufs[s], cols.b], [wkv.b])
                        n += 1
                    tk.dma(dtb.t[:], bass.AP(dt_bias_d.tensor, 0, [[0, 128], [1, 16]]), writes=[dtb.b])
                    tk.barrier()
                if debug and 'stop0' in debug:
                    return
                NB = 512
                xb = sb(ph, "bxb", [128, 2, 4, D], F32, nb=2)
                xn = sb(ph, "bxn", [128, 4, D], BF16)
                hT = sb(ph, "bhT", [128, 2, 8, NB], BF16, nb=2)
                tmpm = sb(ph, "btmpm", [128, 8, 128], F32)
                ss = sb(ph, "bss", [128, 16], F32)
                zt = sb(ph, "bzt", [128, 4, 512], BF16)
                xo = sb(ph, "bxo", [128, 8, NB], BF16)
                cq = sb(ph, "bcq", [128, 6, NB], BF16)
                cqn = sb(ph, "bcqn", [128, 6, NB], BF16)
                ckv = sb(ph, "bckv", [128, 2, NB], BF16)
                ckvn = sb(ph, "bckvn", [128, 2, NB], BF16)
                sq = sb(ph, "bsq", [128, 8, NB], BF16, nb=2)
                rb = sb(ph, "brb", [128, 2, 2, NB], F32, nb=2)
                qo = sb(ph, "bqo", [96, 8, NB], BF16)
                ko = sb(ph, "bko", [128, 4, NB], BF16)
                kpo = sb(ph, "bkpo", [96, NB], BF16)
                vo = sb(ph, "bvo", [128, 4, 512], BF16)
                rp = sb(ph, "brp", [96, 2, NB], F32)
                rt = sb(ph, "brt", [96, 2, 2, NB], F32, nb=2)
                dtt = sb(ph, "bdtt", [128, 4, 16], F32)
                pA = ps(ph, "pA", [128, 4, 512], F32, nb=4)
                pT = ps(ph, "pT1", [128, 8, 128], BF16)
                pS = ps(ph, "pS", [128, 512], F32)
                pD = ps(ph, "pD", [128, 4, 16], F32)
                pai = [0]

                def nxt():
                    pai[0] = (pai[0] + 1) % 4
                    return pai[0]

                blocks = [(x1_d[0:CL, :], CL, 1, 0)] + [(x1_d[CL + i * NB:CL + (i + 1) * NB, :], NB, 0, CL + i * NB) for i in range(L // NB)]
                if debug and "only_ctx" in debug:
                    blocks = blocks[:2]
                def load_b(bi):
                    src_, N_, _, _ = blocks[bi]
                    tk.dma(xb.t[:, bi % 2, 0:N_ // 128], src_.rearrange("(t p) d -> p t d", p=128), reads=[db("x1")], writes=[xb.bufs[bi % 2]])

                def front(bi):
                    src, N, b_, tok0 = blocks[bi]
                    nt = N // 128
                    s = bi % 2
                    for t in range(nt):
                        tk.op("act", lambda e, t=t: e.activation(out=xn.t[:, 3, :], in_=xb.t[:, s, t, :], func=AF.Square, accum_out=ss.t[:, t:t + 1]),
                              [xb.bufs[s]], [xn.b, ss.b])
                    tk.op("dve", lambda e: e.tensor_scalar(out=ss.t[:, 4:8], in0=ss.t[:, 0:4], scalar1=1.0 / D, scalar2=EPS, op0=ALU.mult, op1=ALU.add), [ss.b], [ss.b])
                    tk.op("act", lambda e: e.sqrt(out=ss.t[:, 8:12], in_=ss.t[:, 4:8]), [ss.b], [ss.b])
                    tk.op("dve", lambda e: e.reciprocal(out=ss.t[:, 12:16], in_=ss.t[:, 8:12]), [ss.b], [ss.b])
                    G = modp.t[:, 1, b_, 0, :].unsqueeze(2).to_broadcast([128, 8, 128])
                    S = modp.t[:, 1, b_, 1, :].unsqueeze(2).to_broadcast([128, 8, 128])
                    for t in range(nt):
                        tk.op("dve", lambda e, t=t: e.tensor_scalar_mul(out=xn.t[:, t, :], in0=xb.t[:, s, t, :], scalar1=ss.t[:, 12 + t:13 + t]),
                              [xb.bufs[s], ss.b], [xn.b])
                        for kc in range(8):
                            tk.op("pe", lambda e, t=t, kc=kc: e.transpose(out=pT.t[:, kc, :], in_=xn.t[:, t, kc * 128:(kc + 1) * 128], identity=idb.t[:]),
                                  [xn.b, idb.b], [pT.b])
                        tk.op("dve", lambda e: e.tensor_tensor(out=tmpm.t[:], in0=pT.t[:], in1=G, op=ALU.mult), [pT.b, modp.b], [tmpm.b])
                        tk.op("pool", lambda e, t=t: e.tensor_tensor(out=hT.t[:, s, :, t * 128:(t + 1) * 128], in0=tmpm.t[:], in1=S, op=ALU.add),
                              [tmpm.b, modp.b], [hT.bufs[s]])

                def fm(N, lhs_fn, nk, rhs_fn, M, rd):
                    a = nxt()
                    for kc in range(nk):
                        tk.op("pe", lambda e, kc=kc: e.matmul(pA.t[0:M, a, 0:N], lhsT=lhs_fn(kc), rhs=rhs_fn(kc), start=(kc == 0), stop=(kc == nk - 1)),
                              rd, [pA.bufs[a]])
                    return a

                CQ = ((cq, cqn, 6, 1600, 768.0, 0), (ckv, ckvn, 2, 2368, 256.0, 1))

                def indep(bi):
                    src, N, b_, tok0 = blocks[bi]
                    nt = N // 128
                    s = bi % 2
                    hb_ = hT.bufs[s]
                    if b_ == 0:
                        tk.dma(rp.t[64:96, :, 0:N], rope_d[:, :, tok0 - CL:tok0 - CL + N].rearrange("c d n -> d c n"), writes=[rp.b])
                        for t in range(nt):
                            a = fm(N, lambda kc, t=t: hT.t[:, s, kc, t * 128:(t + 1) * 128], 8, lambda kc: wp.t[:, kc, 0:512], 128, [hb_, wp.b])
                            tk.op("act", lambda e, t=t, a=a: e.copy(out=zt.t[:, t, :], in_=pA.t[:, a, :]), [pA.bufs[a]], [zt.b])
                        r0 = tok0 - CL
                        tk.dma(z_d[r0:r0 + N, :].rearrange("(t p) d -> p t d", p=128), zt.t[:, 0:nt], reads=[zt.b], writes=[db("z")])
                    for cc in range(8):
                        a = fm(N, lambda kc, cc=cc: wp.t[:, kc, 512 + cc * 128:512 + (cc + 1) * 128], 8, lambda kc: hT.t[:, s, kc, 0:N], 128, [hb_, wp.b])
                        tk.op("act" if cc % 2 else "dve",
                              (lambda e, cc=cc, a=a: e.copy(out=xo.t[:, cc, 0:N], in_=pA.t[:, a, 0:N])) if cc % 2 else
                              (lambda e, cc=cc, a=a: e.tensor_copy(out=xo.t[:, cc, 0:N], in_=pA.t[:, a, 0:N])), [pA.bufs[a]], [xo.b])
                    tk.dma(xbcT_d[:, :, tok0:tok0 + N].rearrange("c p n -> p c n"), xo.t[:, :, 0:N], reads=[xo.b], writes=[db("xbcT")])
                    for t in range(nt):
                        for kc in range(8):
                            tk.op("pe", lambda e, t=t, kc=kc: e.matmul(pD.t[:, t, :], lhsT=hT.t[:, s, kc, t * 128:(t + 1) * 128], rhs=wp.t[:, kc, 1536:1552],
                                                                       start=(kc == 0), stop=(kc == 7)), [hb_, wp.b], [pD.b])
                    tg = tok0 // 128
                    tk.op("dve", lambda e: e.tensor_tensor(out=dtt.t[:, 0:nt, :], in0=pD.t[:, 0:nt, :], in1=dtb.t[:].unsqueeze(1).to_broadcast([128, nt, 16]), op=ALU.add),
                          [pD.b, dtb.b], [dtt.b])
                    tk.op("act", lambda e: e.activation(out=dtt.t[:, 0:nt, :], in_=dtt.t[:, 0:nt, :], func=AF.Exp), [dtt.b], [dtt.b])
                    tk.op("act", lambda e: e.activation(out=dtv.t[:, tg:tg + nt, :], in_=dtt.t[:, 0:nt, :], func=AF.Ln, bias=1.0), [dtt.b], [dtv.b])
                    for (dst, dstn, nk, c0, dim, ri) in CQ:
                        for c_ in range(nk):
                            a = fm(N, lambda kc, c_=c_: wp.t[:, kc, c0 + c_ * 128:c0 + (c_ + 1) * 128], 8, lambda kc: hT.t[:, s, kc, 0:N], 128, [hb_, wp.b])
                            tk.op("dve" if c_ % 2 else "act",
                                  (lambda e, c_=c_, a=a: e.tensor_copy(out=dst.t[:, c_, 0:N], in_=pA.t[:, a, 0:N])) if c_ % 2 else
                                  (lambda e, c_=c_, a=a: e.copy(out=dst.t[:, c_, 0:N], in_=pA.t[:, a, 0:N])), [pA.bufs[a]], [dst.b])
                            tk.op("pool", lambda e, c_=c_: e.tensor_tensor(out=sq.t[:, 6 * ri + c_, 0:N], in0=dst.t[:, c_, 0:N], in1=dst.t[:, c_, 0:N], op=ALU.mult), [dst.b], [sq.bufs[ri]])
                    a1 = fm(N, lambda kc: wkp.t[:, kc, 0, :], 8, lambda kc: hT.t[:, s, kc, 0:N], 96, [hb_, wkp.b])
                    if b_ == 0:
                        a2 = fm(N, lambda kc: wkp.t[:, kc, 1, :], 8, lambda kc: hT.t[:, s, kc, 0:N], 96, [hb_, wkp.b])
                        tk.op("dve", lambda e: e.tensor_tensor(out=rt.t[64:96, 0, 0, 0:N], in0=pA.t[64:96, a1, 0:N], in1=rp.t[64:96, 0, 0:N], op=ALU.mult),
                              [pA.bufs[a1], rp.b], [rt.bufs[0]])
                        tk.op("dve", lambda e: e.tensor_tensor(out=rt.t[64:96, 0, 1, 0:N], in0=pA.t[64:96, a2, 0:N], in1=rp.t[64:96, 1, 0:N], op=ALU.mult),
                              [pA.bufs[a2], rp.b], [rt.bufs[0]])
                        tk.op("pool", lambda e: e.tensor_tensor(out=kpo.t[64:96, 0:N], in0=rt.t[64:96, 0, 0, 0:N], in1=rt.t[64:96, 0, 1, 0:N], op=ALU.add),
                              [rt.bufs[0]], [kpo.b])
                    else:
                        tk.op("dve", lambda e: e.tensor_copy(out=kpo.t[64:96, 0:N], in_=pA.t[64:96, a1, 0:N]), [pA.bufs[a1]], [kpo.b])
                    for h in range(8):
                        tk.dma(kT_d[h, 64:96, tok0:tok0 + N], kpo.t[64:96, 0:N], reads=[kpo.b], writes=[db("kT")])

                def dep(bi):
                    src, N, b_, tok0 = blocks[bi]
                    nt = N // 128
                    for (dst, dstn, nk, c0, dim, ri) in CQ:
                        for c_ in range(nk):
                            tk.op("pe", lambda e, c_=c_: e.matmul(pS.t[:, 0:N], lhsT=onb.t[:], rhs=sq.t[:, 6 * ri + c_, 0:N], start=(c_ == 0), stop=(c_ == nk - 1)),
                                  [onb.b, sq.bufs[ri]], [pS.b])
                        tk.op("dve", lambda e: e.tensor_scalar(out=rb.t[:, ri, 0, 0:N], in0=pS.t[:, 0:N], scalar1=1.0 / dim, scalar2=EPS, op0=ALU.mult, op1=ALU.add),
                              [pS.b], [rb.bufs[ri]])
                        tk.op("act", lambda e: e.sqrt(out=rb.t[:, ri, 0, 0:N], in_=rb.t[:, ri, 0, 0:N]), [rb.bufs[ri]], [rb.bufs[ri]])
                        tk.op("dve", lambda e: e.reciprocal(out=rb.t[:, ri, 1, 0:N], in_=rb.t[:, ri, 0, 0:N]), [rb.bufs[ri]], [rb.bufs[ri]])
                        tk.op("dve" if ri else "pool", lambda e: e.tensor_tensor(out=dstn.t[:, :, 0:N], in0=dst.t[:, :, 0:N],
                                                                                  in1=rb.t[:, ri, 1, 0:N].unsqueeze(1).to_broadcast([128, nk, N]), op=ALU.mult),
                              [dst.b, rb.bufs[ri]], [dstn.b])
                    for hp in range(4):
                        a = fm(N, lambda kc, hp=hp: wkv.t[:, kc, hp * 128:(hp + 1) * 128], 2, lambda kc: ckvn.t[:, kc, 0:N], 128, [ckvn.b, wkv.b])
                        tk.op("act", lambda e, hp=hp, a=a: e.copy(out=ko.t[:, hp, 0:N], in_=pA.t[:, a, 0:N]), [pA.bufs[a]], [ko.b])
                    for h in range(8):
                        tk.dma(kT_d[h, 0:64, tok0:tok0 + N], ko.t[(h % 2) * 64:(h % 2) * 64 + 64, h // 2, 0:N], reads=[ko.b], writes=[db("kT")])
                    for t in range(nt):
                        a = nxt()
                        for kc in range(2):
                            tk.op("pe", lambda e, t=t, kc=kc, a=a: e.matmul(pA.t[:, a, :], lhsT=ckvn.t[:, kc, t * 128:(t + 1) * 128], rhs=wkv.t[:, kc, 512:1024],
                                                                            start=(kc == 0), stop=(kc == 1)), [ckvn.b, wkv.b], [pA.bufs[a]])
                        tk.op("dve", lambda e, t=t, a=a: e.tensor_copy(out=vo.t[:, t, :], in_=pA.t[:, a, :]), [pA.bufs[a]], [vo.b])
                    tk.dma(V_d[tok0:tok0 + N, :].rearrange("(t p) d -> p t d", p=128), vo.t[:, 0:nt], reads=[vo.b], writes=[db("V")])
                    if b_ == 0:
                        for h in range(8):
                            a1 = fm(N, lambda kc, h=h: wq.t[:, kc, h, :], 6, lambda kc: cqn.t[:, kc, 0:N], 96, [cqn.b, wq.b])
                            a2 = fm(N, lambda kc, h=h: wqs.t[:, kc, h, :], 6, lambda kc: cqn.t[:, kc, 0:N], 96, [cqn.b, wqs.b])
                            tk.op("dve", lambda e, h=h, a1=a1: e.tensor_copy(out=qo.t[0:64, h, 0:N], in_=pA.t[0:64, a1, 0:N]), [pA.bufs[a1]], [qo.b])
                            r = h % 2
                            tk.op("dve", lambda e, a1=a1, r=r: e.tensor_tensor(out=rt.t[64:96, r, 0, 0:N], in0=pA.t[64:96, a1, 0:N], in1=rp.t[64:96, 0, 0:N], op=ALU.mult),
                                  [pA.bufs[a1], rp.b], [rt.bufs[r]])
                            tk.op("dve", lambda e, a2=a2, r=r: e.tensor_tensor(out=rt.t[64:96, r, 1, 0:N], in0=pA.t[64:96, a2, 0:N], in1=rp.t[64:96, 1, 0:N], op=ALU.mult),
                                  [pA.bufs[a2], rp.b], [rt.bufs[r]])
                            tk.op("pool", lambda e, h=h, r=r: e.tensor_tensor(out=qo.t[64:96, h, 0:N], in0=rt.t[64:96, r, 0, 0:N], in1=rt.t[64:96, r, 1, 0:N], op=ALU.add),
                                  [rt.bufs[r]], [qo.b])
                        r0 = tok0 - CL
                        tk.dma(qT_d[:, :, r0:r0 + N].rearrange("h d n -> d h n"), qo.t[:, :, 0:N], reads=[qo.b], writes=[db("qT")])

                load_b(0)
                front(0)
                for bi in range(len(blocks)):
                    if bi + 1 < len(blocks):
                        load_b(bi + 1)
                    indep(bi)
                    if bi + 1 < len(blocks):
                        front(bi + 1)
                    dep(bi)
                tk.barrier()

        z_d = dscr("z", [L, 512], BF16)
        xbcT_d = dscr("xbcT", [8, 128, T], BF16)
        kT_d = dscr("kT", [8, 96, T], BF16)
        qT_d = dscr("qT", [8, 96, L], BF16)
        V_d = dscr("V", [T, 512], BF16)
        mid = contextlib.ExitStack()
        dtv = sb(mid, "dtv", [128, NT, 16], F32)
        onb = sb(mid, "onb", [128, 128], BF16)
        tk.op("dve", lambda e: e.tensor_copy(out=onb.t[:], in_=cst.t[:, C_ONE:C_ONE + 128]), [cst.b], [onb.b])
        dtv_d = dscr("dtv_dbg", [128, NT, 16], F32)
        phase1b()
        if debug and "dtv_dbg" in debug:
            tk.dma(dtv_d, dtv.t[:], reads=[dtv.b], writes=[db("dtvd")])


        mixT_d = dscr("mixT", [8, 128, L], BF16)

        def phase2b():
            with contextlib.ExitStack() as ph, scope("p2b"):
                kT = sb(ph, "akT", [128, 2, T], BF16, nb=2)
                qT = sb(ph, "aqT", [128, 2, L], BF16, nb=2)
                Vh = sb(ph, "aV", [128, 2, NT, 128], BF16, nb=2)
                P = sb(ph, "aP", [128, 3, 2, 512], BF16, nb=3)
                oc = sb(ph, "aoc", [65, 2, 512], F32, nb=2)
                rc = sb(ph, "arc", [65, 2, 512], F32, nb=2)
                bc = sb(ph, "abc", [64, 2, 512], F32, nb=2)
                on = sb(ph, "aon", [64, 2, 512], BF16, nb=2)
                pS_ = ps(ph, "aS", [128, 3, 2, 512], F32, nb=3)
                pO = ps(ph, "aO", [128, 512], F32)
                pB = ps(ph, "aB", [64, 512], F32)
                for s in range(2):
                    tk.op("pool", lambda e, s=s: e.memset(Vh.t[:, s], 0.0), [], [Vh.bufs[s]])
                    tk.op("pool", lambda e, s=s: e.memset(Vh.t[:, s, :, 64:65], 1.0), [], [Vh.bufs[s]])
                    tk.op("dve", lambda e, s=s: e.memset(kT.t[:, s, :], 0.0), [], [kT.bufs[s]])
                    tk.op("dve", lambda e, s=s: e.memset(qT.t[:, s, :], 0.0), [], [qT.bufs[s]])
                Vr = V_d.rearrange("(t p) (h d) -> p t h d", p=128, h=8)
                nh = 8
                nqb = 1 if (debug and "only_ctx" in debug) else 8
                NP = NT // 2

                def load_h(h):
                    s = h % 2
                    tk.dma(kT.t[0:96, s, :], kT_d[h], reads=[db("kT")], writes=[kT.bufs[s]])
                    tk.dma(qT.t[0:96, s, :], qT_d[h], reads=[db("qT")], writes=[qT.bufs[s]])
                    tk.dma(Vh.t[:, s, :, 0:64], Vr[:, :, h, :], reads=[db("V")], writes=[Vh.bufs[s]])

                units = [(h, qb) for h in range(nh) for qb in range(nqb)]
                gi = [0]

                def S(h, qb, kp, g):
                    s = h % 2
                    sl = g % 3
                    for j in range(2):
                        kt = kp * 2 + j
                        tk.op("pe", lambda e, j=j, kt=kt: e.matmul(pS_.t[:, sl, j, :], lhsT=kT.t[:, s, kt * 128:(kt + 1) * 128],
                                                                   rhs=qT.t[:, s, qb * 512:(qb + 1) * 512], start=True, stop=True),
                              [kT.bufs[s], qT.bufs[s]], [pS_.bufs[sl]])

                items = [(u, kp) for u in range(len(units)) for kp in range(NP)]
                load_h(0)
                for n_ahead in range(min(2, len(items))):
                    u, kp = items[n_ahead]
                    S(units[u][0], units[u][1], kp, n_ahead)
                for n, (u, kp) in enumerate(items):
                    h, qb = units[u]
                    s = h % 2
                    o = u % 2
                    if kp == 0 and qb == 0 and h + 1 < nh:
                        load_h(h + 1)
                    if n + 2 < len(items):
                        u2, kp2 = items[n + 2]
                        S(units[u2][0], units[u2][1], kp2, n + 2)
                    sl = n % 3
                    tk.op("act", lambda e: e.activation(out=P.t[:, sl], in_=pS_.t[:, sl], func=AF.Exp, scale=SCALE), [pS_.bufs[sl]], [P.bufs[sl]])
                    for j in range(2):
                        kt = kp * 2 + j
                        tk.op("pe", lambda e, j=j, kt=kt: e.matmul(pO.t[:], lhsT=Vh.t[:, s, kt, :], rhs=P.t[:, sl, j, :],
                                                                   start=(kt == 0), stop=(kt == NT - 1)), [Vh.bufs[s], P.bufs[sl]], [pO.b])
                    if kp == NP - 1:
                        tk.op("dve", lambda e: e.tensor_copy(out=oc.t[:, o, :], in_=pO.t[0:65, :]), [pO.b], [oc.bufs[o]])
                        tk.op("dve", lambda e: e.reciprocal(out=rc.t[64:65, o, :], in_=oc.t[64:65, o, :]), [oc.bufs[o]], [rc.bufs[o]])
                        tk.op("pe", lambda e: e.matmul(pB.t[:], lhsT=cst.t[64:65, C_ONE:C_ONE + 64], rhs=rc.t[64:65, o, :], start=True, stop=True),
                              [cst.b, rc.bufs[o]], [pB.b])
                        tk.op("act", lambda e: e.copy(out=bc.t[:, o, :], in_=pB.t[:]), [pB.b], [bc.bufs[o]])
                        tk.op("pool", lambda e: e.tensor_tensor(out=on.t[:, o, :], in0=oc.t[0:64, o, :], in1=bc.t[:, o, :], op=ALU.mult),
                              [oc.bufs[o], bc.bufs[o]], [on.bufs[o]])
                        tk.dma(mixT_d[4 + h // 2, (h % 2) * 64:(h % 2) * 64 + 64, qb * 512:(qb + 1) * 512], on.t[:, o, :], reads=[on.bufs[o]], writes=[db("mixT")])
                tk.barrier()

        def phase2c():
            with contextlib.ExitStack() as ph, scope("p2c"):
                wo = sb(ph, "cwo", [128, 8, D], BF16)
                stg = sb(ph, "cstg", [128, 2, D], F32, nb=2)
                gb = sb(ph, "cgb", [128, D], F32)
                mx = sb(ph, "cmx", [128, 2, 8, 256], BF16, nb=2)
                xb = sb(ph, "cxb", [128, 2, 2, D], F32, nb=2)
                tt = sb(ph, "ctt", [128, 2, 512], F32, nb=2)
                pY = ps(ph, "cpY", [128, 2, 512], F32, nb=2)
                wor = w_out_d.rearrange("(kc k) n -> k kc n", k=128)
                for kc in range(8):
                    s = kc % 2
                    tk.dma(stg.t[:, s, :], wor[:, kc, :], writes=[stg.bufs[s]])
                    if kc < 4:
                        tk.op("dve", lambda e, s=s, kc=kc: e.tensor_scalar_mul(out=wo.t[:, kc, :], in0=stg.t[:, s, :], scalar1=cols.t[:, kc, 13:14]),
                              [stg.bufs[s], cols.b], [wo.b])
                    else:
                        tk.op("dve", lambda e, s=s, kc=kc: e.tensor_copy(out=wo.t[:, kc, :], in_=stg.t[:, s, :]), [stg.bufs[s]], [wo.b])
                src = mrow_d[0:1, 5 * D:6 * D]
                tk.dma(gb.t[:], bass.AP(src.tensor, src.offset, [[0, 128], [1, D]]), reads=[db("mrow")], writes=[gb.b])
                nblk = 2 if (debug and "only_ctx" in debug) else L // 256
                def rows_of(bi):
                    return x1_d[CL + bi * 256:CL + (bi + 1) * 256, :].rearrange("(t p) d -> p t d", p=128)

                def load_c(bi):
                    s = bi % 2
                    tk.dma(mx.t[:, s], mixT_d[:, :, bi * 256:(bi + 1) * 256].rearrange("c p n -> p c n"), reads=[db("mixT")], writes=[mx.bufs[s]])
                    tk.dma(xb.t[:, s], rows_of(bi), reads=[Buf()], writes=[xb.bufs[s]])

                load_c(0)
                for bi in range(nblk):
                    s = bi % 2
                    rows = rows_of(bi)
                    if bi + 1 < nblk:
                        load_c(bi + 1)
                    for t in range(2):
                        for hf in range(2):
                            for kc in range(8):
                                tk.op("pe", lambda e, t=t, hf=hf, kc=kc: e.matmul(pY.t[:, hf, :], lhsT=mx.t[:, s, kc, t * 128:(t + 1) * 128],
                                                                                  rhs=wo.t[:, kc, hf * 512:(hf + 1) * 512], start=(kc == 0), stop=(kc == 7)),
                                      [mx.bufs[s], wo.b], [pY.bufs[hf]])
                            tk.op("dve", lambda e, hf=hf: e.tensor_tensor(out=tt.t[:, hf, :], in0=pY.t[:, hf, :], in1=gb.t[:, hf * 512:(hf + 1) * 512], op=ALU.mult),
                                  [pY.bufs[hf], gb.b], [tt.bufs[hf]])
                            tk.op("pool", lambda e, t=t, hf=hf: e.tensor_tensor(out=xb.t[:, s, t, hf * 512:(hf + 1) * 512], in0=xb.t[:, s, t, hf * 512:(hf + 1) * 512],
                                                                                in1=tt.t[:, hf, :], op=ALU.add), [tt.bufs[hf], xb.bufs[s]], [xb.bufs[s]])
                    tk.dma(rows, xb.t[:, s], reads=[xb.bufs[s]], writes=[db("x1")])
                tk.barrier()


        yf_d = dscr("yf", [L, 512], F32)
        hf_d = dscr("hf_dbg", [2, 128, 512], F32)
        ecw_d = dscr("ecw_dbg", [128, 40], F32)
        xdtd_d = dscr("xdtd_dbg", [128, 512], BF16)
        xs_dbg = dscr("xs_dbg", [128, NT, 512], BF16)
        bmt_dbg = dscr("bmt_dbg", [128, NT, 256], BF16)

        def phase2a():
            with contextlib.ExitStack() as ph, scope("p2a"):
                xs = sb(ph, "sxs", [128, NT, 512], BF16)
                bmt = sb(ph, "sbmt", [128, NT, 256], BF16)
                bmT = sb(ph, "sbmT", [128, 2, T], BF16)
                cmT = sb(ph, "scmT", [128, 2, T], BF16)
                aall = sb(ph, "saall", [128, NT, 16], F32)
                anb = sb(ph, "sanb", [128, 16], F32)
                dsk = sb(ph, "sdsk", [128, 8], F32)
                tk.dma(anb.t[:], bass.AP(a_log_d.tensor, 0, [[0, 128], [1, 16]]), writes=[anb.b])
                tk.dma(dsk.t[:], bass.AP(d_skip_d.tensor, 0, [[0, 128], [1, 8]]), writes=[dsk.b])
                tk.op("act", lambda e: e.activation(out=anb.t[:], in_=anb.t[:], func=AF.Exp), [anb.b], [anb.b])
                tk.op("dve", lambda e: e.tensor_scalar_mul(out=anb.t[:], in0=anb.t[:], scalar1=-1.0), [anb.b], [anb.b])
                tk.op("dve", lambda e: e.tensor_tensor(out=aall.t[:], in0=dtv.t[:], in1=anb.t[:].unsqueeze(1).to_broadcast([128, NT, 16]), op=ALU.mult),
                      [dtv.b, anb.b], [aall.b])
                if debug and 'cv0' in debug:
                    tk.barrier()
                    return
                with contextlib.ExitStack() as cv:
                    xr = sb(cv, "sxr", [128, 2, T], BF16, nb=2)
                    acc = sb(cv, "sacc", [128, T], F32)
                    co = sb(cv, "sco", [128, 2, T], BF16, nb=2)
                    pT = ps(cv, "spT", [128, 2, 8, 128], BF16, nb=2)
                    ng = 0
                    for cc in range(8):
                        s = cc % 2
                        tk.dma(xr.t[:, s, :], xbcT_d[cc], reads=[db("xbcT")], writes=[xr.bufs[s]])
                        tk.op("dve", lambda e, cc=cc: e.tensor_scalar(out=acc.t[:], in0=xr.t[:, s, :], scalar1=cols.t[:, cc, 7:8], scalar2=cols.t[:, cc, 10:11],
                                                                      op0=ALU.mult, op1=ALU.add), [xr.bufs[s], cols.b], [acc.b])
                        if debug and 'cv1' in debug:
                            continue
                        for (lo, hi) in ((0, CL), (CL, T)):
                            for j in (0, 1, 3, 4):
                                sh = j - 2
                                o0 = lo + max(0, -sh)
                                o1 = hi - max(0, sh)
                                tk.op("dve", lambda e, cc=cc, j=j, o0=o0, o1=o1, sh=sh: e.scalar_tensor_tensor(
                                    out=acc.t[:, o0:o1], in0=xr.t[:, s, o0 + sh:o1 + sh], scalar=cols.t[:, cc, 5 + j:6 + j], in1=acc.t[:, o0:o1],
                                    op0=ALU.mult, op1=ALU.add), [xr.bufs[s], cols.b, acc.b], [acc.b])
                        if debug and 'cv2' in debug:
                            continue
                        if cc < 4:
                            dest, dbuf = co.t[:, s, :], co.bufs[s]
                        elif cc < 6:
                            dest, dbuf = bmT.t[:, cc - 4, :], bmT.b
                        else:
                            dest, dbuf = cmT.t[:, cc - 6, :], cmT.b
                        tk.op("act", lambda e, dest=dest: e.activation(out=dest, in_=acc.t[:], func=AF.Silu), [acc.b], [dbuf])
                        if debug and 'cv3' in debug:
                            continue
                        if cc < 6:
                            for tg in range(0, NT, 4):
                                n = min(4, NT - tg)
                                sl = ng % 2
                                ng += 1
                                for k in range(n):
                                    t = tg + k
                                    tk.op("pe", lambda e, k=k, t=t, dest=dest: e.transpose(out=pT.t[:, sl, k, :], in_=dest[:, t * 128:(t + 1) * 128], identity=idb.t[:]),
                                          [dbuf, idb.b], [pT.bufs[sl]])
                                if cc < 4:
                                    tk.op("dve" if ng % 2 else "pool" if False else "dve", lambda e, cc=cc, tg=tg, n=n: e.tensor_copy(out=xs.t[:, tg:tg + n, cc * 128:(cc + 1) * 128], in_=pT.t[:, sl, 0:n, :]),
                                          [pT.bufs[sl]], [xs.b])
                                else:
                                    tk.op("dve", lambda e, cc=cc, tg=tg, n=n: e.tensor_copy(out=bmt.t[:, tg:tg + n, (cc - 4) * 128:(cc - 3) * 128], in_=pT.t[:, sl, 0:n, :]),
                                          [pT.bufs[sl]], [bmt.b])
                    tk.barrier()
                if debug and 'xs_dbg' in debug:
                    tk.dma(xs_dbg, xs.t[:], reads=[xs.b], writes=[db('xsd')])
                    tk.dma(bmt_dbg, bmt.t[:], reads=[bmt.b], writes=[db('bmd')])
                if debug and 'conv_only' in debug:
                    return
                with contextlib.ExitStack() as sw:
                    hst = sb(sw, "shst", [128, 512], F32)
                    hbf = sb(sw, "shbf", [128, 2, 512], BF16, nb=2)
                    ec = sb(sw, "sec", [128, 2, 24], F32, nb=2)
                    aexp = sb(sw, "saexp", [128, 8, 128], F32)
                    eseg = sb(sw, "seseg", [128, 8, 128], F32)
                    Mt = sb(sw, "sM", [128, 2, 8, 128], BF16, nb=2)
                    cbm = sb(sw, "scbm", [128, 2, 128], F32)
                    w8 = sb(sw, "sw8", [128, 8], F32)
                    xdt = sb(sw, "sxdt", [128, 2, 8, 64], BF16, nb=2)
                    xdtd = sb(sw, "sxdtd", [128, 2, 8, 64], BF16, nb=2)
                    t1 = sb(sw, "st1", [128, 8, 64], F32)
                    yv = sb(sw, "syv", [128, 2, 512], F32, nb=2)
                    yfi = sb(sw, "syfi", [128, 2, 512], F32, nb=2)
                    zt = sb(sw, "szt", [128, 2, 512], BF16, nb=2)
                    sz = sb(sw, "ssz", [128, 512], F32)
                    sq2 = sb(sw, "ssq2", [128, 512], F32)
                    ss = sb(sw, "sss", [128, 8], F32)
                    ob = sb(sw, "sob", [128, 512], BF16)
                    obT = sb(sw, "sobT", [128, 2, 4, 128], BF16, nb=2)
                    pC = ps(sw, "spC", [128, 512], F32)
                    pSeg = ps(sw, "spSeg", [128, 2, 512], F32)
                    pCB = ps(sw, "spCB", [128, 4, 128], F32)
                    pY = ps(sw, "spY", [128, 512], F32)
                    pYo = ps(sw, "spYo", [128, 512], F32)
                    pSt = ps(sw, "spSt", [128, 512], F32)
                    pTo = ps(sw, "spTo", [128, 8, 128], BF16)
                    nx = 4 if (debug and "only_ctx" in debug) else NT - 2
                    for d in range(2):
                        if d == 0:
                            order = list(range(2 + nx))
                            m_incl, m_excl, m_seg = C_LE, C_GT, C_GT
                        else:
                            order = [1, 0] + list(range(2 + nx - 1, 1, -1))
                            m_incl, m_excl, m_seg = C_GE, C_LT, C_LT
                        tk.op("dve", lambda e: e.memset(hst.t[:], 0.0), [], [hst.b])
                        tk.op("dve", lambda e: e.memset(hbf.t[:, 0, :], 0.0), [], [hbf.bufs[0]])

                        def front(i):
                            c = order[i]
                            s_ = i % 2
                            a = aall.t[:, c, d * 8:(d + 1) * 8]
                            dtc = dtv.t[:, c, d * 8:(d + 1) * 8]
                            for q_, msk in enumerate((m_incl, m_excl, C_ONE)):
                                tk.op("pe", lambda e, q_=q_, msk=msk: e.matmul(pC.t[:, q_ * 8:(q_ + 1) * 8], lhsT=cst.t[:, msk:msk + 128], rhs=a, start=True, stop=True),
                                      [cst.b, aall.b], [pC.b])
                            ecs = ec.t[:, s_, :]
                            tk.op("act", lambda e: e.activation(out=ecs, in_=pC.t[:, 0:24], func=AF.Exp), [pC.b], [ec.bufs[s_]])
                            tk.op("dve", lambda e: e.tensor_tensor(out=w8.t[:], in0=dtc, in1=ecs[:, 8:16], op=ALU.mult), [dtv.b, ec.bufs[s_]], [w8.b])
                            xsc = xs.t[:, c, :].rearrange("p (h d) -> p h d", h=8)
                            tk.op("pool", lambda e: e.tensor_tensor(out=xdtd.t[:, s_], in0=xsc, in1=w8.t[:].unsqueeze(2).to_broadcast([128, 8, 64]), op=ALU.mult),
                                  [xs.b, w8.b], [xdtd.bufs[s_]])
                            if c < 2:
                                return
                            tk.op("dve", lambda e: e.tensor_tensor(out=aexp.t[:], in0=cst.t[:, m_incl:m_incl + 128].unsqueeze(1).to_broadcast([128, 8, 128]),
                                                                   in1=a.unsqueeze(2).to_broadcast([128, 8, 128]), op=ALU.mult), [cst.b, aall.b], [aexp.b])
                            for hh in range(2):
                                tk.op("pe", lambda e, hh=hh: e.matmul(pSeg.t[:, hh, :], lhsT=cst.t[:, m_seg:m_seg + 128],
                                                                      rhs=aexp.t[:, hh * 4:(hh + 1) * 4, :].rearrange("p h i -> p (h i)"), start=True, stop=True),
                                      [cst.b, aexp.b], [pSeg.b])
                            tk.op("act", lambda e: e.activation(out=eseg.t[:].rearrange("p h i -> p (h i)"), in_=pSeg.t[:].rearrange("p a n -> p (a n)"), func=AF.Exp),
                                  [pSeg.b], [eseg.b])
                            for g in range(2):
                                tk.op("pe", lambda e, g=g: e.matmul(pCB.t[:, g, :], lhsT=bmT.t[:, g, c * 128:(c + 1) * 128], rhs=cmT.t[:, g, c * 128:(c + 1) * 128],
                                                                    start=True, stop=True), [bmT.b, cmT.b], [pCB.b])
                            tk.op("dve", lambda e: e.tensor_tensor(out=cbm.t[:], in0=pCB.t[:, 0:2, :], in1=cst.t[:, m_incl:m_incl + 128].unsqueeze(1).to_broadcast([128, 2, 128]), op=ALU.mult),
                                  [pCB.b, cst.b], [cbm.b])
                            tk.op("dve", lambda e: e.tensor_tensor(out=Mt.t[:, s_].rearrange("p (g h) i -> p g h i", g=2), in0=eseg.t[:].rearrange("p (g h) i -> p g h i", g=2),
                                                                   in1=cbm.t[:].unsqueeze(2).to_broadcast([128, 2, 4, 128]), op=ALU.mult), [eseg.b, cbm.b], [Mt.bufs[s_]])
                            tk.op("pool", lambda e: e.tensor_tensor(out=xdt.t[:, s_], in0=xsc, in1=dtc.unsqueeze(2).to_broadcast([128, 8, 64]), op=ALU.mult),
                                  [xs.b, dtv.b], [xdt.bufs[s_]])

                        def back(i):
                            c = order[i]
                            s_ = i % 2
                            ecs = ec.t[:, s_, :]
                            xsc = xs.t[:, c, :].rearrange("p (h d) -> p h d", h=8)
                            for g in range(2):
                                tk.op("pe", lambda e, g=g: e.matmul(pSt.t[:, g * 256:(g + 1) * 256], lhsT=bmt.t[:, c, g * 128:(g + 1) * 128],
                                                                    rhs=xdtd.t[:, s_, g * 4:(g + 1) * 4, :].rearrange("p h d -> p (h d)"), start=True, stop=True),
                                      [bmt.b, xdtd.bufs[s_]], [pSt.b])
                            tk.op("dve", lambda e: e.tensor_tensor(out=hst.t[:].rearrange("p (h d) -> p h d", h=8), in0=hst.t[:].rearrange("p (h d) -> p h d", h=8),
                                                                   in1=ecs[:, 16:24].unsqueeze(2).to_broadcast([128, 8, 64]), op=ALU.mult), [hst.b, ec.bufs[s_]], [hst.b])
                            tk.op("dve", lambda e: e.tensor_tensor(out=hst.t[:], in0=hst.t[:], in1=pSt.t[:], op=ALU.add), [hst.b, pSt.b], [hst.b])
                            tk.op("pool", lambda e: e.tensor_copy(out=hbf.t[:, 1 - s_, :], in_=hst.t[:]), [hst.b], [hbf.bufs[1 - s_]])
                            if c < 2:
                                return
                            for h in range(8):
                                tk.op("pe", lambda e, h=h: e.matmul(pY.t[:, h * 64:(h + 1) * 64], lhsT=Mt.t[:, s_, h, :], rhs=xdt.t[:, s_, h, :], start=True, stop=True),
                                      [Mt.bufs[s_], xdt.bufs[s_]], [pY.b])
                            for g in range(2):
                                tk.op("pe", lambda e, g=g: e.matmul(pYo.t[:, g * 256:(g + 1) * 256], lhsT=cmT.t[:, g, c * 128:(c + 1) * 128], rhs=hbf.t[:, s_, g * 256:(g + 1) * 256],
                                                                    start=True, stop=True), [cmT.b, hbf.bufs[s_]], [pYo.b])
                            ys = i % 2
                            tk.op("dve", lambda e: e.tensor_tensor(out=t1.t[:], in0=pYo.t[:].rearrange("p (h d) -> p h d", h=8),
                                                                   in1=ecs[:, 0:8].unsqueeze(2).to_broadcast([128, 8, 64]), op=ALU.mult), [pYo.b, ec.bufs[s_]], [t1.b])
                            tk.op("dve", lambda e: e.tensor_tensor(out=yv.t[:, ys, :], in0=t1.t[:].rearrange("p h d -> p (h d)"), in1=pY.t[:], op=ALU.add),
                                  [t1.b, pY.b], [yv.bufs[ys]])
                            r0 = (c - 2) * 128
                            if d == 0:
                                tk.dma(yf_d[r0:r0 + 128, :], yv.t[:, ys, :], reads=[yv.bufs[ys]], writes=[db("yf")])
                                return
                            tk.dma(yfi.t[:, ys, :], yf_d[r0:r0 + 128, :], reads=[db("yf")], writes=[yfi.bufs[ys]])
                            tk.dma(zt.t[:, ys, :], z_d[r0:r0 + 128, :], reads=[db("z")], writes=[zt.bufs[ys]])
                            tk.op("pool", lambda e: e.tensor_tensor(out=yv.t[:, ys, :], in0=yv.t[:, ys, :], in1=yfi.t[:, ys, :], op=ALU.add),
                                  [yv.bufs[ys], yfi.bufs[ys]], [yv.bufs[ys]])
                            tk.op("dve", lambda e: e.tensor_tensor(out=t1.t[:], in0=xsc, in1=dsk.t[:].unsqueeze(2).to_broadcast([128, 8, 64]), op=ALU.mult),
                                  [xs.b, dsk.b], [t1.b])
                            tk.op("pool", lambda e: e.tensor_tensor(out=yv.t[:, ys, :], in0=yv.t[:, ys, :], in1=t1.t[:].rearrange("p h d -> p (h d)"), op=ALU.add),
                                  [yv.bufs[ys], t1.b], [yv.bufs[ys]])
                            tk.op("act", lambda e: e.activation(out=sz.t[:], in_=zt.t[:, ys, :], func=AF.Silu), [zt.bufs[ys]], [sz.b])
                            tk.op("dve", lambda e: e.tensor_tensor(out=yv.t[:, ys, :], in0=yv.t[:, ys, :], in1=sz.t[:], op=ALU.mult), [yv.bufs[ys], sz.b], [yv.bufs[ys]])
                            tk.op("pool", lambda e: e.tensor_tensor(out=sq2.t[:], in0=yv.t[:, ys, :], in1=yv.t[:, ys, :], op=ALU.mult), [yv.bufs[ys]], [sq2.b])
                            tk.op("dve", lambda e: e.reduce_sum(out=ss.t[:, 0:1], in_=sq2.t[:], axis=mybir.AxisListType.X), [sq2.b], [ss.b])
                            tk.op("dve", lambda e: e.tensor_scalar(out=ss.t[:, 1:2], in0=ss.t[:, 0:1], scalar1=1.0 / 512, scalar2=EPS, op0=ALU.mult, op1=ALU.add), [ss.b], [ss.b])
                            tk.op("act", lambda e: e.sqrt(out=ss.t[:, 2:3], in_=ss.t[:, 1:2]), [ss.b], [ss.b])
                            tk.op("dve", lambda e: e.reciprocal(out=ss.t[:, 3:4], in_=ss.t[:, 2:3]), [ss.b], [ss.b])
                            tk.op("dve", lambda e: e.tensor_scalar_mul(out=ob.t[:], in0=yv.t[:, ys, :], scalar1=ss.t[:, 3:4]), [yv.bufs[ys], ss.b], [ob.b])
                            for k in range(4):
                                tk.op("pe", lambda e, k=k: e.transpose(out=pTo.t[:, k, :], in_=ob.t[:, k * 128:(k + 1) * 128], identity=idb.t[:]), [ob.b, idb.b], [pTo.b])
                            tk.op("act", lambda e: e.copy(out=obT.t[:, ys], in_=pTo.t[:, 0:4, :]), [pTo.b], [obT.bufs[ys]])
                            tk.dma(mixT_d[0:4, :, r0:r0 + 128].rearrange("c p n -> p c n"), obT.t[:, ys], reads=[obT.bufs[ys]], writes=[db("mixT")])

                        front(0)
                        for i in range(len(order)):
                            if i + 1 < len(order):
                                front(i + 1)
                            back(i)
                    tk.barrier()

        if ENABLE_SSD and not (debug and "skip_ssd" in debug):
            phase2a()
        phase2b()
        phase2c()
        mid.close()
        blocks3 = []
        for i in range(L // 256):
            blocks3.append((x1_d[CL + i * 256:CL + (i + 1) * 256, :], out_d[i * 256:(i + 1) * 256, :], 0, db("x1"), db("out")))
        if debug and "only_ctx" in debug:
            blocks3 = blocks3[:2]
        ffn_phase("f2", w_ffn2_in_d, w_ffn2_out_d, 2, 8, blocks3, True)

        tk.barrier(("sp",))
    return nc


_CONSTS = None


def _consts():
    global _CONSTS
    if _CONSTS is None:
        c = np.zeros((128, NCONST), np.float32)
        k = np.arange(128)[:, None]
        i = np.arange(128)[None, :]
        c[:, C_ID:C_ID + 128] = (k == i)
        c[:, C_LE:C_LE + 128] = (k <= i)
        c[:, C_GT:C_GT + 128] = (k > i)
        c[:, C_GE:C_GE + 128] = (k >= i)
        c[:, C_LT:C_LT + 128] = (k < i)
        c[:, C_ONE:C_ONE + 128] = 1.0
        c[0, C_SEL:C_SEL + 128] = 1.0
        c[1, C_SEL + 128:C_SEL + 256] = 1.0
        half = 16
        inv = 10000.0 ** (-np.arange(0, half, 2, dtype=np.float32) / half)
        pos = np.arange(L)
        row = (pos // 64).astype(np.float32)
        colp = (pos % 64).astype(np.float32)
        rope = np.zeros((2, 32, L), np.float32)
        for base, p in ((0, row), (16, colp)):
            ang = p[None, :] * inv[:, None]
            cs, sn = np.cos(ang), np.sin(ang)
            rope[0, base:base + 8] = cs
            rope[0, base + 8:base + 16] = cs
            rope[1, base:base + 8] = -sn
            rope[1, base + 8:base + 16] = sn
        _CONSTS = (c, rope)
    return _CONSTS


def make_in_maps(inputs, cores=range(8)):
    f = lambda a: np.ascontiguousarray(np.asarray(a, dtype=np.float32))
    c, rope = _consts()
    shared = {
        "w_ada": f(inputs["w_ada"][0]), "b_ada": f(inputs["b_ada"][0]).reshape(1, -1),
        "norm_ffn1": f(inputs["norm_ffn1"][0]).reshape(1, -1), "w_ffn1_in": f(inputs["w_ffn1_in"][0]),
        "w_ffn1_out": f(inputs["w_ffn1_out"][0]), "norm_mix": f(inputs["norm_mix"][0]).reshape(1, -1),
        "w_in": f(inputs["w_in"][0]), "conv_w": f(inputs["conv_w"][0]), "conv_b": f(inputs["conv_b"][0]).reshape(1, -1),
        "dt_bias": f(inputs["dt_bias"][0]).reshape(1, 16), "a_log": f(inputs["a_log"][0]).reshape(1, 16),
        "d_skip": f(inputs["d_skip"][0]).reshape(1, 8), "ssd_norm": f(inputs["ssd_norm"][0]).reshape(1, -1),
        "q_norm": f(inputs["q_norm"][0]).reshape(1, -1), "w_uq": f(inputs["w_uq"][0]),
        "kv_norm": f(inputs["kv_norm"][0]).reshape(1, -1), "w_ukv": f(inputs["w_ukv"][0]), "w_out": f(inputs["w_out"][0]),
        "norm_ffn2": f(inputs["norm_ffn2"][0]).reshape(1, -1), "w_ffn2_in": f(inputs["w_ffn2_in"][0]),
        "w_ffn2_out": f(inputs["w_ffn2_out"][0]), "final_norm": f(inputs["final_norm"]).reshape(1, -1),
        "consts": c, "rope": rope,
    }
    maps = []
    for b in cores:
        m = dict(shared)
        m["x"] = f(inputs["x"][b])
        m["ctx"] = f(inputs["ctx"][b])
        m["cc"] = np.stack([f(inputs["c"][b]), f(inputs["c_ctx"])], 0)
        maps.append(m)
    return maps


def kernel(**inputs):
    nc = build()
    maps = make_in_maps(inputs)
    res = run_bass_kernel_spmd(nc, maps, core_ids=list(range(8)))
    return np.stack([np.asarray(r["out"], dtype=np.float32) for r in res.results], 0)
```

```python
import contextlib
import numpy as np
import ml_dtypes
import concourse.bass as bass
import concourse.mybir as mybir
from concourse.bass_utils import run_bass_kernel_spmd

F32 = mybir.dt.float32
BF16 = mybir.dt.bfloat16
AF = mybir.ActivationFunctionType
ALU = mybir.AluOpType

D = 1024
L = 4096
CL = 256
T = L + CL
NT = T // 128
DFF = 2816
NKF = DFF // 128
EPS = 1e-6
IN_DIM = 2608
SCALE = 96 ** -0.5
ENABLE_SSD = True

C_ID = 0
C_LE = 128
C_GT = 256
C_GE = 384
C_LT = 512
C_ONE = 640
C_SEL = 768
NCONST = 1024


def _merge(d, s):
    for k, v in s.items():
        if d.get(k, 0) < v:
            d[k] = v


class Buf:
    __slots__ = ("name", "w", "r")

    def __init__(self, name=""):
        self.name = name
        self.w = {}
        self.r = {}


class TK:
    def __init__(self, nc, es, n_dma=14):
        self.nc = nc
        self.eng = {"pe": nc.tensor, "act": nc.scalar, "dve": nc.vector, "pool": nc.gpsimd, "sp": nc.sync}
        self.sem = {}
        self.cnt = {}
        for n in ("pe", "act", "dve", "pool"):
            self.sem[n] = es.enter_context(nc.semaphore("s_" + n))
            self.cnt[n] = 0
        self.nd = n_dma
        for i in range(n_dma):
            self.sem[f"d{i}"] = es.enter_context(nc.semaphore(f"s_d{i}"))
            self.cnt[f"d{i}"] = 0
        self.drr = 0
        self.seen = {n: {} for n in self.eng}

    def _wait(self, e, deps):
        eng = self.eng[e]
        seen = self.seen[e]
        for k, v in deps.items():
            if k == e:
                continue
            if seen.get(k, 0) < v:
                eng.wait_ge(self.sem[k], v)
                seen[k] = v

    def op(self, e, fn, reads=(), writes=()):
        deps = {}
        for b in reads:
            _merge(deps, b.w)
        own = deps.get(e, 0)
        if e != "pe" and own > 0 and self.cnt[e] - own < 2 and self.seen[e].get(e, 0) < own:
            self.eng[e].wait_ge(self.sem[e], own)
            self.seen[e][e] = own
        for b in writes:
            _merge(deps, b.r)
            _merge(deps, b.w)
        self._wait(e, deps)
        ins = fn(self.eng[e])
        self.cnt[e] += 1
        ins.then_inc(self.sem[e], 1)
        c = self.cnt[e]
        for b in reads:
            b.r[e] = c
        for b in writes:
            b.w[e] = c
        return ins

    def dma(self, out, in_, reads=(), writes=(), q="sp", **kw):
        k = self.drr
        self.drr = (self.drr + 1) % self.nd
        key = f"d{k}"
        deps = {}
        for b in reads:
            _merge(deps, b.w)
        for b in writes:
            _merge(deps, b.r)
            _merge(deps, b.w)
        if self.cnt[key] > 0:
            deps[key] = max(deps.get(key, 0), self.cnt[key])
        self._wait(q, deps)
        ins = self.eng[q].dma_start(out=out, in_=in_, **kw)
        self.cnt[key] += 16
        ins.then_inc(self.sem[key], 16)
        c = self.cnt[key]
        for b in reads:
            b.r[key] = c
        for b in writes:
            b.w[key] = c
        return ins

    def barrier(self, engines=("pe", "act", "dve", "pool", "sp")):
        for e in engines:
            deps = {k: v for k, v in self.cnt.items() if v > 0}
            self._wait(e, deps)


class SB:
    def __init__(self, t, nb=1):
        self.t = t
        self.bufs = [Buf() for _ in range(nb)]

    @property
    def b(self):
        return self.bufs[0]


def build(debug=None):
    nc = bass.Bass("TRN2", target_bir_lowering=False)

    def din(name, shape, dt=F32):
        return nc.dram_tensor(name, list(shape), dt, kind="ExternalInput").ap()

    x_d = din("x", [L, D])
    ctx_d = din("ctx", [CL, D])
    cc_d = din("cc", [2, D])
    w_ada_d = din("w_ada", [D, 9 * D])
    b_ada_d = din("b_ada", [1, 9 * D])
    norm_ffn1_d = din("norm_ffn1", [1, D])
    w_ffn1_in_d = din("w_ffn1_in", [D, 2 * DFF])
    w_ffn1_out_d = din("w_ffn1_out", [DFF, D])
    norm_mix_d = din("norm_mix", [1, D])
    w_in_d = din("w_in", [D, IN_DIM])
    conv_w_d = din("conv_w", [5, D])
    conv_b_d = din("conv_b", [1, D])
    dt_bias_d = din("dt_bias", [1, 16])
    a_log_d = din("a_log", [1, 16])
    d_skip_d = din("d_skip", [1, 8])
    ssd_norm_d = din("ssd_norm", [1, 512])
    q_norm_d = din("q_norm", [1, 768])
    w_uq_d = din("w_uq", [768, 768])
    kv_norm_d = din("kv_norm", [1, 256])
    w_ukv_d = din("w_ukv", [256, 1024])
    w_out_d = din("w_out", [D, D])
    norm_ffn2_d = din("norm_ffn2", [1, D])
    w_ffn2_in_d = din("w_ffn2_in", [D, 2 * DFF])
    w_ffn2_out_d = din("w_ffn2_out", [DFF, D])
    final_norm_d = din("final_norm", [1, D])
    consts_d = din("consts", [128, NCONST])
    rope_d = din("rope", [2, 32, L])

    out_d = nc.dram_tensor("out", [L, D], F32, kind="ExternalOutput").ap()

    def dscr(name, shape, dt):
        kind = "ExternalOutput" if (debug and name in debug) else "Internal"
        return nc.dram_tensor(name, list(shape), dt, kind=kind).ap()

    mrow_d = dscr("mrow", [2, 9 * D], F32)
    x1_d = dscr("x1", [T, D], F32)

    def scope(name):
        return nc.named_scope(name) if (debug and "scopes" in debug) else contextlib.nullcontext()

    with contextlib.ExitStack() as es:
        tk = TK(nc, es)

        def sb(st, name, shape, dt=F32, nb=1):
            return SB(st.enter_context(nc.sbuf_tensor(name, list(shape), dt)), nb)

        def ps(st, name, shape, dt=F32, nb=1):
            return SB(st.enter_context(nc.psum_tensor(name, list(shape), dt)), nb)

        dram_bufs = {}

        def db(name):
            if name not in dram_bufs:
                dram_bufs[name] = Buf(name)
            return dram_bufs[name]

        cst = sb(es, "cst", [128, NCONST])
        idb = sb(es, "idb", [128, 128], BF16)
        cols = sb(es, "cols", [128, 8, 32])
        modp = sb(es, "modp", [128, 3, 2, 2, 8])
        tk.dma(cst.t[:], consts_d, writes=[cst.b])
        tk.op("dve", lambda e: e.tensor_copy(out=idb.t[:], in_=cst.t[:, C_ID:C_ID + 128]), [cst.b], [idb.b])

        with contextlib.ExitStack() as p0, scope("p0"):
            stage = sb(p0, "stage", [32, D])
            wa = sb(p0, "wa", [128, 4, 8, 512], F32, nb=4)
            mrow = sb(p0, "mrow_sb", [2, 9 * D])
            bada = sb(p0, "bada", [1, 9 * D])
            pm = ps(p0, "pm", [2, 2, 512], F32, nb=2)
            pt = ps(p0, "pt", [128, 8, 32], F32)
            pc = ps(p0, "pc", [128, 72, 2], F32)
            mcol = sb(p0, "mcol", [128, 72, 2])

            tk.op("pool", lambda e: e.memset(stage.t[:], 0.0), [], [stage.b])
            tk.dma(stage.t[0:2, :], cc_d, writes=[stage.b])
            rows = [(2, norm_ffn1_d, D), (3, norm_mix_d, D), (4, norm_ffn2_d, D), (10, conv_b_d, D),
                    (11, q_norm_d, 768), (12, kv_norm_d, 256), (13, ssd_norm_d, 512)]
            for r, src, n in rows:
                tk.dma(stage.t[r:r + 1, 0:n], src, writes=[stage.b])
            tk.dma(stage.t[5:10, :], conv_w_d, writes=[stage.b])
            tk.dma(bada.t[:], b_ada_d, writes=[bada.b])
            tk.op("act", lambda e: e.activation(out=stage.t[0:2, :], in_=stage.t[0:2, :], func=AF.Silu),
                  [stage.b], [stage.b])
            for kc in range(8):
                tk.op("pe", lambda e, kc=kc: e.transpose(out=pt.t[:, kc, :], in_=stage.t[0:32, kc * 128:(kc + 1) * 128],
                                                         identity=cst.t[0:32, C_ID:C_ID + 32]),
                      [stage.b, cst.b], [pt.b])
            tk.op("dve", lambda e: e.tensor_copy(out=cols.t[:], in_=pt.t[:]), [pt.b], [cols.b])

            war = w_ada_d.rearrange("(kc k) n -> k kc n", k=128)
            for nb in range(18):
                s = nb % 4
                tk.dma(wa.t[:, s], war[:, :, nb * 512:(nb + 1) * 512], writes=[wa.bufs[s]])
                pb = nb % 2
                for kc in range(8):
                    tk.op("pe", lambda e, kc=kc, s=s, pb=pb: e.matmul(pm.t[:, pb, :], lhsT=cols.t[:, kc, 0:2],
                                                                     rhs=wa.t[:, s, kc, :], start=(kc == 0), stop=False),
                          [cols.b, wa.bufs[s]], [pm.bufs[pb]])
                tk.op("pe", lambda e, pb=pb, nb=nb: e.matmul(pm.t[:, pb, :], lhsT=cst.t[0:1, C_ONE:C_ONE + 2],
                                                            rhs=bada.t[0:1, nb * 512:(nb + 1) * 512], start=False, stop=True),
                      [cst.b, bada.b], [pm.bufs[pb]])
                tk.op("dve", lambda e, pb=pb, nb=nb: e.tensor_copy(out=mrow.t[:, nb * 512:(nb + 1) * 512], in_=pm.t[:, pb, :]),
                      [pm.bufs[pb]], [mrow.b])
            tk.dma(mrow_d, mrow.t[:], reads=[mrow.b], writes=[db("mrow")])
            for j in range(72):
                tk.op("pe", lambda e, j=j: e.transpose(out=pc.t[:, j, :], in_=mrow.t[0:2, j * 128:(j + 1) * 128],
                                                       identity=cst.t[0:2, C_ID:C_ID + 2]),
                      [mrow.b, cst.b], [pc.b])
            tk.op("dve", lambda e: e.tensor_copy(out=mcol.t[:], in_=pc.t[:]), [pc.b], [mcol.b])
            for s_ in range(3):
                for b_ in range(2):
                    tk.op("dve", lambda e, s_=s_, b_=b_: e.scalar_tensor_tensor(
                        out=modp.t[:, s_, b_, 0, :], in0=mcol.t[:, (3 * s_ + 1) * 8:(3 * s_ + 2) * 8, b_], scalar=1.0,
                        in1=cols.t[:, :, 2 + s_], op0=ALU.add, op1=ALU.mult), [mcol.b, cols.b], [modp.b])
                    tk.op("dve", lambda e, s_=s_, b_=b_: e.tensor_copy(
                        out=modp.t[:, s_, b_, 1, :], in_=mcol.t[:, (3 * s_) * 8:(3 * s_ + 1) * 8, b_]), [mcol.b], [modp.b])
            tk.barrier()

        def ffn_phase(tag, w_in_dram, w_out_dram, mset, gate_idx, blocks, final):
            with contextlib.ExitStack() as ph, scope(tag):
                wi = sb(ph, tag + "wi", [128, 8, 2 * DFF], BF16)
                wo = sb(ph, tag + "wo", [128, NKF, D], BF16)
                gbs = sb(ph, tag + "gb", [128, 2, D], F32, nb=2)
                fnb = sb(ph, tag + "fnb", [128, D], F32) if final else None
                bs = set(b_ for (_, _, b_, _, _) in blocks)
                for b_ in sorted(bs):
                    src = mrow_d[b_:b_ + 1, gate_idx * D:(gate_idx + 1) * D]
                    tk.dma(gbs.t[:, b_, :], bass.AP(src.tensor, src.offset, [[0, 128], [1, D]]),
                           reads=[db("mrow")], writes=[gbs.bufs[b_]])
                    tk.op("pool", lambda e, b_=b_: e.tensor_scalar_mul(out=gbs.t[:, b_, :], in0=gbs.t[:, b_, :], scalar1=0.5),
                          [gbs.bufs[b_]], [gbs.bufs[b_]])
                if final:
                    tk.dma(fnb.t[:], bass.AP(final_norm_d.tensor, 0, [[0, 128], [1, D]]), writes=[fnb.b])
                with contextlib.ExitStack() as st:
                    stg = sb(st, tag + "stg", [128, 3, 2048], F32, nb=3)
                    pieces = []
                    wir = w_in_dram.rearrange("(kc k) n -> k kc n", k=128)
                    for kc in range(8):
                        for q4 in range(4):
                            pieces.append((wir[:, kc, q4 * 1408:(q4 + 1) * 1408], wi.t[:, kc, q4 * 1408:(q4 + 1) * 1408], 1408, wi.b))
                    wor = w_out_dram.rearrange("(kc k) n -> k kc n", k=128)
                    for kc in range(NKF):
                        pieces.append((wor[:, kc, :], wo.t[:, kc, :], 1024, wo.b))
                    cast_eng = ["dve", "pool", "act"]
                    for i, (src, dst, n, wb) in enumerate(pieces):
                        s = i % 3
                        tk.dma(stg.t[:, s, 0:n], src, writes=[stg.bufs[s]])
                        ce = cast_eng[i % 3]
                        if ce == "act":
                            tk.op("act", lambda e, s=s, n=n, dst=dst: e.copy(out=dst, in_=stg.t[:, s, 0:n]), [stg.bufs[s]], [wb])
                        else:
                            tk.op(ce, lambda e, s=s, n=n, dst=dst: e.tensor_copy(out=dst, in_=stg.t[:, s, 0:n]), [stg.bufs[s]], [wb])
                    tk.barrier()
                xb = sb(ph, tag + "xb", [128, 2, 2, D], F32, nb=2)
                xn = sb(ph, tag + "xn", [128, 2, 2, D], BF16, nb=2)
                hT = sb(ph, tag + "hT", [128, 2, 8, 256], BF16, nb=2)
                tmpm = sb(ph, tag + "tmpm", [128, 8, 128], F32)
                gsb = sb(ph, tag + "g", [128, NKF, 256], BF16)
                sg = sb(ph, tag + "sg", [128, 2, 512], F32, nb=2)
                tt = sb(ph, tag + "tt", [128, 2, 512], F32, nb=2)
                junk = sb(ph, tag + "junk", [128, D], BF16)
                ss = sb(ph, tag + "ss", [128, 2, 8], F32, nb=2)
                pG = ps(ph, tag + "pG", [128, 2, 512], F32, nb=2)
                pU = ps(ph, tag + "pU", [128, 2, 512], F32, nb=2)
                pY = ps(ph, tag + "pY", [128, 2, 512], F32, nb=2)
                pT = ps(ph, tag + "pT", [128, 2, 8, 128], BF16, nb=2)
                nblk = len(blocks)

                def load(i):
                    src, _, _, sbuf, _ = blocks[i]
                    s = i % 2
                    tk.dma(xb.t[:, s], src.rearrange("(t p) d -> p t d", p=128), reads=[sbuf], writes=[xb.bufs[s]])

                def norm(i):
                    s = i % 2
                    sbf = ss.bufs[s]
                    for t in range(2):
                        tk.op("act", lambda e, t=t: e.activation(out=junk.t[:], in_=xb.t[:, s, t, :], func=AF.Square,
                                                                  accum_out=ss.t[:, s, t:t + 1]), [xb.bufs[s]], [junk.b, sbf])
                    tk.op("dve", lambda e: e.tensor_scalar(out=ss.t[:, s, 2:4], in0=ss.t[:, s, 0:2], scalar1=1.0 / D, scalar2=EPS,
                                                           op0=ALU.mult, op1=ALU.add), [sbf], [sbf])
                    tk.op("act", lambda e: e.sqrt(out=ss.t[:, s, 4:6], in_=ss.t[:, s, 2:4]), [sbf], [sbf])
                    tk.op("dve", lambda e: e.reciprocal(out=ss.t[:, s, 6:8], in_=ss.t[:, s, 4:6]), [sbf], [sbf])
                    for t in range(2):
                        tk.op("dve", lambda e, t=t: e.tensor_scalar_mul(out=xn.t[:, s, t, :], in0=xb.t[:, s, t, :],
                                                                        scalar1=ss.t[:, s, 6 + t:7 + t]),
                              [xb.bufs[s], sbf], [xn.bufs[s]])

                def transp(i):
                    s = i % 2
                    b_ = blocks[i][2]
                    for t in range(2):
                        for kc in range(8):
                            tk.op("pe", lambda e, t=t, kc=kc: e.transpose(out=pT.t[:, t, kc, :], in_=xn.t[:, s, t, kc * 128:(kc + 1) * 128],
                                                                            identity=idb.t[:]), [xn.bufs[s], idb.b], [pT.bufs[t]])
                        G = modp.t[:, mset, b_, 0, :].unsqueeze(2).to_broadcast([128, 8, 128])
                        S = modp.t[:, mset, b_, 1, :].unsqueeze(2).to_broadcast([128, 8, 128])
                        tk.op("dve", lambda e, t=t, G=G: e.tensor_tensor(out=tmpm.t[:], in0=pT.t[:, t], in1=G, op=ALU.mult),
                              [pT.bufs[t], modp.b], [tmpm.b])
                        tk.op("pool", lambda e, t=t, S=S: e.tensor_tensor(out=hT.t[:, s, :, t * 128:(t + 1) * 128], in0=tmpm.t[:], in1=S, op=ALU.add),
                              [tmpm.b, modp.b], [hT.bufs[s]])

                def ffn_in(i):
                    s = i % 2
                    for fp in range(NKF // 2):
                        pb = fp % 2
                        for j in range(2):
                            fc = fp * 2 + j
                            for kc in range(8):
                                tk.op("pe", lambda e, j=j, fc=fc, kc=kc: e.matmul(pG.t[:, pb, j * 256:(j + 1) * 256], lhsT=wi.t[:, kc, fc * 128:(fc + 1) * 128],
                                                                                   rhs=hT.t[:, s, kc, :], start=(kc == 0), stop=(kc == 7)),
                                      [wi.b, hT.bufs[s]], [pG.bufs[pb]])
                            for kc in range(8):
                                tk.op("pe", lambda e, j=j, fc=fc, kc=kc: e.matmul(pU.t[:, pb, j * 256:(j + 1) * 256], lhsT=wi.t[:, kc, DFF + fc * 128:DFF + (fc + 1) * 128],
                                                                                   rhs=hT.t[:, s, kc, :], start=(kc == 0), stop=(kc == 7)),
                                      [wi.b, hT.bufs[s]], [pU.bufs[pb]])
                        tk.op("act", lambda e: e.activation(out=sg.t[:, pb, :], in_=pG.t[:, pb, :], func=AF.Silu), [pG.bufs[pb]], [sg.bufs[pb]])
                        tk.op("dve", lambda e, fp=fp: e.tensor_tensor(out=gsb.t[:, fp * 2:fp * 2 + 2, :], in0=sg.t[:, pb, :].rearrange("p (a b) -> p a b", a=2),
                                                                      in1=pU.t[:, pb, :].rearrange("p (a b) -> p a b", a=2), op=ALU.mult),
                              [sg.bufs[pb], pU.bufs[pb]], [gsb.b])

                def ffn_out(i):
                    s = i % 2
                    _, dst, b_, _, dbuf = blocks[i]
                    for t in range(2):
                        for h in range(2):
                            for kc in range(NKF):
                                tk.op("pe", lambda e, t=t, h=h, kc=kc: e.matmul(pY.t[:, h, :], lhsT=gsb.t[:, kc, t * 128:(t + 1) * 128],
                                                                                 rhs=wo.t[:, kc, h * 512:(h + 1) * 512], start=(kc == 0), stop=(kc == NKF - 1)),
                                      [gsb.b, wo.b], [pY.bufs[h]])
                            tk.op("dve", lambda e, h=h: e.tensor_tensor(out=tt.t[:, h, :], in0=pY.t[:, h, :], in1=gbs.t[:, b_, h * 512:(h + 1) * 512], op=ALU.mult),
                                  [pY.bufs[h], gbs.bufs[b_]], [tt.bufs[h]])
                            tk.op("pool", lambda e, t=t, h=h: e.tensor_tensor(out=xb.t[:, s, t, h * 512:(h + 1) * 512], in0=xb.t[:, s, t, h * 512:(h + 1) * 512],
                                                                               in1=tt.t[:, h, :], op=ALU.add), [tt.bufs[h], xb.bufs[s]], [xb.bufs[s]])
                    if final:
                        sbf = ss.bufs[s]
                        for t in range(2):
                            tk.op("act", lambda e, t=t: e.activation(out=junk.t[:], in_=xb.t[:, s, t, :], func=AF.Square,
                                                                      accum_out=ss.t[:, s, t:t + 1]), [xb.bufs[s]], [junk.b, sbf])
                        tk.op("dve", lambda e: e.tensor_scalar(out=ss.t[:, s, 2:4], in0=ss.t[:, s, 0:2], scalar1=1.0 / D, scalar2=EPS,
                                                               op0=ALU.mult, op1=ALU.add), [sbf], [sbf])
                        tk.op("act", lambda e: e.sqrt(out=ss.t[:, s, 4:6], in_=ss.t[:, s, 2:4]), [sbf], [sbf])
                        tk.op("dve", lambda e: e.reciprocal(out=ss.t[:, s, 6:8], in_=ss.t[:, s, 4:6]), [sbf], [sbf])
                        for t in range(2):
                            tk.op("dve", lambda e, t=t: e.scalar_tensor_tensor(out=xb.t[:, s, t, :], in0=xb.t[:, s, t, :], scalar=ss.t[:, s, 6 + t:7 + t],
                                                                               in1=fnb.t[:], op0=ALU.mult, op1=ALU.mult),
                                  [xb.bufs[s], sbf, fnb.b], [xb.bufs[s]])
                    tk.dma(dst.rearrange("(t p) d -> p t d", p=128), xb.t[:, s], reads=[xb.bufs[s]], writes=[dbuf])

                load(0)
                if nblk > 1:
                    load(1)
                norm(0)
                transp(0)
                for i in range(nblk):
                    ffn_in(i)
                    if i + 1 < nblk:
                        norm(i + 1)
                        transp(i + 1)
                    ffn_out(i)
                    if i + 2 < nblk:
                        load(i + 2)
                tk.barrier()

        blocks1 = [(ctx_d, x1_d[0:256, :], 1, Buf(), db("x1"))]
        for i in range(L // 256):
            blocks1.append((x_d[i * 256:(i + 1) * 256, :], x1_d[CL + i * 256:CL + (i + 1) * 256, :], 0, Buf(), db("x1")))
        if debug and "only_ctx" in debug:
            blocks1 = blocks1[:3]
        ffn_phase("f1", w_ffn1_in_d, w_ffn1_out_d, 0, 2, blocks1, False)


        def phase1b():
            with contextlib.ExitStack() as ph, scope("p1b"):
                wp = sb(ph, "wp", [128, 8, 2624], BF16)
                wq = sb(ph, "wq", [128, 6, 8, 96], BF16)
                wqs = sb(ph, "wqs", [128, 6, 8, 96], BF16)
                wkv = sb(ph, "wkv", [128, 2, 1024], BF16)
                wkp = sb(ph, "wkp", [128, 8, 2, 96], BF16)
                dtb = sb(ph, "dtb", [128, 16], F32)
                with contextlib.ExitStack() as st:
                    stg = sb(st, "stg1", [128, 3, IN_DIM], F32, nb=3)
                    kst = sb(st, "kst", [128, 8, 2, 32], F32)
                    wir = w_in_d.rearrange("(kc k) n -> k kc n", k=128)
                    ce = ["dve", "pool"]
                    n = 0
                    for kc in range(8):
                        s = n % 3
                        tk.dma(stg.t[:, s, :], wir[:, kc, :], writes=[stg.bufs[s]])
                        tk.op(ce[n % 2], lambda e, s=s, kc=kc: e.tensor_copy(out=wp.t[:, kc, 0:1552], in_=stg.t[:, s, 0:1552]), [stg.bufs[s]], [wp.b])
                        tk.op(ce[(n + 1) % 2], lambda e, s=s, kc=kc: e.tensor_copy(out=wp.t[:, kc, 1600:2624], in_=stg.t[:, s, 1552:2576]), [stg.bufs[s]], [wp.b])
                        n += 1
                    tk.op("pool", lambda e: e.memset(wkp.t[:], 0.0), [], [wkp.b])
                    tk.dma(kst.t[:, :, 0, :], wir[:, :, 2576:2608], writes=[kst.b])
                    for half in range(2):
                        o = 2576 + 16 * half
                        tk.dma(kst.t[:, :, 1, 16 * half:16 * half + 8], wir[:, :, o + 8:o + 16], writes=[kst.b])
                        tk.dma(kst.t[:, :, 1, 16 * half + 8:16 * half + 16], wir[:, :, o:o + 8], writes=[kst.b])
                    tk.op("dve", lambda e: e.tensor_copy(out=wkp.t[:, :, :, 64:96], in_=kst.t[:]), [kst.b], [wkp.b])
                    wur = w_uq_d.rearrange("(kc k) n -> k kc n", k=128)
                    for kc in range(6):
                        s = n % 3
                        tk.dma(stg.t[:, s, 0:768], wur[:, kc, :], writes=[stg.bufs[s]])
                        tk.op("dve", lambda e, s=s, kc=kc: e.tensor_scalar_mul(out=wq.t[:, kc].rearrange("p h d -> p (h d)"), in0=stg.t[:, s, 0:768],
                                                                               scalar1=cols.t[:, kc, 11:12]), [stg.bufs[s], cols.b], [wq.b])
                        n += 1
                    tk.op("pool", lambda e: e.tensor_copy(out=wqs.t[:], in_=wq.t[:]), [wq.b], [wqs.b])
                    for half in range(2):
                        o = 64 + 16 * half
                        tk.op("dve", lambda e, o=o: e.tensor_copy(out=wqs.t[:, :, :, o:o + 8], in_=wq.t[:, :, :, o + 8:o + 16]), [wq.b], [wqs.b])
                        tk.op("dve", lambda e, o=o: e.tensor_copy(out=wqs.t[:, :, :, o + 8:o + 16], in_=wq.t[:, :, :, o:o + 8]), [wq.b], [wqs.b])
                    wkr = w_ukv_d.rearrange("(kc k) n -> k kc n", k=128)
                    for kc in range(2):
                        s = n % 3
                        tk.dma(stg.t[:, s, 0:1024], wkr[:, kc, :], writes=[stg.bufs[s]])
                        src = stg.t[:, s, 0:1024].rearrange("p (h two d) -> p two h d", h=8, two=2)
                        tk.op("dve", lambda e, src=src, kc=kc: e.tensor_scalar_mul(out=wkv.t[:, kc, :].rearrange("p (two h d) -> p two h d", two=2, h=8), in0=src,
                                                                                  scalar1=cols.t[:, kc, 12:13]), [stg.bufs[s], cols.b], [wkv.b])
                        n += 1
                    tk.dma(dtb.t[:], bass.AP(dt_bias_d.tensor, 0, [[0, 128], [1, 16]]), writes=[dtb.b])
                    tk.barrier()
                if debug and 'stop0' in debug:
                    return
                NB = 512
                xb = sb(ph, "bxb", [128, 2, 4, D], F32, nb=2)
                xn = sb(ph, "bxn", [128, 4, D], BF16)
                hT = sb(ph, "bhT", [128, 2, 8, NB], BF16, nb=2)
                tmpm = sb(ph, "btmpm", [128, 8, 128], F32)
                ss = sb(ph, "bss", [128, 16], F32)
                zt = sb(ph, "bzt", [128, 4, 512], BF16)
                xo = sb(ph, "bxo", [128, 8, NB], BF16)
                cq = sb(ph, "bcq", [128, 6, NB], BF16)
                cqn = sb(ph, "bcqn", [128, 6, NB], BF16)
                ckv = sb(ph, "bckv", [128, 2, NB], BF16)
                ckvn = sb(ph, "bckvn", [128, 2, NB], BF16)
                sq = sb(ph, "bsq", [128, 8, NB], BF16, nb=2)
                rb = sb(ph, "brb", [128, 2, 2, NB], F32, nb=2)
                qo = sb(ph, "bqo", [96, 8, NB], BF16)
                ko = sb(ph, "bko", [128, 4, NB], BF16)
                kpo = sb(ph, "bkpo", [96, NB], BF16)
                vo = sb(ph, "bvo", [128, 4, 512], BF16)
                rp = sb(ph, "brp", [96, 2, NB], F32)
                rt = sb(ph, "brt", [96, 2, 2, NB], F32, nb=2)
                dtt = sb(ph, "bdtt", [128, 4, 16], F32)
                pA = ps(ph, "pA", [128, 4, 512], F32, nb=4)
                pT = ps(ph, "pT1", [128, 8, 128], BF16)
                pS = ps(ph, "pS", [128, 512], F32)
                pD = ps(ph, "pD", [128, 4, 16], F32)
                pai = [0]

                def nxt():
                    pai[0] = (pai[0] + 1) % 4
                    return pai[0]

                blocks = [(x1_d[0:CL, :], CL, 1, 0)] + [(x1_d[CL + i * NB:CL + (i + 1) * NB, :], NB, 0, CL + i * NB) for i in range(L // NB)]
                if debug and "only_ctx" in debug:
                    blocks = blocks[:2]
                def load_b(bi):
                    src_, N_, _, _ = blocks[bi]
                    tk.dma(xb.t[:, bi % 2, 0:N_ // 128], src_.rearrange("(t p) d -> p t d", p=128), reads=[db("x1")], writes=[xb.bufs[bi % 2]])

                def front(bi):
                    src, N, b_, tok0 = blocks[bi]
                    nt = N // 128
                    s = bi % 2
                    for t in range(nt):
                        tk.op("act", lambda e, t=t: e.activation(out=xn.t[:, 3, :], in_=xb.t[:, s, t, :], func=AF.Square, accum_out=ss.t[:, t:t + 1]),
                              [xb.bufs[s]], [xn.b, ss.b])
                    tk.op("dve", lambda e: e.tensor_scalar(out=ss.t[:, 4:8], in0=ss.t[:, 0:4], scalar1=1.0 / D, scalar2=EPS, op0=ALU.mult, op1=ALU.add), [ss.b], [ss.b])
                    tk.op("act", lambda e: e.sqrt(out=ss.t[:, 8:12], in_=ss.t[:, 4:8]), [ss.b], [ss.b])
                    tk.op("dve", lambda e: e.reciprocal(out=ss.t[:, 12:16], in_=ss.t[:, 8:12]), [ss.b], [ss.b])
                    G = modp.t[:, 1, b_, 0, :].unsqueeze(2).to_broadcast([128, 8, 128])
                    S = modp.t[:, 1, b_, 1, :].unsqueeze(2).to_broadcast([128, 8, 128])
                    for t in range(nt):
                        tk.op("dve", lambda e, t=t: e.tensor_scalar_mul(out=xn.t[:, t, :], in0=xb.t[:, s, t, :], scalar1=ss.t[:, 12 + t:13 + t]),
                              [xb.bufs[s], ss.b], [xn.b])
                        for kc in range(8):
                            tk.op("pe", lambda e, t=t, kc=kc: e.transpose(out=pT.t[:, kc, :], in_=xn.t[:, t, kc * 128:(kc + 1) * 128], identity=idb.t[:]),
                                  [xn.b, idb.b], [pT.b])
                        tk.op("dve", lambda e: e.tensor_tensor(out=tmpm.t[:], in0=pT.t[:], in1=G, op=ALU.mult), [pT.b, modp.b], [tmpm.b])
                        tk.op("pool", lambda e, t=t: e.tensor_tensor(out=hT.t[:, s, :, t * 128:(t + 1) * 128], in0=tmpm.t[:], in1=S, op=ALU.add),
                              [tmpm.b, modp.b], [hT.bufs[s]])

                def fm(N, lhs_fn, nk, rhs_fn, M, rd):
                    a = nxt()
                    for kc in range(nk):
                        tk.op("pe", lambda e, kc=kc: e.matmul(pA.t[0:M, a, 0:N], lhsT=lhs_fn(kc), rhs=rhs_fn(kc), start=(kc == 0), stop=(kc == nk - 1)),
                              rd, [pA.bufs[a]])
                    return a

                CQ = ((cq, cqn, 6, 1600, 768.0, 0), (ckv, ckvn, 2, 2368, 256.0, 1))

                def indep(bi):
                    src, N, b_, tok0 = blocks[bi]
                    nt = N // 128
                    s = bi % 2
                    hb_ = hT.bufs[s]
                    if b_ == 0:
                        tk.dma(rp.t[64:96, :, 0:N], rope_d[:, :, tok0 - CL:tok0 - CL + N].rearrange("c d n -> d c n"), writes=[rp.b])
                        for t in range(nt):
                            a = fm(N, lambda kc, t=t: hT.t[:, s, kc, t * 128:(t + 1) * 128], 8, lambda kc: wp.t[:, kc, 0:512], 128, [hb_, wp.b])
                            tk.op("act", lambda e, t=t, a=a: e.copy(out=zt.t[:, t, :], in_=pA.t[:, a, :]), [pA.bufs[a]], [zt.b])
                        r0 = tok0 - CL
                        tk.dma(z_d[r0:r0 + N, :].rearrange("(t p) d -> p t d", p=128), zt.t[:, 0:nt], reads=[zt.b], writes=[db("z")])
                    for cc in range(8):
                        a = fm(N, lambda kc, cc=cc: wp.t[:, kc, 512 + cc * 128:512 + (cc + 1) * 128], 8, lambda kc: hT.t[:, s, kc, 0:N], 128, [hb_, wp.b])
                        tk.op("act" if cc % 2 else "dve",
                              (lambda e, cc=cc, a=a: e.copy(out=xo.t[:, cc, 0:N], in_=pA.t[:, a, 0:N])) if cc % 2 else
                              (lambda e, cc=cc, a=a: e.tensor_copy(out=xo.t[:, cc, 0:N], in_=pA.t[:, a, 0:N])), [pA.bufs[a]], [xo.b])
                    tk.dma(xbcT_d[:, :, tok0:tok0 + N].rearrange("c p n -> p c n"), xo.t[:, :, 0:N], reads=[xo.b], writes=[db("xbcT")])
                    for t in range(nt):
                        for kc in range(8):
                            tk.op("pe", lambda e, t=t, kc=kc: e.matmul(pD.t[:, t, :], lhsT=hT.t[:, s, kc, t * 128:(t + 1) * 128], rhs=wp.t[:, kc, 1536:1552],
                                                                       start=(kc == 0), stop=(kc == 7)), [hb_, wp.b], [pD.b])
                    tg = tok0 // 128
                    tk.op("dve", lambda e: e.tensor_tensor(out=dtt.t[:, 0:nt, :], in0=pD.t[:, 0:nt, :], in1=dtb.t[:].unsqueeze(1).to_broadcast([128, nt, 16]), op=ALU.add),
                          [pD.b, dtb.b], [dtt.b])
                    tk.op("act", lambda e: e.activation(out=dtt.t[:, 0:nt, :], in_=dtt.t[:, 0:nt, :], func=AF.Exp), [dtt.b], [dtt.b])
                    tk.op("act", lambda e: e.activation(out=dtv.t[:, tg:tg + nt, :], in_=dtt.t[:, 0:nt, :], func=AF.Ln, bias=1.0), [dtt.b], [dtv.b])
                    for (dst, dstn, nk, c0, dim, ri) in CQ:
                        for c_ in range(nk):
                            a = fm(N, lambda kc, c_=c_: wp.t[:, kc, c0 + c_ * 128:c0 + (c_ + 1) * 128], 8, lambda kc: hT.t[:, s, kc, 0:N], 128, [hb_, wp.b])
                            tk.op("dve" if c_ % 2 else "act",
                                  (lambda e, c_=c_, a=a: e.tensor_copy(out=dst.t[:, c_, 0:N], in_=pA.t[:, a, 0:N])) if c_ % 2 else
                                  (lambda e, c_=c_, a=a: e.copy(out=dst.t[:, c_, 0:N], in_=pA.t[:, a, 0:N])), [pA.bufs[a]], [dst.b])
                            tk.op("pool", lambda e, c_=c_: e.tensor_tensor(out=sq.t[:, 6 * ri + c_, 0:N], in0=dst.t[:, c_, 0:N], in1=dst.t[:, c_, 0:N], op=ALU.mult), [dst.b], [sq.bufs[ri]])
                    a1 = fm(N, lambda kc: wkp.t[:, kc, 0, :], 8, lambda kc: hT.t[:, s, kc, 0:N], 96, [hb_, wkp.b])
                    if b_ == 0:
                        a2 = fm(N, lambda kc: wkp.t[:, kc, 1, :], 8, lambda kc: hT.t[:, s, kc, 0:N], 96, [hb_, wkp.b])
                        tk.op("dve", lambda e: e.tensor_tensor(out=rt.t[64:96, 0, 0, 0:N], in0=pA.t[64:96, a1, 0:N], in1=rp.t[64:96, 0, 0:N], op=ALU.mult),
                              [pA.bufs[a1], rp.b], [rt.bufs[0]])
                        tk.op("dve", lambda e: e.tensor_tensor(out=rt.t[64:96, 0, 1, 0:N], in0=pA.t[64:96, a2, 0:N], in1=rp.t[64:96, 1, 0:N], op=ALU.mult),
                              [pA.bufs[a2], rp.b], [rt.bufs[0]])
                        tk.op("pool", lambda e: e.tensor_tensor(out=kpo.t[64:96, 0:N], in0=rt.t[64:96, 0, 0, 0:N], in1=rt.t[64:96, 0, 1, 0:N], op=ALU.add),
                              [rt.bufs[0]], [kpo.b])
                    else:
                        tk.op("dve", lambda e: e.tensor_copy(out=kpo.t[64:96, 0:N], in_=pA.t[64:96, a1, 0:N]), [pA.bufs[a1]], [kpo.b])
                    for h in range(8):
                        tk.dma(kT_d[h, 64:96, tok0:tok0 + N], kpo.t[64:96, 0:N], reads=[kpo.b], writes=[db("kT")])

                def dep(bi):
                    src, N, b_, tok0 = blocks[bi]
                    nt = N // 128
                    for (dst, dstn, nk, c0, dim, ri) in CQ:
                        for c_ in range(nk):
                            tk.op("pe", lambda e, c_=c_: e.matmul(pS.t[:, 0:N], lhsT=onb.t[:], rhs=sq.t[:, 6 * ri + c_, 0:N], start=(c_ == 0), stop=(c_ == nk - 1)),
                                  [onb.b, sq.bufs[ri]], [pS.b])
                        tk.op("dve", lambda e: e.tensor_scalar(out=rb.t[:, ri, 0, 0:N], in0=pS.t[:, 0:N], scalar1=1.0 / dim, scalar2=EPS, op0=ALU.mult, op1=ALU.add),
                              [pS.b], [rb.bufs[ri]])
                        tk.op("act", lambda e: e.sqrt(out=rb.t[:, ri, 0, 0:N], in_=rb.t[:, ri, 0, 0:N]), [rb.bufs[ri]], [rb.bufs[ri]])
                        tk.op("dve", lambda e: e.reciprocal(out=rb.t[:, ri, 1, 0:N], in_=rb.t[:, ri, 0, 0:N]), [rb.bufs[ri]], [rb.bufs[ri]])
                        tk.op("dve" if ri else "pool", lambda e: e.tensor_tensor(out=dstn.t[:, :, 0:N], in0=dst.t[:, :, 0:N],
                                                                                  in1=rb.t[:, ri, 1, 0:N].unsqueeze(1).to_broadcast([128, nk, N]), op=ALU.mult),
                              [dst.b, rb.bufs[ri]], [dstn.b])
                    for hp in range(4):
                        a = fm(N, lambda kc, hp=hp: wkv.t[:, kc, hp * 128:(hp + 1) * 128], 2, lambda kc: ckvn.t[:, kc, 0:N], 128, [ckvn.b, wkv.b])
                        tk.op("act", lambda e, hp=hp, a=a: e.copy(out=ko.t[:, hp, 0:N], in_=pA.t[:, a, 0:N]), [pA.bufs[a]], [ko.b])
                    for h in range(8):
                        tk.dma(kT_d[h, 0:64, tok0:tok0 + N], ko.t[(h % 2) * 64:(h % 2) * 64 + 64, h // 2, 0:N], reads=[ko.b], writes=[db("kT")])
                    for t in range(nt):
                        a = nxt()
                        for kc in range(2):
                            tk.op("pe", lambda e, t=t, kc=kc, a=a: e.matmul(pA.t[:, a, :], lhsT=ckvn.t[:, kc, t * 128:(t + 1) * 128], rhs=wkv.t[:, kc, 512:1024],
                                                                            start=(kc == 0), stop=(kc == 1)), [ckvn.b, wkv.b], [pA.bufs[a]])
                        tk.op("dve", lambda e, t=t, a=a: e.tensor_copy(out=vo.t[:, t, :], in_=pA.t[:, a, :]), [pA.bufs[a]], [vo.b])
                    tk.dma(V_d[tok0:tok0 + N, :].rearrange("(t p) d -> p t d", p=128), vo.t[:, 0:nt], reads=[vo.b], writes=[db("V")])
                    if b_ == 0:
                        for h in range(8):
                            a1 = fm(N, lambda kc, h=h: wq.t[:, kc, h, :], 6, lambda kc: cqn.t[:, kc, 0:N], 96, [cqn.b, wq.b])
                            a2 = fm(N, lambda kc, h=h: wqs.t[:, kc, h, :], 6, lambda kc: cqn.t[:, kc, 0:N], 96, [cqn.b, wqs.b])
                            tk.op("dve", lambda e, h=h, a1=a1: e.tensor_copy(out=qo.t[0:64, h, 0:N], in_=pA.t[0:64, a1, 0:N]), [pA.bufs[a1]], [qo.b])
                            r = h % 2
                            tk.op("dve", lambda e, a1=a1, r=r: e.tensor_tensor(out=rt.t[64:96, r, 0, 0:N], in0=pA.t[64:96, a1, 0:N], in1=rp.t[64:96, 0, 0:N], op=ALU.mult),
                                  [pA.bufs[a1], rp.b], [rt.bufs[r]])
                            tk.op("dve", lambda e, a2=a2, r=r: e.tensor_tensor(out=rt.t[64:96, r, 1, 0:N], in0=pA.t[64:96, a2, 0:N], in1=rp.t[64:96, 1, 0:N], op=ALU.mult),
                                  [pA.bufs[a2], rp.b], [rt.bufs[r]])
                            tk.op("pool", lambda e, h=h, r=r: e.tensor_tensor(out=qo.t[64:96, h, 0:N], in0=rt.t[64:96, r, 0, 0:N], in1=rt.t[64:96, r, 1, 0:N], op=ALU.add),
                                  [rt.bufs[r]], [qo.b])
                        r0 = tok0 - CL
                        tk.dma(qT_d[:, :, r0:r0 + N].rearrange("h d n -> d h n"), qo.t[:, :, 0:N], reads=[qo.b], writes=[db("qT")])

                load_b(0)
                front(0)
                for bi in range(len(blocks)):
                    if bi + 1 < len(blocks):
                        load_b(bi + 1)
                    indep(bi)
                    if bi + 1 < len(blocks):
                        front(bi + 1)
                    dep(bi)
                tk.barrier()

        z_d = dscr("z", [L, 512], BF16)
        xbcT_d = dscr("xbcT", [8, 128, T], BF16)
        kT_d = dscr("kT", [8, 96, T], BF16)
        qT_d = dscr("qT", [8, 96, L], BF16)
        V_d = dscr("V", [T, 512], BF16)
        mid = contextlib.ExitStack()
        dtv = sb(mid, "dtv", [128, NT, 16], F32)
        onb = sb(mid, "onb", [128, 128], BF16)
        tk.op("dve", lambda e: e.tensor_copy(out=onb.t[:], in_=cst.t[:, C_ONE:C_ONE + 128]), [cst.b], [onb.b])
        dtv_d = dscr("dtv_dbg", [128, NT, 16], F32)
        phase1b()
        if debug and "dtv_dbg" in debug:
            tk.dma(dtv_d, dtv.t[:], reads=[dtv.b], writes=[db("dtvd")])


        mixT_d = dscr("mixT", [8, 128, L], BF16)

        def phase2b():
            with contextlib.ExitStack() as ph, scope("p2b"):
                kT = sb(ph, "akT", [128, 2, T], BF16, nb=2)
                qT = sb(ph, "aqT", [128, 2, L], BF16, nb=2)
                Vh = sb(ph, "aV", [128, 2, NT, 128], BF16, nb=2)
                P = sb(ph, "aP", [128, 3, 2, 512], BF16, nb=3)
                oc = sb(ph, "aoc", [65, 2, 512], F32, nb=2)
                rc = sb(ph, "arc", [65, 2, 512], F32, nb=2)
                bc = sb(ph, "abc", [64, 2, 512], F32, nb=2)
                on = sb(ph, "aon", [64, 2, 512], BF16, nb=2)
                pS_ = ps(ph, "aS", [128, 3, 2, 512], F32, nb=3)
                pO = ps(ph, "aO", [128, 512], F32)
                pB = ps(ph, "aB", [64, 512], F32)
                for s in range(2):
                    tk.op("pool", lambda e, s=s: e.memset(Vh.t[:, s], 0.0), [], [Vh.bufs[s]])
                    tk.op("pool", lambda e, s=s: e.memset(Vh.t[:, s, :, 64:65], 1.0), [], [Vh.bufs[s]])
                    tk.op("dve", lambda e, s=s: e.memset(kT.t[:, s, :], 0.0), [], [kT.bufs[s]])
                    tk.op("dve", lambda e, s=s: e.memset(qT.t[:, s, :], 0.0), [], [qT.bufs[s]])
                Vr = V_d.rearrange("(t p) (h d) -> p t h d", p=128, h=8)
                nh = 8
                nqb = 1 if (debug and "only_ctx" in debug) else 8
                NP = NT // 2

                def load_h(h):
                    s = h % 2
                    tk.dma(kT.t[0:96, s, :], kT_d[h], reads=[db("kT")], writes=[kT.bufs[s]])
                    tk.dma(qT.t[0:96, s, :], qT_d[h], reads=[db("qT")], writes=[qT.bufs[s]])
                    tk.dma(Vh.t[:, s, :, 0:64], Vr[:, :, h, :], reads=[db("V")], writes=[Vh.bufs[s]])

                units = [(h, qb) for h in range(nh) for qb in range(nqb)]
                gi = [0]

                def S(h, qb, kp, g):
                    s = h % 2
                    sl = g % 3
                    for j in range(2):
                        kt = kp * 2 + j
                        tk.op("pe", lambda e, j=j, kt=kt: e.matmul(pS_.t[:, sl, j, :], lhsT=kT.t[:, s, kt * 128:(kt + 1) * 128],
                                                                   rhs=qT.t[:, s, qb * 512:(qb + 1) * 512], start=True, stop=True),
                              [kT.bufs[s], qT.bufs[s]], [pS_.bufs[sl]])

                items = [(u, kp) for u in range(len(units)) for kp in range(NP)]
                pending = []
                load_h(0)
                for n_ahead in range(min(2, len(items))):
                    u, kp = items[n_ahead]
                    S(units[u][0], units[u][1], kp, n_ahead)
                for n, (u, kp) in enumerate(items):
                    h, qb = units[u]
                    s = h % 2
                    o = u % 2
                    if kp == 0 and qb == 0 and h + 1 < nh:
                        load_h(h + 1)
                    if n + 2 < len(items):
                        u2, kp2 = items[n + 2]
                        S(units[u2][0], units[u2][1], kp2, n + 2)
                    sl = n % 3
                    tk.op("act", lambda e: e.activation(out=P.t[:, sl], in_=pS_.t[:, sl], func=AF.Exp, scale=SCALE), [pS_.bufs[sl]], [P.bufs[sl]])
                    for j in range(2):
                        kt = kp * 2 + j
                        tk.op("pe", lambda e, j=j, kt=kt: e.matmul(pO.t[:], lhsT=Vh.t[:, s, kt, :], rhs=P.t[:, sl, j, :],
                                                                   start=(kt == 0), stop=(kt == NT - 1)), [Vh.bufs[s], P.bufs[sl]], [pO.b])
                    for fn_ in pending:
                        fn_()
                    pending.clear()
                    if kp == NP - 1:
                        tk.op("dve", lambda e: e.tensor_copy(out=oc.t[:, o, :], in_=pO.t[0:65, :]), [pO.b], [oc.bufs[o]])
                        tk.op("dve", lambda e: e.reciprocal(out=rc.t[64:65, o, :], in_=oc.t[64:65, o, :]), [oc.bufs[o]], [rc.bufs[o]])

                        def epi(o=o, h=h, qb=qb):
                            tk.op("pe", lambda e: e.matmul(pB.t[:], lhsT=cst.t[64:65, C_ONE:C_ONE + 64], rhs=rc.t[64:65, o, :], start=True, stop=True),
                                  [cst.b, rc.bufs[o]], [pB.b])
                            tk.op("dve", lambda e: e.tensor_copy(out=bc.t[:, o, :], in_=pB.t[:]), [pB.b], [bc.bufs[o]])
                            tk.op("pool", lambda e: e.tensor_tensor(out=on.t[:, o, :], in0=oc.t[0:64, o, :], in1=bc.t[:, o, :], op=ALU.mult),
                                  [oc.bufs[o], bc.bufs[o]], [on.bufs[o]])
                            tk.dma(mixT_d[4 + h // 2, (h % 2) * 64:(h % 2) * 64 + 64, qb * 512:(qb + 1) * 512], on.t[:, o, :], reads=[on.bufs[o]], writes=[db("mixT")])
                        pending.append(epi)
                for fn_ in pending:
                    fn_()
                tk.barrier()

        def phase2c():
            with contextlib.ExitStack() as ph, scope("p2c"):
                wo = sb(ph, "cwo", [128, 8, D], BF16)
                stg = sb(ph, "cstg", [128, 2, D], F32, nb=2)
                gb = sb(ph, "cgb", [128, D], F32)
                mx = sb(ph, "cmx", [128, 2, 8, 256], BF16, nb=2)
                xb = sb(ph, "cxb", [128, 2, 2, D], F32, nb=2)
                tt = sb(ph, "ctt", [128, 2, 512], F32, nb=2)
                pY = ps(ph, "cpY", [128, 2, 512], F32, nb=2)
                wor = w_out_d.rearrange("(kc k) n -> k kc n", k=128)
                for kc in range(8):
                    s = kc % 2
                    tk.dma(stg.t[:, s, :], wor[:, kc, :], writes=[stg.bufs[s]])
                    if kc < 4:
                        tk.op("dve", lambda e, s=s, kc=kc: e.tensor_scalar_mul(out=wo.t[:, kc, :], in0=stg.t[:, s, :], scalar1=cols.t[:, kc, 13:14]),
                              [stg.bufs[s], cols.b], [wo.b])
                    else:
                        tk.op("dve", lambda e, s=s, kc=kc: e.tensor_copy(out=wo.t[:, kc, :], in_=stg.t[:, s, :]), [stg.bufs[s]], [wo.b])
                src = mrow_d[0:1, 5 * D:6 * D]
                tk.dma(gb.t[:], bass.AP(src.tensor, src.offset, [[0, 128], [1, D]]), reads=[db("mrow")], writes=[gb.b])
                nblk = 2 if (debug and "only_ctx" in debug) else L // 256
                def rows_of(bi):
                    return x1_d[CL + bi * 256:CL + (bi + 1) * 256, :].rearrange("(t p) d -> p t d", p=128)

                def load_c(bi):
                    s = bi % 2
                    tk.dma(mx.t[:, s], mixT_d[:, :, bi * 256:(bi + 1) * 256].rearrange("c p n -> p c n"), reads=[db("mixT")], writes=[mx.bufs[s]])
                    tk.dma(xb.t[:, s], rows_of(bi), reads=[Buf()], writes=[xb.bufs[s]])

                load_c(0)
                for bi in range(nblk):
                    s = bi % 2
                    rows = rows_of(bi)
                    if bi + 1 < nblk:
                        load_c(bi + 1)
                    for t in range(2):
                        for hf in range(2):
                            for kc in range(8):
                                tk.op("pe", lambda e, t=t, hf=hf, kc=kc: e.matmul(pY.t[:, hf, :], lhsT=mx.t[:, s, kc, t * 128:(t + 1) * 128],
                                                                                  rhs=wo.t[:, kc, hf * 512:(hf + 1) * 512], start=(kc == 0), stop=(kc == 7)),
                                      [mx.bufs[s], wo.b], [pY.bufs[hf]])
                            tk.op("dve", lambda e, hf=hf: e.tensor_tensor(out=tt.t[:, hf, :], in0=pY.t[:, hf, :], in1=gb.t[:, hf * 512:(hf + 1) * 512], op=ALU.mult),
                                  [pY.bufs[hf], gb.b], [tt.bufs[hf]])
                            tk.op("pool", lambda e, t=t, hf=hf: e.tensor_tensor(out=xb.t[:, s, t, hf * 512:(hf + 1) * 512], in0=xb.t[:, s, t, hf * 512:(hf + 1) * 512],
                                                                                in1=tt.t[:, hf, :], op=ALU.add), [tt.bufs[hf], xb.bufs[s]], [xb.bufs[s]])
                    tk.dma(rows, xb.t[:, s], reads=[xb.bufs[s]], writes=[db("x1")])
                tk.barrier()


        yf_d = dscr("yf", [L, 512], F32)
        hf_d = dscr("hf_dbg", [2, 128, 512], F32)
        ecw_d = dscr("ecw_dbg", [128, 40], F32)
        xdtd_d = dscr("xdtd_dbg", [128, 512], BF16)
        xs_dbg = dscr("xs_dbg", [128, NT, 512], BF16)
        bmt_dbg = dscr("bmt_dbg", [128, NT, 256], BF16)

        def phase2a():
            with contextlib.ExitStack() as ph, scope("p2a"):
                xs = sb(ph, "sxs", [128, NT, 512], BF16)
                bmt = sb(ph, "sbmt", [128, NT, 256], BF16)
                bmT = sb(ph, "sbmT", [128, 2, T], BF16)
                cmT = sb(ph, "scmT", [128, 2, T], BF16)
                aall = sb(ph, "saall", [128, NT, 16], F32)
                anb = sb(ph, "sanb", [128, 16], F32)
                dsk = sb(ph, "sdsk", [128, 8], F32)
                tk.dma(anb.t[:], bass.AP(a_log_d.tensor, 0, [[0, 128], [1, 16]]), writes=[anb.b])
                tk.dma(dsk.t[:], bass.AP(d_skip_d.tensor, 0, [[0, 128], [1, 8]]), writes=[dsk.b])
                tk.op("act", lambda e: e.activation(out=anb.t[:], in_=anb.t[:], func=AF.Exp), [anb.b], [anb.b])
                tk.op("dve", lambda e: e.tensor_scalar_mul(out=anb.t[:], in0=anb.t[:], scalar1=-1.0), [anb.b], [anb.b])
                tk.op("dve", lambda e: e.tensor_tensor(out=aall.t[:], in0=dtv.t[:], in1=anb.t[:].unsqueeze(1).to_broadcast([128, NT, 16]), op=ALU.mult),
                      [dtv.b, anb.b], [aall.b])
                if debug and 'cv0' in debug:
                    tk.barrier()
                    return
                with contextlib.ExitStack() as cv:
                    xr = sb(cv, "sxr", [128, 2, T], BF16, nb=2)
                    acc = sb(cv, "sacc", [128, T], F32)
                    co = sb(cv, "sco", [128, 2, T], BF16, nb=2)
                    pT = ps(cv, "spT", [128, 2, 8, 128], BF16, nb=2)
                    ng = 0
                    for cc in range(8):
                        s = cc % 2
                        tk.dma(xr.t[:, s, :], xbcT_d[cc], reads=[db("xbcT")], writes=[xr.bufs[s]])
                        tk.op("dve", lambda e, cc=cc: e.tensor_scalar(out=acc.t[:], in0=xr.t[:, s, :], scalar1=cols.t[:, cc, 7:8], scalar2=cols.t[:, cc, 10:11],
                                                                      op0=ALU.mult, op1=ALU.add), [xr.bufs[s], cols.b], [acc.b])
                        if debug and 'cv1' in debug:
                            continue
                        for (lo, hi) in ((0, CL), (CL, T)):
                            for j in (0, 1, 3, 4):
                                sh = j - 2
                                o0 = lo + max(0, -sh)
                                o1 = hi - max(0, sh)
                                tk.op("dve", lambda e, cc=cc, j=j, o0=o0, o1=o1, sh=sh: e.scalar_tensor_tensor(
                                    out=acc.t[:, o0:o1], in0=xr.t[:, s, o0 + sh:o1 + sh], scalar=cols.t[:, cc, 5 + j:6 + j], in1=acc.t[:, o0:o1],
                                    op0=ALU.mult, op1=ALU.add), [xr.bufs[s], cols.b, acc.b], [acc.b])
                        if debug and 'cv2' in debug:
                            continue
                        if cc < 4:
                            dest, dbuf = co.t[:, s, :], co.bufs[s]
                        elif cc < 6:
                            dest, dbuf = bmT.t[:, cc - 4, :], bmT.b
                        else:
                            dest, dbuf = cmT.t[:, cc - 6, :], cmT.b
                        tk.op("act", lambda e, dest=dest: e.activation(out=dest, in_=acc.t[:], func=AF.Silu), [acc.b], [dbuf])
                        if debug and 'cv3' in debug:
                            continue
                        if cc < 6:
                            for tg in range(0, NT, 4):
                                n = min(4, NT - tg)
                                sl = ng % 2
                                ng += 1
                                for k in range(n):
                                    t = tg + k
                                    tk.op("pe", lambda e, k=k, t=t, dest=dest: e.transpose(out=pT.t[:, sl, k, :], in_=dest[:, t * 128:(t + 1) * 128], identity=idb.t[:]),
                                          [dbuf, idb.b], [pT.bufs[sl]])
                                if cc < 4:
                                    tk.op("dve" if ng % 2 else "pool" if False else "dve", lambda e, cc=cc, tg=tg, n=n: e.tensor_copy(out=xs.t[:, tg:tg + n, cc * 128:(cc + 1) * 128], in_=pT.t[:, sl, 0:n, :]),
                                          [pT.bufs[sl]], [xs.b])
                                else:
                                    tk.op("dve", lambda e, cc=cc, tg=tg, n=n: e.tensor_copy(out=bmt.t[:, tg:tg + n, (cc - 4) * 128:(cc - 3) * 128], in_=pT.t[:, sl, 0:n, :]),
                                          [pT.bufs[sl]], [bmt.b])
                    tk.barrier()
                if debug and 'xs_dbg' in debug:
                    tk.dma(xs_dbg, xs.t[:], reads=[xs.b], writes=[db('xsd')])
                    tk.dma(bmt_dbg, bmt.t[:], reads=[bmt.b], writes=[db('bmd')])
                if debug and 'conv_only' in debug:
                    return
                with contextlib.ExitStack() as sw:
                    hst = sb(sw, "shst", [128, 512], F32)
                    hbf = sb(sw, "shbf", [128, 2, 512], BF16, nb=2)
                    ec = sb(sw, "sec", [128, 2, 24], F32, nb=2)
                    aexp = sb(sw, "saexp", [128, 8, 128], F32)
                    eseg = sb(sw, "seseg", [128, 8, 128], F32)
                    Mt = sb(sw, "sM", [128, 2, 8, 128], BF16, nb=2)
                    cbm = sb(sw, "scbm", [128, 2, 128], F32)
                    w8 = sb(sw, "sw8", [128, 8], F32)
                    xdt = sb(sw, "sxdt", [128, 2, 8, 64], BF16, nb=2)
                    xdtd = sb(sw, "sxdtd", [128, 2, 8, 64], BF16, nb=2)
                    t1 = sb(sw, "st1", [128, 8, 64], F32)
                    yv = sb(sw, "syv", [128, 2, 512], F32, nb=2)
                    yfi = sb(sw, "syfi", [128, 2, 512], F32, nb=2)
                    zt = sb(sw, "szt", [128, 2, 512], BF16, nb=2)
                    sz = sb(sw, "ssz", [128, 512], F32)
                    sq2 = sb(sw, "ssq2", [128, 512], F32)
                    ss = sb(sw, "sss", [128, 8], F32)
                    ob = sb(sw, "sob", [128, 512], BF16)
                    obT = sb(sw, "sobT", [128, 2, 4, 128], BF16, nb=2)
                    pC = ps(sw, "spC", [128, 512], F32)
                    pSeg = ps(sw, "spSeg", [128, 2, 512], F32)
                    pCB = ps(sw, "spCB", [128, 4, 128], F32)
                    pY = ps(sw, "spY", [128, 512], F32)
                    pYo = ps(sw, "spYo", [128, 512], F32)
                    pSt = ps(sw, "spSt", [128, 512], F32)
                    pTo = ps(sw, "spTo", [128, 8, 128], BF16)
                    nx = 4 if (debug and "only_ctx" in debug) else NT - 2
                    for d in range(2):
                        if d == 0:
                            order = list(range(2 + nx))
                            m_incl, m_excl, m_seg = C_LE, C_GT, C_GT
                        else:
                            order = [1, 0] + list(range(2 + nx - 1, 1, -1))
                            m_incl, m_excl, m_seg = C_GE, C_LT, C_LT
                        tk.op("dve", lambda e: e.memset(hst.t[:], 0.0), [], [hst.b])
                        tk.op("dve", lambda e: e.memset(hbf.t[:, 0, :], 0.0), [], [hbf.bufs[0]])

                        def front(i):
                            c = order[i]
                            s_ = i % 2
                            a = aall.t[:, c, d * 8:(d + 1) * 8]
                            dtc = dtv.t[:, c, d * 8:(d + 1) * 8]
                            for q_, msk in enumerate((m_incl, m_excl, C_ONE)):
                                tk.op("pe", lambda e, q_=q_, msk=msk: e.matmul(pC.t[:, q_ * 8:(q_ + 1) * 8], lhsT=cst.t[:, msk:msk + 128], rhs=a, start=True, stop=True),
                                      [cst.b, aall.b], [pC.b])
                            ecs = ec.t[:, s_, :]
                            tk.op("act", lambda e: e.activation(out=ecs, in_=pC.t[:, 0:24], func=AF.Exp), [pC.b], [ec.bufs[s_]])
                            tk.op("dve", lambda e: e.tensor_tensor(out=w8.t[:], in0=dtc, in1=ecs[:, 8:16], op=ALU.mult), [dtv.b, ec.bufs[s_]], [w8.b])
                            xsc = xs.t[:, c, :].rearrange("p (h d) -> p h d", h=8)
                            tk.op("pool", lambda e: e.tensor_tensor(out=xdtd.t[:, s_], in0=xsc, in1=w8.t[:].unsqueeze(2).to_broadcast([128, 8, 64]), op=ALU.mult),
                                  [xs.b, w8.b], [xdtd.bufs[s_]])
                            if c < 2:
                                return
                            tk.op("dve", lambda e: e.tensor_tensor(out=aexp.t[:], in0=cst.t[:, m_incl:m_incl + 128].unsqueeze(1).to_broadcast([128, 8, 128]),
                                                                   in1=a.unsqueeze(2).to_broadcast([128, 8, 128]), op=ALU.mult), [cst.b, aall.b], [aexp.b])
                            for hh in range(2):
                                tk.op("pe", lambda e, hh=hh: e.matmul(pSeg.t[:, hh, :], lhsT=cst.t[:, m_seg:m_seg + 128],
                                                                      rhs=aexp.t[:, hh * 4:(hh + 1) * 4, :].rearrange("p h i -> p (h i)"), start=True, stop=True),
                                      [cst.b, aexp.b], [pSeg.b])
                            tk.op("act", lambda e: e.activation(out=eseg.t[:].rearrange("p h i -> p (h i)"), in_=pSeg.t[:].rearrange("p a n -> p (a n)"), func=AF.Exp),
                                  [pSeg.b], [eseg.b])
                            for g in range(2):
                                tk.op("pe", lambda e, g=g: e.matmul(pCB.t[:, g, :], lhsT=bmT.t[:, g, c * 128:(c + 1) * 128], rhs=cmT.t[:, g, c * 128:(c + 1) * 128],
                                                                    start=True, stop=True), [bmT.b, cmT.b], [pCB.b])
                            tk.op("dve", lambda e: e.tensor_tensor(out=cbm.t[:], in0=pCB.t[:, 0:2, :], in1=cst.t[:, m_incl:m_incl + 128].unsqueeze(1).to_broadcast([128, 2, 128]), op=ALU.mult),
                                  [pCB.b, cst.b], [cbm.b])
                            tk.op("dve", lambda e: e.tensor_tensor(out=Mt.t[:, s_].rearrange("p (g h) i -> p g h i", g=2), in0=eseg.t[:].rearrange("p (g h) i -> p g h i", g=2),
                                                                   in1=cbm.t[:].unsqueeze(2).to_broadcast([128, 2, 4, 128]), op=ALU.mult), [eseg.b, cbm.b], [Mt.bufs[s_]])
                            tk.op("pool", lambda e: e.tensor_tensor(out=xdt.t[:, s_], in0=xsc, in1=dtc.unsqueeze(2).to_broadcast([128, 8, 64]), op=ALU.mult),
                                  [xs.b, dtv.b], [xdt.bufs[s_]])

                        def back(i):
                            c = order[i]
                            s_ = i % 2
                            ecs = ec.t[:, s_, :]
                            xsc = xs.t[:, c, :].rearrange("p (h d) -> p h d", h=8)
                            for g in range(2):
                                tk.op("pe", lambda e, g=g: e.matmul(pSt.t[:, g * 256:(g + 1) * 256], lhsT=bmt.t[:, c, g * 128:(g + 1) * 128],
                                                                    rhs=xdtd.t[:, s_, g * 4:(g + 1) * 4, :].rearrange("p h d -> p (h d)"), start=True, stop=True),
                                      [bmt.b, xdtd.bufs[s_]], [pSt.b])
                            tk.op("dve", lambda e: e.tensor_tensor(out=hst.t[:].rearrange("p (h d) -> p h d", h=8), in0=hst.t[:].rearrange("p (h d) -> p h d", h=8),
                                                                   in1=ecs[:, 16:24].unsqueeze(2).to_broadcast([128, 8, 64]), op=ALU.mult), [hst.b, ec.bufs[s_]], [hst.b])
                            tk.op("dve", lambda e: e.tensor_tensor(out=hst.t[:], in0=hst.t[:], in1=pSt.t[:], op=ALU.add), [hst.b, pSt.b], [hst.b])
                            tk.op("pool", lambda e: e.tensor_copy(out=hbf.t[:, 1 - s_, :], in_=hst.t[:]), [hst.b], [hbf.bufs[1 - s_]])
                            if c < 2:
                                return
                            for h in range(8):
                                tk.op("pe", lambda e, h=h: e.matmul(pY.t[:, h * 64:(h + 1) * 64], lhsT=Mt.t[:, s_, h, :], rhs=xdt.t[:, s_, h, :], start=True, stop=True),
                                      [Mt.bufs[s_], xdt.bufs[s_]], [pY.b])
                            for g in range(2):
                                tk.op("pe", lambda e, g=g: e.matmul(pYo.t[:, g * 256:(g + 1) * 256], lhsT=cmT.t[:, g, c * 128:(c + 1) * 128], rhs=hbf.t[:, s_, g * 256:(g + 1) * 256],
                                                                    start=True, stop=True), [cmT.b, hbf.bufs[s_]], [pYo.b])
                            ys = i % 2
                            tk.op("dve", lambda e: e.tensor_tensor(out=t1.t[:], in0=pYo.t[:].rearrange("p (h d) -> p h d", h=8),
                                                                   in1=ecs[:, 0:8].unsqueeze(2).to_broadcast([128, 8, 64]), op=ALU.mult), [pYo.b, ec.bufs[s_]], [t1.b])
                            tk.op("dve", lambda e: e.tensor_tensor(out=yv.t[:, ys, :], in0=t1.t[:].rearrange("p h d -> p (h d)"), in1=pY.t[:], op=ALU.add),
                                  [t1.b, pY.b], [yv.bufs[ys]])
                            r0 = (c - 2) * 128
                            if d == 0:
                                tk.dma(yf_d[r0:r0 + 128, :], yv.t[:, ys, :], reads=[yv.bufs[ys]], writes=[db("yf")])
                                return
                            tk.dma(yfi.t[:, ys, :], yf_d[r0:r0 + 128, :], reads=[db("yf")], writes=[yfi.bufs[ys]])
                            tk.dma(zt.t[:, ys, :], z_d[r0:r0 + 128, :], reads=[db("z")], writes=[zt.bufs[ys]])
                            tk.op("pool", lambda e: e.tensor_tensor(out=yv.t[:, ys, :], in0=yv.t[:, ys, :], in1=yfi.t[:, ys, :], op=ALU.add),
                                  [yv.bufs[ys], yfi.bufs[ys]], [yv.bufs[ys]])
                            tk.op("dve", lambda e: e.tensor_tensor(out=t1.t[:], in0=xsc, in1=dsk.t[:].unsqueeze(2).to_broadcast([128, 8, 64]), op=ALU.mult),
                                  [xs.b, dsk.b], [t1.b])
                            tk.op("pool", lambda e: e.tensor_tensor(out=yv.t[:, ys, :], in0=yv.t[:, ys, :], in1=t1.t[:].rearrange("p h d -> p (h d)"), op=ALU.add),
                                  [yv.bufs[ys], t1.b], [yv.bufs[ys]])
                            tk.op("act", lambda e: e.activation(out=sz.t[:], in_=zt.t[:, ys, :], func=AF.Silu), [zt.bufs[ys]], [sz.b])
                            tk.op("dve", lambda e: e.tensor_tensor(out=yv.t[:, ys, :], in0=yv.t[:, ys, :], in1=sz.t[:], op=ALU.mult), [yv.bufs[ys], sz.b], [yv.bufs[ys]])
                            tk.op("pool", lambda e: e.tensor_tensor(out=sq2.t[:], in0=yv.t[:, ys, :], in1=yv.t[:, ys, :], op=ALU.mult), [yv.bufs[ys]], [sq2.b])
                            tk.op("dve", lambda e: e.reduce_sum(out=ss.t[:, 0:1], in_=sq2.t[:], axis=mybir.AxisListType.X), [sq2.b], [ss.b])
                            tk.op("dve", lambda e: e.tensor_scalar(out=ss.t[:, 1:2], in0=ss.t[:, 0:1], scalar1=1.0 / 512, scalar2=EPS, op0=ALU.mult, op1=ALU.add), [ss.b], [ss.b])
                            tk.op("act", lambda e: e.sqrt(out=ss.t[:, 2:3], in_=ss.t[:, 1:2]), [ss.b], [ss.b])
                            tk.op("dve", lambda e: e.reciprocal(out=ss.t[:, 3:4], in_=ss.t[:, 2:3]), [ss.b], [ss.b])
                            tk.op("dve", lambda e: e.tensor_scalar_mul(out=ob.t[:], in0=yv.t[:, ys, :], scalar1=ss.t[:, 3:4]), [yv.bufs[ys], ss.b], [ob.b])
                            for k in range(4):
                                tk.op("pe", lambda e, k=k: e.transpose(out=pTo.t[:, k, :], in_=ob.t[:, k * 128:(k + 1) * 128], identity=idb.t[:]), [ob.b, idb.b], [pTo.b])
                            tk.op("act", lambda e: e.copy(out=obT.t[:, ys], in_=pTo.t[:, 0:4, :]), [pTo.b], [obT.bufs[ys]])
                            tk.dma(mixT_d[0:4, :, r0:r0 + 128].rearrange("c p n -> p c n"), obT.t[:, ys], reads=[obT.bufs[ys]], writes=[db("mixT")])

                        front(0)
                        for i in range(len(order)):
                            if i + 1 < len(order):
                                front(i + 1)
                            back(i)
                    tk.barrier()

        if ENABLE_SSD and not (debug and "skip_ssd" in debug):
            phase2a()
        phase2b()
        phase2c()
        mid.close()
        blocks3 = []
        for i in range(L // 256):
            blocks3.append((x1_d[CL + i * 256:CL + (i + 1) * 256, :], out_d[i * 256:(i + 1) * 256, :], 0, db("x1"), db("out")))
        if debug and "only_ctx" in debug:
            blocks3 = blocks3[:2]
        ffn_phase("f2", w_ffn2_in_d, w_ffn2_out_d, 2, 8, blocks3, True)

        tk.barrier(("sp",))
    return nc


_CONSTS = None


def _consts():
    global _CONSTS
    if _CONSTS is None:
        c = np.zeros((128, NCONST), np.float32)
        k = np.arange(128)[:, None]
        i = np.arange(128)[None, :]
        c[:, C_ID:C_ID + 128] = (k == i)
        c[:, C_LE:C_LE + 128] = (k <= i)
        c[:, C_GT:C_GT + 128] = (k > i)
        c[:, C_GE:C_GE + 128] = (k >= i)
        c[:, C_LT:C_LT + 128] = (k < i)
        c[:, C_ONE:C_ONE + 128] = 1.0
        c[0, C_SEL:C_SEL + 128] = 1.0
        c[1, C_SEL + 128:C_SEL + 256] = 1.0
        half = 16
        inv = 10000.0 ** (-np.arange(0, half, 2, dtype=np.float32) / half)
        pos = np.arange(L)
        row = (pos // 64).astype(np.float32)
        colp = (pos % 64).astype(np.float32)
        rope = np.zeros((2, 32, L), np.float32)
        for base, p in ((0, row), (16, colp)):
            ang = p[None, :] * inv[:, None]
            cs, sn = np.cos(ang), np.sin(ang)
            rope[0, base:base + 8] = cs
            rope[0, base + 8:base + 16] = cs
            rope[1, base:base + 8] = -sn
            rope[1, base + 8:base + 16] = sn
        _CONSTS = (c, rope)
    return _CONSTS


def make_in_maps(inputs, cores=range(8)):
    f = lambda a: np.ascontiguousarray(np.asarray(a, dtype=np.float32))
    c, rope = _consts()
    shared = {
        "w_ada": f(inputs["w_ada"][0]), "b_ada": f(inputs["b_ada"][0]).reshape(1, -1),
        "norm_ffn1": f(inputs["norm_ffn1"][0]).reshape(1, -1), "w_ffn1_in": f(inputs["w_ffn1_in"][0]),
        "w_ffn1_out": f(inputs["w_ffn1_out"][0]), "norm_mix": f(inputs["norm_mix"][0]).reshape(1, -1),
        "w_in": f(inputs["w_in"][0]), "conv_w": f(inputs["conv_w"][0]), "conv_b": f(inputs["conv_b"][0]).reshape(1, -1),
        "dt_bias": f(inputs["dt_bias"][0]).reshape(1, 16), "a_log": f(inputs["a_log"][0]).reshape(1, 16),
        "d_skip": f(inputs["d_skip"][0]).reshape(1, 8), "ssd_norm": f(inputs["ssd_norm"][0]).reshape(1, -1),
        "q_norm": f(inputs["q_norm"][0]).reshape(1, -1), "w_uq": f(inputs["w_uq"][0]),
        "kv_norm": f(inputs["kv_norm"][0]).reshape(1, -1), "w_ukv": f(inputs["w_ukv"][0]), "w_out": f(inputs["w_out"][0]),
        "norm_ffn2": f(inputs["norm_ffn2"][0]).reshape(1, -1), "w_ffn2_in": f(inputs["w_ffn2_in"][0]),
        "w_ffn2_out": f(inputs["w_ffn2_out"][0]), "final_norm": f(inputs["final_norm"]).reshape(1, -1),
        "consts": c, "rope": rope,
    }
    maps = []
    for b in cores:
        m = dict(shared)
        m["x"] = f(inputs["x"][b])
        m["ctx"] = f(inputs["ctx"][b])
        m["cc"] = np.stack([f(inputs["c"][b]), f(inputs["c_ctx"])], 0)
        maps.append(m)
    return maps


def kernel(**inputs):
    nc = build()
    maps = make_in_maps(inputs)
    res = run_bass_kernel_spmd(nc, maps, core_ids=list(range(8)))
    return np.stack([np.asarray(r["out"], dtype=np.float32) for r in res.results], 0)
```

```python
import contextlib
import numpy as np
import ml_dtypes
import concourse.bass as bass
import concourse.mybir as mybir
from concourse.bass_utils import run_bass_kernel_spmd

F32 = mybir.dt.float32
BF16 = mybir.dt.bfloat16
AF = mybir.ActivationFunctionType
ALU = mybir.AluOpType

D = 1024
L = 4096
CL = 256
T = L + CL
NT = T // 128
DFF = 2816
NKF = DFF // 128
EPS = 1e-6
IN_DIM = 2608
SCALE = 96 ** -0.5
ENABLE_SSD = True

C_ID = 0
C_LE = 128
C_GT = 256
C_GE = 384
C_LT = 512
C_ONE = 640
C_SEL = 768
NCONST = 1024


def _merge(d, s):
    for k, v in s.items():
        if d.get(k, 0) < v:
            d[k] = v


class Buf:
    __slots__ = ("name", "w", "r")

    def __init__(self, name=""):
        self.name = name
        self.w = {}
        self.r = {}


class TK:
    def __init__(self, nc, es, n_dma=14):
        self.nc = nc
        self.eng = {"pe": nc.tensor, "act": nc.scalar, "dve": nc.vector, "pool": nc.gpsimd, "sp": nc.sync}
        self.sem = {}
        self.cnt = {}
        for n in ("pe", "act", "dve", "pool"):
            self.sem[n] = es.enter_context(nc.semaphore("s_" + n))
            self.cnt[n] = 0
        self.nd = n_dma
        for i in range(n_dma):
            self.sem[f"d{i}"] = es.enter_context(nc.semaphore(f"s_d{i}"))
            self.cnt[f"d{i}"] = 0
        self.drr = 0
        self.seen = {n: {} for n in self.eng}

    def _wait(self, e, deps):
        eng = self.eng[e]
        seen = self.seen[e]
        for k, v in deps.items():
            if k == e:
                continue
            if seen.get(k, 0) < v:
                eng.wait_ge(self.sem[k], v)
                seen[k] = v

    def op(self, e, fn, reads=(), writes=()):
        deps = {}
        for b in reads:
            _merge(deps, b.w)
        own = deps.get(e, 0)
        if e != "pe" and own > 0 and self.cnt[e] - own < 2 and self.seen[e].get(e, 0) < own:
            self.eng[e].wait_ge(self.sem[e], own)
            self.seen[e][e] = own
        for b in writes:
            _merge(deps, b.r)
            _merge(deps, b.w)
        self._wait(e, deps)
        ins = fn(self.eng[e])
        self.cnt[e] += 1
        ins.then_inc(self.sem[e], 1)
        c = self.cnt[e]
        for b in reads:
            b.r[e] = c
        for b in writes:
            b.w[e] = c
        return ins

    def dma(self, out, in_, reads=(), writes=(), q="sp", **kw):
        k = self.drr
        self.drr = (self.drr + 1) % self.nd
        key = f"d{k}"
        deps = {}
        for b in reads:
            _merge(deps, b.w)
        for b in writes:
            _merge(deps, b.r)
            _merge(deps, b.w)
        if self.cnt[key] > 0:
            deps[key] = max(deps.get(key, 0), self.cnt[key])
        self._wait(q, deps)
        ins = self.eng[q].dma_start(out=out, in_=in_, **kw)
        self.cnt[key] += 16
        ins.then_inc(self.sem[key], 16)
        c = self.cnt[key]
        for b in reads:
            b.r[key] = c
        for b in writes:
            b.w[key] = c
        return ins

    def barrier(self, engines=("pe", "act", "dve", "pool", "sp")):
        for e in engines:
            deps = {k: v for k, v in self.cnt.items() if v > 0}
            self._wait(e, deps)


class SB:
    def __init__(self, t, nb=1):
        self.t = t
        self.bufs = [Buf() for _ in range(nb)]

    @property
    def b(self):
        return self.bufs[0]


def build(debug=None):
    nc = bass.Bass("TRN2", target_bir_lowering=False)

    def din(name, shape, dt=F32):
        return nc.dram_tensor(name, list(shape), dt, kind="ExternalInput").ap()

    x_d = din("x", [L, D])
    ctx_d = din("ctx", [CL, D])
    cc_d = din("cc", [2, D])
    w_ada_d = din("w_ada", [D, 9 * D])
    b_ada_d = din("b_ada", [1, 9 * D])
    norm_ffn1_d = din("norm_ffn1", [1, D])
    w_ffn1_in_d = din("w_ffn1_in", [D, 2 * DFF])
    w_ffn1_out_d = din("w_ffn1_out", [DFF, D])
    norm_mix_d = din("norm_mix", [1, D])
    w_in_d = din("w_in", [D, IN_DIM])
    conv_w_d = din("conv_w", [5, D])
    conv_b_d = din("conv_b", [1, D])
    dt_bias_d = din("dt_bias", [1, 16])
    a_log_d = din("a_log", [1, 16])
    d_skip_d = din("d_skip", [1, 8])
    ssd_norm_d = din("ssd_norm", [1, 512])
    q_norm_d = din("q_norm", [1, 768])
    w_uq_d = din("w_uq", [768, 768])
    kv_norm_d = din("kv_norm", [1, 256])
    w_ukv_d = din("w_ukv", [256, 1024])
    w_out_d = din("w_out", [D, D])
    norm_ffn2_d = din("norm_ffn2", [1, D])
    w_ffn2_in_d = din("w_ffn2_in", [D, 2 * DFF])
    w_ffn2_out_d = din("w_ffn2_out", [DFF, D])
    final_norm_d = din("final_norm", [1, D])
    consts_d = din("consts", [128, NCONST])
    rope_d = din("rope", [2, 32, L])

    out_d = nc.dram_tensor("out", [L, D], F32, kind="ExternalOutput").ap()

    def dscr(name, shape, dt):
        kind = "ExternalOutput" if (debug and name in debug) else "Internal"
        return nc.dram_tensor(name, list(shape), dt, kind=kind).ap()

    mrow_d = dscr("mrow", [2, 9 * D], F32)
    x1_d = dscr("x1", [T, D], F32)

    def scope(name):
        return nc.named_scope(name) if (debug and "scopes" in debug) else contextlib.nullcontext()

    with contextlib.ExitStack() as es:
        tk = TK(nc, es)

        def sb(st, name, shape, dt=F32, nb=1):
            return SB(st.enter_context(nc.sbuf_tensor(name, list(shape), dt)), nb)

        def ps(st, name, shape, dt=F32, nb=1):
            return SB(st.enter_context(nc.psum_tensor(name, list(shape), dt)), nb)

        dram_bufs = {}

        def db(name):
            if name not in dram_bufs:
                dram_bufs[name] = Buf(name)
            return dram_bufs[name]

        cst = sb(es, "cst", [128, NCONST])
        idb = sb(es, "idb", [128, 128], BF16)
        cols = sb(es, "cols", [128, 8, 32])
        modp = sb(es, "modp", [128, 3, 2, 2, 8])
        tk.dma(cst.t[:], consts_d, writes=[cst.b])
        tk.op("dve", lambda e: e.tensor_copy(out=idb.t[:], in_=cst.t[:, C_ID:C_ID + 128]), [cst.b], [idb.b])

        with contextlib.ExitStack() as p0, scope("p0"):
            stage = sb(p0, "stage", [32, D])
            wa = sb(p0, "wa", [128, 4, 8, 512], F32, nb=4)
            mrow = sb(p0, "mrow_sb", [2, 9 * D])
            bada = sb(p0, "bada", [1, 9 * D])
            pm = ps(p0, "pm", [2, 2, 512], F32, nb=2)
            pt = ps(p0, "pt", [128, 8, 32], F32)
            pc = ps(p0, "pc", [128, 72, 2], F32)
            mcol = sb(p0, "mcol", [128, 72, 2])

            tk.op("pool", lambda e: e.memset(stage.t[:], 0.0), [], [stage.b])
            tk.dma(stage.t[0:2, :], cc_d, writes=[stage.b])
            rows = [(2, norm_ffn1_d, D), (3, norm_mix_d, D), (4, norm_ffn2_d, D), (10, conv_b_d, D),
                    (11, q_norm_d, 768), (12, kv_norm_d, 256), (13, ssd_norm_d, 512)]
            for r, src, n in rows:
                tk.dma(stage.t[r:r + 1, 0:n], src, writes=[stage.b])
            tk.dma(stage.t[5:10, :], conv_w_d, writes=[stage.b])
            tk.dma(bada.t[:], b_ada_d, writes=[bada.b])
            tk.op("act", lambda e: e.activation(out=stage.t[0:2, :], in_=stage.t[0:2, :], func=AF.Silu),
                  [stage.b], [stage.b])
            for kc in range(8):
                tk.op("pe", lambda e, kc=kc: e.transpose(out=pt.t[:, kc, :], in_=stage.t[0:32, kc * 128:(kc + 1) * 128],
                                                         identity=cst.t[0:32, C_ID:C_ID + 32]),
                      [stage.b, cst.b], [pt.b])
            tk.op("dve", lambda e: e.tensor_copy(out=cols.t[:], in_=pt.t[:]), [pt.b], [cols.b])

            war = w_ada_d.rearrange("(kc k) n -> k kc n", k=128)
            for nb in range(18):
                s = nb % 4
                tk.dma(wa.t[:, s], war[:, :, nb * 512:(nb + 1) * 512], writes=[wa.bufs[s]])
                pb = nb % 2
                for kc in range(8):
                    tk.op("pe", lambda e, kc=kc, s=s, pb=pb: e.matmul(pm.t[:, pb, :], lhsT=cols.t[:, kc, 0:2],
                                                                     rhs=wa.t[:, s, kc, :], start=(kc == 0), stop=False),
                          [cols.b, wa.bufs[s]], [pm.bufs[pb]])
                tk.op("pe", lambda e, pb=pb, nb=nb: e.matmul(pm.t[:, pb, :], lhsT=cst.t[0:1, C_ONE:C_ONE + 2],
                                                            rhs=bada.t[0:1, nb * 512:(nb + 1) * 512], start=False, stop=True),
                      [cst.b, bada.b], [pm.bufs[pb]])
                tk.op("dve", lambda e, pb=pb, nb=nb: e.tensor_copy(out=mrow.t[:, nb * 512:(nb + 1) * 512], in_=pm.t[:, pb, :]),
                      [pm.bufs[pb]], [mrow.b])
            tk.dma(mrow_d, mrow.t[:], reads=[mrow.b], writes=[db("mrow")])
            for j in range(72):
                tk.op("pe", lambda e, j=j: e.transpose(out=pc.t[:, j, :], in_=mrow.t[0:2, j * 128:(j + 1) * 128],
                                                       identity=cst.t[0:2, C_ID:C_ID + 2]),
                      [mrow.b, cst.b], [pc.b])
            tk.op("dve", lambda e: e.tensor_copy(out=mcol.t[:], in_=pc.t[:]), [pc.b], [mcol.b])
            for s_ in range(3):
                for b_ in range(2):
                    tk.op("dve", lambda e, s_=s_, b_=b_: e.scalar_tensor_tensor(
                        out=modp.t[:, s_, b_, 0, :], in0=mcol.t[:, (3 * s_ + 1) * 8:(3 * s_ + 2) * 8, b_], scalar=1.0,
                        in1=cols.t[:, :, 2 + s_], op0=ALU.add, op1=ALU.mult), [mcol.b, cols.b], [modp.b])
                    tk.op("dve", lambda e, s_=s_, b_=b_: e.tensor_copy(
                        out=modp.t[:, s_, b_, 1, :], in_=mcol.t[:, (3 * s_) * 8:(3 * s_ + 1) * 8, b_]), [mcol.b], [modp.b])
            tk.barrier()

        def ffn_phase(tag, w_in_dram, w_out_dram, mset, gate_idx, blocks, final):
            with contextlib.ExitStack() as ph, scope(tag):
                wi = sb(ph, tag + "wi", [128, 8, 2 * DFF], BF16)
                wo = sb(ph, tag + "wo", [128, NKF, D], BF16)
                gbs = sb(ph, tag + "gb", [128, 2, D], F32, nb=2)
                fnb = sb(ph, tag + "fnb", [128, D], F32) if final else None
                bs = set(b_ for (_, _, b_, _, _) in blocks)
                for b_ in sorted(bs):
                    src = mrow_d[b_:b_ + 1, gate_idx * D:(gate_idx + 1) * D]
                    tk.dma(gbs.t[:, b_, :], bass.AP(src.tensor, src.offset, [[0, 128], [1, D]]),
                           reads=[db("mrow")], writes=[gbs.bufs[b_]])
                    tk.op("pool", lambda e, b_=b_: e.tensor_scalar_mul(out=gbs.t[:, b_, :], in0=gbs.t[:, b_, :], scalar1=0.5),
                          [gbs.bufs[b_]], [gbs.bufs[b_]])
                if final:
                    tk.dma(fnb.t[:], bass.AP(final_norm_d.tensor, 0, [[0, 128], [1, D]]), writes=[fnb.b])
                with contextlib.ExitStack() as st:
                    stg = sb(st, tag + "stg", [128, 3, 2048], F32, nb=3)
                    pieces = []
                    wir = w_in_dram.rearrange("(kc k) n -> k kc n", k=128)
                    for kc in range(8):
                        for q4 in range(4):
                            pieces.append((wir[:, kc, q4 * 1408:(q4 + 1) * 1408], wi.t[:, kc, q4 * 1408:(q4 + 1) * 1408], 1408, wi.b))
                    wor = w_out_dram.rearrange("(kc k) n -> k kc n", k=128)
                    for kc in range(NKF):
                        pieces.append((wor[:, kc, :], wo.t[:, kc, :], 1024, wo.b))
                    cast_eng = ["dve", "pool", "act"]
                    for i, (src, dst, n, wb) in enumerate(pieces):
                        s = i % 3
                        tk.dma(stg.t[:, s, 0:n], src, writes=[stg.bufs[s]])
                        ce = cast_eng[i % 3]
                        if ce == "act":
                            tk.op("act", lambda e, s=s, n=n, dst=dst: e.copy(out=dst, in_=stg.t[:, s, 0:n]), [stg.bufs[s]], [wb])
                        else:
                            tk.op(ce, lambda e, s=s, n=n, dst=dst: e.tensor_copy(out=dst, in_=stg.t[:, s, 0:n]), [stg.bufs[s]], [wb])
                    tk.barrier()
                xb = sb(ph, tag + "xb", [128, 2, 2, D], F32, nb=2)
                xn = sb(ph, tag + "xn", [128, 2, 2, D], BF16, nb=2)
                hT = sb(ph, tag + "hT", [128, 2, 8, 256], BF16, nb=2)
                tmpm = sb(ph, tag + "tmpm", [128, 8, 128], F32)
                gsb = sb(ph, tag + "g", [128, NKF, 256], BF16)
                sg = sb(ph, tag + "sg", [128, 2, 512], F32, nb=2)
                tt = sb(ph, tag + "tt", [128, 2, 512], F32, nb=2)
                junk = sb(ph, tag + "junk", [128, D], BF16)
                ss = sb(ph, tag + "ss", [128, 2, 8], F32, nb=2)
                pG = ps(ph, tag + "pG", [128, 2, 512], F32, nb=2)
                pU = ps(ph, tag + "pU", [128, 2, 512], F32, nb=2)
                pY = ps(ph, tag + "pY", [128, 2, 512], F32, nb=2)
                pT = ps(ph, tag + "pT", [128, 2, 8, 128], BF16, nb=2)
                nblk = len(blocks)

                def load(i):
                    src, _, _, sbuf, _ = blocks[i]
                    s = i % 2
                    tk.dma(xb.t[:, s], src.rearrange("(t p) d -> p t d", p=128), reads=[sbuf], writes=[xb.bufs[s]])

                def norm(i):
                    s = i % 2
                    sbf = ss.bufs[s]
                    for t in range(2):
                        tk.op("act", lambda e, t=t: e.activation(out=junk.t[:], in_=xb.t[:, s, t, :], func=AF.Square,
                                                                  accum_out=ss.t[:, s, t:t + 1]), [xb.bufs[s]], [junk.b, sbf])
                    tk.op("dve", lambda e: e.tensor_scalar(out=ss.t[:, s, 2:4], in0=ss.t[:, s, 0:2], scalar1=1.0 / D, scalar2=EPS,
                                                           op0=ALU.mult, op1=ALU.add), [sbf], [sbf])
                    tk.op("act", lambda e: e.sqrt(out=ss.t[:, s, 4:6], in_=ss.t[:, s, 2:4]), [sbf], [sbf])
                    tk.op("dve", lambda e: e.reciprocal(out=ss.t[:, s, 6:8], in_=ss.t[:, s, 4:6]), [sbf], [sbf])
                    for t in range(2):
                        tk.op("dve", lambda e, t=t: e.tensor_scalar_mul(out=xn.t[:, s, t, :], in0=xb.t[:, s, t, :],
                                                                        scalar1=ss.t[:, s, 6 + t:7 + t]),
                              [xb.bufs[s], sbf], [xn.bufs[s]])

                def transp(i):
                    s = i % 2
                    b_ = blocks[i][2]
                    for t in range(2):
                        for kc in range(8):
                            tk.op("pe", lambda e, t=t, kc=kc: e.transpose(out=pT.t[:, t, kc, :], in_=xn.t[:, s, t, kc * 128:(kc + 1) * 128],
                                                                            identity=idb.t[:]), [xn.bufs[s], idb.b], [pT.bufs[t]])
                        G = modp.t[:, mset, b_, 0, :].unsqueeze(2).to_broadcast([128, 8, 128])
                        S = modp.t[:, mset, b_, 1, :].unsqueeze(2).to_broadcast([128, 8, 128])
                        tk.op("dve", lambda e, t=t, G=G: e.tensor_tensor(out=tmpm.t[:], in0=pT.t[:, t], in1=G, op=ALU.mult),
                              [pT.bufs[t], modp.b], [tmpm.b])
                        tk.op("pool", lambda e, t=t, S=S: e.tensor_tensor(out=hT.t[:, s, :, t * 128:(t + 1) * 128], in0=tmpm.t[:], in1=S, op=ALU.add),
                              [tmpm.b, modp.b], [hT.bufs[s]])

                def ffn_in(i):
                    s = i % 2
                    for fp in range(NKF // 2):
                        pb = fp % 2
                        for j in range(2):
                            fc = fp * 2 + j
                            for kc in range(8):
                                tk.op("pe", lambda e, j=j, fc=fc, kc=kc: e.matmul(pG.t[:, pb, j * 256:(j + 1) * 256], lhsT=wi.t[:, kc, fc * 128:(fc + 1) * 128],
                                                                                   rhs=hT.t[:, s, kc, :], start=(kc == 0), stop=(kc == 7)),
                                      [wi.b, hT.bufs[s]], [pG.bufs[pb]])
                            for kc in range(8):
                                tk.op("pe", lambda e, j=j, fc=fc, kc=kc: e.matmul(pU.t[:, pb, j * 256:(j + 1) * 256], lhsT=wi.t[:, kc, DFF + fc * 128:DFF + (fc + 1) * 128],
                                                                                   rhs=hT.t[:, s, kc, :], start=(kc == 0), stop=(kc == 7)),
                                      [wi.b, hT.bufs[s]], [pU.bufs[pb]])
                        tk.op("act", lambda e: e.activation(out=sg.t[:, pb, :], in_=pG.t[:, pb, :], func=AF.Silu), [pG.bufs[pb]], [sg.bufs[pb]])
                        tk.op("dve", lambda e, fp=fp: e.tensor_tensor(out=gsb.t[:, fp * 2:fp * 2 + 2, :], in0=sg.t[:, pb, :].rearrange("p (a b) -> p a b", a=2),
                                                                      in1=pU.t[:, pb, :].rearrange("p (a b) -> p a b", a=2), op=ALU.mult),
                              [sg.bufs[pb], pU.bufs[pb]], [gsb.b])

                def ffn_out(i):
                    s = i % 2
                    _, dst, b_, _, dbuf = blocks[i]
                    for t in range(2):
                        for h in range(2):
                            for kc in range(NKF):
                                tk.op("pe", lambda e, t=t, h=h, kc=kc: e.matmul(pY.t[:, h, :], lhsT=gsb.t[:, kc, t * 128:(t + 1) * 128],
                                                                                 rhs=wo.t[:, kc, h * 512:(h + 1) * 512], start=(kc == 0), stop=(kc == NKF - 1)),
                                      [gsb.b, wo.b], [pY.bufs[h]])
                            tk.op("dve", lambda e, h=h: e.tensor_tensor(out=tt.t[:, h, :], in0=pY.t[:, h, :], in1=gbs.t[:, b_, h * 512:(h + 1) * 512], op=ALU.mult),
                                  [pY.bufs[h], gbs.bufs[b_]], [tt.bufs[h]])
                            tk.op("pool", lambda e, t=t, h=h: e.tensor_tensor(out=xb.t[:, s, t, h * 512:(h + 1) * 512], in0=xb.t[:, s, t, h * 512:(h + 1) * 512],
                                                                               in1=tt.t[:, h, :], op=ALU.add), [tt.bufs[h], xb.bufs[s]], [xb.bufs[s]])
                    if final:
                        sbf = ss.bufs[s]
                        for t in range(2):
                            tk.op("act", lambda e, t=t: e.activation(out=junk.t[:], in_=xb.t[:, s, t, :], func=AF.Square,
                                                                      accum_out=ss.t[:, s, t:t + 1]), [xb.bufs[s]], [junk.b, sbf])
                        tk.op("dve", lambda e: e.tensor_scalar(out=ss.t[:, s, 2:4], in0=ss.t[:, s, 0:2], scalar1=1.0 / D, scalar2=EPS,
                                                               op0=ALU.mult, op1=ALU.add), [sbf], [sbf])
                        tk.op("act", lambda e: e.sqrt(out=ss.t[:, s, 4:6], in_=ss.t[:, s, 2:4]), [sbf], [sbf])
                        tk.op("dve", lambda e: e.reciprocal(out=ss.t[:, s, 6:8], in_=ss.t[:, s, 4:6]), [sbf], [sbf])
                        for t in range(2):
                            tk.op("dve", lambda e, t=t: e.scalar_tensor_tensor(out=xb.t[:, s, t, :], in0=xb.t[:, s, t, :], scalar=ss.t[:, s, 6 + t:7 + t],
                                                                               in1=fnb.t[:], op0=ALU.mult, op1=ALU.mult),
                                  [xb.bufs[s], sbf, fnb.b], [xb.bufs[s]])
                    tk.dma(dst.rearrange("(t p) d -> p t d", p=128), xb.t[:, s], reads=[xb.bufs[s]], writes=[dbuf])

                load(0)
                if nblk > 1:
                    load(1)
                norm(0)
                transp(0)
                for i in range(nblk):
                    ffn_in(i)
                    if i + 1 < nblk:
                        norm(i + 1)
                        transp(i + 1)
                    ffn_out(i)
                    if i + 2 < nblk:
                        load(i + 2)
                tk.barrier()

        blocks1 = [(ctx_d, x1_d[0:256, :], 1, Buf(), db("x1"))]
        for i in range(L // 256):
            blocks1.append((x_d[i * 256:(i + 1) * 256, :], x1_d[CL + i * 256:CL + (i + 1) * 256, :], 0, Buf(), db("x1")))
        if debug and "only_ctx" in debug:
            blocks1 = blocks1[:3]
        ffn_phase("f1", w_ffn1_in_d, w_ffn1_out_d, 0, 2, blocks1, False)


        def phase1b():
            with contextlib.ExitStack() as ph, scope("p1b"):
                wp = sb(ph, "wp", [128, 8, 2624], BF16)
                wq = sb(ph, "wq", [128, 6, 8, 96], BF16)
                wqs = sb(ph, "wqs", [128, 6, 8, 96], BF16)
                wkv = sb(ph, "wkv", [128, 2, 1024], BF16)
                wkp = sb(ph, "wkp", [128, 8, 2, 96], BF16)
                dtb = sb(ph, "dtb", [128, 16], F32)
                with contextlib.ExitStack() as st:
                    stg = sb(st, "stg1", [128, 3, IN_DIM], F32, nb=3)
                    kst = sb(st, "kst", [128, 8, 2, 32], F32)
                    wir = w_in_d.rearrange("(kc k) n -> k kc n", k=128)
                    ce = ["dve", "pool"]
                    n = 0
                    for kc in range(8):
                        s = n % 3
                        tk.dma(stg.t[:, s, :], wir[:, kc, :], writes=[stg.bufs[s]])
                        tk.op(ce[n % 2], lambda e, s=s, kc=kc: e.tensor_copy(out=wp.t[:, kc, 0:1552], in_=stg.t[:, s, 0:1552]), [stg.bufs[s]], [wp.b])
                        tk.op(ce[(n + 1) % 2], lambda e, s=s, kc=kc: e.tensor_copy(out=wp.t[:, kc, 1600:2624], in_=stg.t[:, s, 1552:2576]), [stg.bufs[s]], [wp.b])
                        n += 1
                    tk.op("pool", lambda e: e.memset(wkp.t[:], 0.0), [], [wkp.b])
                    tk.dma(kst.t[:, :, 0, :], wir[:, :, 2576:2608], writes=[kst.b])
                    for half in range(2):
                        o = 2576 + 16 * half
                        tk.dma(kst.t[:, :, 1, 16 * half:16 * half + 8], wir[:, :, o + 8:o + 16], writes=[kst.b])
                        tk.dma(kst.t[:, :, 1, 16 * half + 8:16 * half + 16], wir[:, :, o:o + 8], writes=[kst.b])
                    tk.op("dve", lambda e: e.tensor_copy(out=wkp.t[:, :, :, 64:96], in_=kst.t[:]), [kst.b], [wkp.b])
                    wur = w_uq_d.rearrange("(kc k) n -> k kc n", k=128)
                    for kc in range(6):
                        s = n % 3
                        tk.dma(stg.t[:, s, 0:768], wur[:, kc, :], writes=[stg.bufs[s]])
                        tk.op("dve", lambda e, s=s, kc=kc: e.tensor_scalar_mul(out=wq.t[:, kc].rearrange("p h d -> p (h d)"), in0=stg.t[:, s, 0:768],
                                                                               scalar1=cols.t[:, kc, 11:12]), [stg.bufs[s], cols.b], [wq.b])
                        n += 1
                    tk.op("pool", lambda e: e.tensor_copy(out=wqs.t[:], in_=wq.t[:]), [wq.b], [wqs.b])
                    for half in range(2):
                        o = 64 + 16 * half
                        tk.op("dve", lambda e, o=o: e.tensor_copy(out=wqs.t[:, :, :, o:o + 8], in_=wq.t[:, :, :, o + 8:o + 16]), [wq.b], [wqs.b])
                        tk.op("dve", lambda e, o=o: e.tensor_copy(out=wqs.t[:, :, :, o + 8:o + 16], in_=wq.t[:, :, :, o:o + 8]), [wq.b], [wqs.b])
                    wkr = w_ukv_d.rearrange("(kc k) n -> k kc n", k=128)
                    for kc in range(2):
                        s = n % 3
                        tk.dma(stg.t[:, s, 0:1024], wkr[:, kc, :], writes=[stg.bufs[s]])
                        src = stg.t[:, s, 0:1024].rearrange("p (h two d) -> p two h d", h=8, two=2)
                        tk.op("dve", lambda e, src=src, kc=kc: e.tensor_scalar_mul(out=wkv.t[:, kc, :].rearrange("p (two h d) -> p two h d", two=2, h=8), in0=src,
                                                                                  scalar1=cols.t[:, kc, 12:13]), [stg.bufs[s], cols.b], [wkv.b])
                        n += 1
                    tk.dma(dtb.t[:], bass.AP(dt_bias_d.tensor, 0, [[0, 128], [1, 16]]), writes=[dtb.b])
                    tk.barrier()
                if debug and 'stop0' in debug:
                    return
                NB = 512
                xb = sb(ph, "bxb", [128, 2, 4, D], F32, nb=2)
                xn = sb(ph, "bxn", [128, 4, D], BF16)
                hT = sb(ph, "bhT", [128, 2, 8, NB], BF16, nb=2)
                tmpm = sb(ph, "btmpm", [128, 8, 128], F32)
                ss = sb(ph, "bss", [128, 16], F32)
                zt = sb(ph, "bzt", [128, 4, 512], BF16)
                xo = sb(ph, "bxo", [128, 8, NB], BF16)
                cq = sb(ph, "bcq", [128, 6, NB], BF16)
                cqn = sb(ph, "bcqn", [128, 6, NB], BF16)
                ckv = sb(ph, "bckv", [128, 2, NB], BF16)
                ckvn = sb(ph, "bckvn", [128, 2, NB], BF16)
                sq = sb(ph, "bsq", [128, 8, NB], BF16, nb=2)
                rb = sb(ph, "brb", [128, 2, 2, NB], F32, nb=2)
                qo = sb(ph, "bqo", [96, 8, NB], BF16)
                ko = sb(ph, "bko", [128, 4, NB], BF16)
                kpo = sb(ph, "bkpo", [96, NB], BF16)
                vo = sb(ph, "bvo", [128, 4, 512], BF16)
                rp = sb(ph, "brp", [96, 2, NB], F32)
                rt = sb(ph, "brt", [96, 2, 2, NB], F32, nb=2)
                dtt = sb(ph, "bdtt", [128, 4, 16], F32)
                pA = ps(ph, "pA", [128, 4, 512], F32, nb=4)
                pT = ps(ph, "pT1", [128, 8, 128], BF16)
                pS = ps(ph, "pS", [128, 512], F32)
                pD = ps(ph, "pD", [128, 4, 16], F32)
                pai = [0]

                def nxt():
                    pai[0] = (pai[0] + 1) % 4
                    return pai[0]

                blocks = [(x1_d[0:CL, :], CL, 1, 0)] + [(x1_d[CL + i * NB:CL + (i + 1) * NB, :], NB, 0, CL + i * NB) for i in range(L // NB)]
                if debug and "only_ctx" in debug:
                    blocks = blocks[:2]
                def load_b(bi):
                    src_, N_, _, _ = blocks[bi]
                    tk.dma(xb.t[:, bi % 2, 0:N_ // 128], src_.rearrange("(t p) d -> p t d", p=128), reads=[db("x1")], writes=[xb.bufs[bi % 2]])

                def front(bi):
                    src, N, b_, tok0 = blocks[bi]
                    nt = N // 128
                    s = bi % 2
                    for t in range(nt):
                        tk.op("act", lambda e, t=t: e.activation(out=xn.t[:, 3, :], in_=xb.t[:, s, t, :], func=AF.Square, accum_out=ss.t[:, t:t + 1]),
                              [xb.bufs[s]], [xn.b, ss.b])
                    tk.op("dve", lambda e: e.tensor_scalar(out=ss.t[:, 4:8], in0=ss.t[:, 0:4], scalar1=1.0 / D, scalar2=EPS, op0=ALU.mult, op1=ALU.add), [ss.b], [ss.b])
                    tk.op("act", lambda e: e.sqrt(out=ss.t[:, 8:12], in_=ss.t[:, 4:8]), [ss.b], [ss.b])
                    tk.op("dve", lambda e: e.reciprocal(out=ss.t[:, 12:16], in_=ss.t[:, 8:12]), [ss.b], [ss.b])
                    G = modp.t[:, 1, b_, 0, :].unsqueeze(2).to_broadcast([128, 8, 128])
                    S = modp.t[:, 1, b_, 1, :].unsqueeze(2).to_broadcast([128, 8, 128])
                    for t in range(nt):
                        tk.op("dve", lambda e, t=t: e.tensor_scalar_mul(out=xn.t[:, t, :], in0=xb.t[:, s, t, :], scalar1=ss.t[:, 12 + t:13 + t]),
                              [xb.bufs[s], ss.b], [xn.b])
                        for kc in range(8):
                            tk.op("pe", lambda e, t=t, kc=kc: e.transpose(out=pT.t[:, kc, :], in_=xn.t[:, t, kc * 128:(kc + 1) * 128], identity=idb.t[:]),
                                  [xn.b, idb.b], [pT.b])
                        tk.op("dve", lambda e: e.tensor_tensor(out=tmpm.t[:], in0=pT.t[:], in1=G, op=ALU.mult), [pT.b, modp.b], [tmpm.b])
                        tk.op("pool", lambda e, t=t: e.tensor_tensor(out=hT.t[:, s, :, t * 128:(t + 1) * 128], in0=tmpm.t[:], in1=S, op=ALU.add),
                              [tmpm.b, modp.b], [hT.bufs[s]])

                def fm(N, lhs_fn, nk, rhs_fn, M, rd):
                    a = nxt()
                    for kc in range(nk):
                        tk.op("pe", lambda e, kc=kc: e.matmul(pA.t[0:M, a, 0:N], lhsT=lhs_fn(kc), rhs=rhs_fn(kc), start=(kc == 0), stop=(kc == nk - 1)),
                              rd, [pA.bufs[a]])
                    return a

                CQ = ((cq, cqn, 6, 1600, 768.0, 0), (ckv, ckvn, 2, 2368, 256.0, 1))

                def indep(bi):
                    src, N, b_, tok0 = blocks[bi]
                    nt = N // 128
                    s = bi % 2
                    hb_ = hT.bufs[s]
                    if b_ == 0:
                        tk.dma(rp.t[64:96, :, 0:N], rope_d[:, :, tok0 - CL:tok0 - CL + N].rearrange("c d n -> d c n"), writes=[rp.b])
                        for t in range(nt):
                            a = fm(N, lambda kc, t=t: hT.t[:, s, kc, t * 128:(t + 1) * 128], 8, lambda kc: wp.t[:, kc, 0:512], 128, [hb_, wp.b])
                            tk.op("act", lambda e, t=t, a=a: e.copy(out=zt.t[:, t, :], in_=pA.t[:, a, :]), [pA.bufs[a]], [zt.b])
                        r0 = tok0 - CL
                        tk.dma(z_d[r0:r0 + N, :].rearrange("(t p) d -> p t d", p=128), zt.t[:, 0:nt], reads=[zt.b], writes=[db("z")])
                    for cc in range(8):
                        a = fm(N, lambda kc, cc=cc: wp.t[:, kc, 512 + cc * 128:512 + (cc + 1) * 128], 8, lambda kc: hT.t[:, s, kc, 0:N], 128, [hb_, wp.b])
                        tk.op("act" if cc % 2 else "dve",
                              (lambda e, cc=cc, a=a: e.copy(out=xo.t[:, cc, 0:N], in_=pA.t[:, a, 0:N])) if cc % 2 else
                              (lambda e, cc=cc, a=a: e.tensor_copy(out=xo.t[:, cc, 0:N], in_=pA.t[:, a, 0:N])), [pA.bufs[a]], [xo.b])
                    tk.dma(xbcT_d[:, :, tok0:tok0 + N].rearrange("c p n -> p c n"), xo.t[:, :, 0:N], reads=[xo.b], writes=[db("xbcT")])
                    for t in range(nt):
                        for kc in range(8):
                            tk.op("pe", lambda e, t=t, kc=kc: e.matmul(pD.t[:, t, :], lhsT=hT.t[:, s, kc, t * 128:(t + 1) * 128], rhs=wp.t[:, kc, 1536:1552],
                                                                       start=(kc == 0), stop=(kc == 7)), [hb_, wp.b], [pD.b])
                    tg = tok0 // 128
                    tk.op("dve", lambda e: e.tensor_tensor(out=dtt.t[:, 0:nt, :], in0=pD.t[:, 0:nt, :], in1=dtb.t[:].unsqueeze(1).to_broadcast([128, nt, 16]), op=ALU.add),
                          [pD.b, dtb.b], [dtt.b])
                    tk.op("act", lambda e: e.activation(out=dtt.t[:, 0:nt, :], in_=dtt.t[:, 0:nt, :], func=AF.Exp), [dtt.b], [dtt.b])
                    tk.op("act", lambda e: e.activation(out=dtv.t[:, tg:tg + nt, :], in_=dtt.t[:, 0:nt, :], func=AF.Ln, bias=1.0), [dtt.b], [dtv.b])
                    for (dst, dstn, nk, c0, dim, ri) in CQ:
                        for c_ in range(nk):
                            a = fm(N, lambda kc, c_=c_: wp.t[:, kc, c0 + c_ * 128:c0 + (c_ + 1) * 128], 8, lambda kc: hT.t[:, s, kc, 0:N], 128, [hb_, wp.b])
                            tk.op("dve" if c_ % 2 else "act",
                                  (lambda e, c_=c_, a=a: e.tensor_copy(out=dst.t[:, c_, 0:N], in_=pA.t[:, a, 0:N])) if c_ % 2 else
                                  (lambda e, c_=c_, a=a: e.copy(out=dst.t[:, c_, 0:N], in_=pA.t[:, a, 0:N])), [pA.bufs[a]], [dst.b])
                            tk.op("pool", lambda e, c_=c_: e.tensor_tensor(out=sq.t[:, 6 * ri + c_, 0:N], in0=dst.t[:, c_, 0:N], in1=dst.t[:, c_, 0:N], op=ALU.mult), [dst.b], [sq.bufs[ri]])
                    a1 = fm(N, lambda kc: wkp.t[:, kc, 0, :], 8, lambda kc: hT.t[:, s, kc, 0:N], 96, [hb_, wkp.b])
                    if b_ == 0:
                        a2 = fm(N, lambda kc: wkp.t[:, kc, 1, :], 8, lambda kc: hT.t[:, s, kc, 0:N], 96, [hb_, wkp.b])
                        tk.op("dve", lambda e: e.tensor_tensor(out=rt.t[64:96, 0, 0, 0:N], in0=pA.t[64:96, a1, 0:N], in1=rp.t[64:96, 0, 0:N], op=ALU.mult),
                              [pA.bufs[a1], rp.b], [rt.bufs[0]])
                        tk.op("dve", lambda e: e.tensor_tensor(out=rt.t[64:96, 0, 1, 0:N], in0=pA.t[64:96, a2, 0:N], in1=rp.t[64:96, 1, 0:N], op=ALU.mult),
                              [pA.bufs[a2], rp.b], [rt.bufs[0]])
                        tk.op("pool", lambda e: e.tensor_tensor(out=kpo.t[64:96, 0:N], in0=rt.t[64:96, 0, 0, 0:N], in1=rt.t[64:96, 0, 1, 0:N], op=ALU.add),
                              [rt.bufs[0]], [kpo.b])
                    else:
                        tk.op("dve", lambda e: e.tensor_copy(out=kpo.t[64:96, 0:N], in_=pA.t[64:96, a1, 0:N]), [pA.bufs[a1]], [kpo.b])
                    for h in range(8):
                        tk.dma(kT_d[h, 64:96, tok0:tok0 + N], kpo.t[64:96, 0:N], reads=[kpo.b], writes=[db("kT")])

                def dep(bi):
                    src, N, b_, tok0 = blocks[bi]
                    nt = N // 128
                    for (dst, dstn, nk, c0, dim, ri) in CQ:
                        for c_ in range(nk):
                            tk.op("pe", lambda e, c_=c_: e.matmul(pS.t[:, 0:N], lhsT=onb.t[:], rhs=sq.t[:, 6 * ri + c_, 0:N], start=(c_ == 0), stop=(c_ == nk - 1)),
                                  [onb.b, sq.bufs[ri]], [pS.b])
                        tk.op("dve", lambda e: e.tensor_scalar(out=rb.t[:, ri, 0, 0:N], in0=pS.t[:, 0:N], scalar1=1.0 / dim, scalar2=EPS, op0=ALU.mult, op1=ALU.add),
                              [pS.b], [rb.bufs[ri]])
                        tk.op("act", lambda e: e.sqrt(out=rb.t[:, ri, 0, 0:N], in_=rb.t[:, ri, 0, 0:N]), [rb.bufs[ri]], [rb.bufs[ri]])
                        tk.op("dve", lambda e: e.reciprocal(out=rb.t[:, ri, 1, 0:N], in_=rb.t[:, ri, 0, 0:N]), [rb.bufs[ri]], [rb.bufs[ri]])
                        tk.op("dve" if ri else "pool", lambda e: e.tensor_tensor(out=dstn.t[:, :, 0:N], in0=dst.t[:, :, 0:N],
                                                                                  in1=rb.t[:, ri, 1, 0:N].unsqueeze(1).to_broadcast([128, nk, N]), op=ALU.mult),
                              [dst.b, rb.bufs[ri]], [dstn.b])
                    for hp in range(4):
                        a = fm(N, lambda kc, hp=hp: wkv.t[:, kc, hp * 128:(hp + 1) * 128], 2, lambda kc: ckvn.t[:, kc, 0:N], 128, [ckvn.b, wkv.b])
                        tk.op("act", lambda e, hp=hp, a=a: e.copy(out=ko.t[:, hp, 0:N], in_=pA.t[:, a, 0:N]), [pA.bufs[a]], [ko.b])
                    for h in range(8):
                        tk.dma(kT_d[h, 0:64, tok0:tok0 + N], ko.t[(h % 2) * 64:(h % 2) * 64 + 64, h // 2, 0:N], reads=[ko.b], writes=[db("kT")])
                    for t in range(nt):
                        a = nxt()
                        for kc in range(2):
                            tk.op("pe", lambda e, t=t, kc=kc, a=a: e.matmul(pA.t[:, a, :], lhsT=ckvn.t[:, kc, t * 128:(t + 1) * 128], rhs=wkv.t[:, kc, 512:1024],
                                                                            start=(kc == 0), stop=(kc == 1)), [ckvn.b, wkv.b], [pA.bufs[a]])
                        tk.op("dve", lambda e, t=t, a=a: e.tensor_copy(out=vo.t[:, t, :], in_=pA.t[:, a, :]), [pA.bufs[a]], [vo.b])
                    tk.dma(V_d[tok0:tok0 + N, :].rearrange("(t p) d -> p t d", p=128), vo.t[:, 0:nt], reads=[vo.b], writes=[db("V")])
                    if b_ == 0:
                        for h in range(8):
                            a1 = fm(N, lambda kc, h=h: wq.t[:, kc, h, :], 6, lambda kc: cqn.t[:, kc, 0:N], 96, [cqn.b, wq.b])
                            a2 = fm(N, lambda kc, h=h: wqs.t[:, kc, h, :], 6, lambda kc: cqn.t[:, kc, 0:N], 96, [cqn.b, wqs.b])
                            tk.op("dve", lambda e, h=h, a1=a1: e.tensor_copy(out=qo.t[0:64, h, 0:N], in_=pA.t[0:64, a1, 0:N]), [pA.bufs[a1]], [qo.b])
                            r = h % 2
                            tk.op("dve", lambda e, a1=a1, r=r: e.tensor_tensor(out=rt.t[64:96, r, 0, 0:N], in0=pA.t[64:96, a1, 0:N], in1=rp.t[64:96, 0, 0:N], op=ALU.mult),
                                  [pA.bufs[a1], rp.b], [rt.bufs[r]])
                            tk.op("dve", lambda e, a2=a2, r=r: e.tensor_tensor(out=rt.t[64:96, r, 1, 0:N], in0=pA.t[64:96, a2, 0:N], in1=rp.t[64:96, 1, 0:N], op=ALU.mult),
                                  [pA.bufs[a2], rp.b], [rt.bufs[r]])
                            tk.op("pool", lambda e, h=h, r=r: e.tensor_tensor(out=qo.t[64:96, h, 0:N], in0=rt.t[64:96, r, 0, 0:N], in1=rt.t[64:96, r, 1, 0:N], op=ALU.add),
                                  [rt.bufs[r]], [qo.b])
                        r0 = tok0 - CL
                        tk.dma(qT_d[:, :, r0:r0 + N].rearrange("h d n -> d h n"), qo.t[:, :, 0:N], reads=[qo.b], writes=[db("qT")])

                load_b(0)
                front(0)
                for bi in range(len(blocks)):
                    if bi + 1 < len(blocks):
                        load_b(bi + 1)
                    indep(bi)
                    if bi + 1 < len(blocks):
                        front(bi + 1)
                    dep(bi)
                tk.barrier()

        z_d = dscr("z", [L, 512], BF16)
        xbcT_d = dscr("xbcT", [8, 128, T], BF16)
        kT_d = dscr("kT", [8, 96, T], BF16)
        qT_d = dscr("qT", [8, 96, L], BF16)
        V_d = dscr("V", [T, 512], BF16)
        mid = contextlib.ExitStack()
        dtv = sb(mid, "dtv", [128, NT, 16], F32)
        onb = sb(mid, "onb", [128, 128], BF16)
        tk.op("dve", lambda e: e.tensor_copy(out=onb.t[:], in_=cst.t[:, C_ONE:C_ONE + 128]), [cst.b], [onb.b])
        dtv_d = dscr("dtv_dbg", [128, NT, 16], F32)
        phase1b()
        if debug and "dtv_dbg" in debug:
            tk.dma(dtv_d, dtv.t[:], reads=[dtv.b], writes=[db("dtvd")])


        mixT_d = dscr("mixT", [8, 128, L], BF16)

        def phase2b():
            with contextlib.ExitStack() as ph, scope("p2b"):
                kT = sb(ph, "akT", [128, 2, T], BF16, nb=2)
                qT = sb(ph, "aqT", [128, 2, L], BF16, nb=2)
                Vh = sb(ph, "aV", [128, 2, NT, 128], BF16, nb=2)
                P = sb(ph, "aP", [128, 3, 2, 512], BF16, nb=3)
                oc = sb(ph, "aoc", [65, 2, 512], F32, nb=2)
                rc = sb(ph, "arc", [65, 2, 512], F32, nb=2)
                bc = sb(ph, "abc", [64, 2, 512], F32, nb=2)
                on = sb(ph, "aon", [64, 2, 512], BF16, nb=2)
                pS_ = ps(ph, "aS", [128, 3, 2, 512], F32, nb=3)
                pO = ps(ph, "aO", [128, 512], F32)
                pB = ps(ph, "aB", [64, 512], F32)
                for s in range(2):
                    tk.op("pool", lambda e, s=s: e.memset(Vh.t[:, s], 0.0), [], [Vh.bufs[s]])
                    tk.op("pool", lambda e, s=s: e.memset(Vh.t[:, s, :, 64:65], 1.0), [], [Vh.bufs[s]])
                    tk.op("dve", lambda e, s=s: e.memset(kT.t[:, s, :], 0.0), [], [kT.bufs[s]])
                    tk.op("dve", lambda e, s=s: e.memset(qT.t[:, s, :], 0.0), [], [qT.bufs[s]])
                Vr = V_d.rearrange("(t p) (h d) -> p t h d", p=128, h=8)
                nh = 8
                nqb = 1 if (debug and "only_ctx" in debug) else 8
                NP = NT // 2

                def load_h(h):
                    s = h % 2
                    tk.dma(kT.t[0:96, s, :], kT_d[h], reads=[db("kT")], writes=[kT.bufs[s]])
                    tk.dma(qT.t[0:96, s, :], qT_d[h], reads=[db("qT")], writes=[qT.bufs[s]])
                    tk.dma(Vh.t[:, s, :, 0:64], Vr[:, :, h, :], reads=[db("V")], writes=[Vh.bufs[s]])

                units = [(h, qb) for h in range(nh) for qb in range(nqb)]
                gi = [0]

                def S(h, qb, kp, g):
                    s = h % 2
                    sl = g % 3
                    for j in range(2):
                        kt = kp * 2 + j
                        tk.op("pe", lambda e, j=j, kt=kt: e.matmul(pS_.t[:, sl, j, :], lhsT=kT.t[:, s, kt * 128:(kt + 1) * 128],
                                                                   rhs=qT.t[:, s, qb * 512:(qb + 1) * 512], start=True, stop=True),
                              [kT.bufs[s], qT.bufs[s]], [pS_.bufs[sl]])

                items = [(u, kp) for u in range(len(units)) for kp in range(NP)]
                pending = []
                load_h(0)
                for n_ahead in range(min(2, len(items))):
                    u, kp = items[n_ahead]
                    S(units[u][0], units[u][1], kp, n_ahead)
                for n, (u, kp) in enumerate(items):
                    h, qb = units[u]
                    s = h % 2
                    o = u % 2
                    if kp == 0 and qb == 0 and h + 1 < nh:
                        load_h(h + 1)
                    if n + 2 < len(items):
                        u2, kp2 = items[n + 2]
                        S(units[u2][0], units[u2][1], kp2, n + 2)
                    sl = n % 3
                    tk.op("act", lambda e: e.activation(out=P.t[:, sl], in_=pS_.t[:, sl], func=AF.Exp, scale=SCALE), [pS_.bufs[sl]], [P.bufs[sl]])
                    for j in range(2):
                        kt = kp * 2 + j
                        tk.op("pe", lambda e, j=j, kt=kt: e.matmul(pO.t[:], lhsT=Vh.t[:, s, kt, :], rhs=P.t[:, sl, j, :],
                                                                   start=(kt == 0), stop=(kt == NT - 1)), [Vh.bufs[s], P.bufs[sl]], [pO.b])
                    for fn_ in pending:
                        fn_()
                    pending.clear()
                    if kp == NP - 1:
                        tk.op("dve", lambda e: e.tensor_copy(out=oc.t[:, o, :], in_=pO.t[0:65, :]), [pO.b], [oc.bufs[o]])
                        tk.op("dve", lambda e: e.reciprocal(out=rc.t[64:65, o, :], in_=oc.t[64:65, o, :]), [oc.bufs[o]], [rc.bufs[o]])

                        def epi(o=o, h=h, qb=qb):
                            tk.op("pe", lambda e: e.matmul(pB.t[:], lhsT=cst.t[64:65, C_ONE:C_ONE + 64], rhs=rc.t[64:65, o, :], start=True, stop=True),
                                  [cst.b, rc.bufs[o]], [pB.b])
                            tk.op("dve", lambda e: e.tensor_copy(out=bc.t[:, o, :], in_=pB.t[:]), [pB.b], [bc.bufs[o]])
                            tk.op("pool", lambda e: e.tensor_tensor(out=on.t[:, o, :], in0=oc.t[0:64, o, :], in1=bc.t[:, o, :], op=ALU.mult),
                                  [oc.bufs[o], bc.bufs[o]], [on.bufs[o]])
                            tk.dma(mixT_d[4 + h // 2, (h % 2) * 64:(h % 2) * 64 + 64, qb * 512:(qb + 1) * 512], on.t[:, o, :], reads=[on.bufs[o]], writes=[db("mixT")])
                        pending.append(epi)
                for fn_ in pending:
                    fn_()
                tk.barrier()

        def phase2c():
            with contextlib.ExitStack() as ph, scope("p2c"):
                wo = sb(ph, "cwo", [128, 8, D], BF16)
                stg = sb(ph, "cstg", [128, 2, D], F32, nb=2)
                gb = sb(ph, "cgb", [128, D], F32)
                mx = sb(ph, "cmx", [128, 2, 8, 256], BF16, nb=2)
                xb = sb(ph, "cxb", [128, 2, 2, D], F32, nb=2)
                tt = sb(ph, "ctt", [128, 2, 512], F32, nb=2)
                pY = ps(ph, "cpY", [128, 2, 512], F32, nb=2)
                wor = w_out_d.rearrange("(kc k) n -> k kc n", k=128)
                for kc in range(8):
                    s = kc % 2
                    tk.dma(stg.t[:, s, :], wor[:, kc, :], writes=[stg.bufs[s]])
                    if kc < 4:
                        tk.op("dve", lambda e, s=s, kc=kc: e.tensor_scalar_mul(out=wo.t[:, kc, :], in0=stg.t[:, s, :], scalar1=cols.t[:, kc, 13:14]),
                              [stg.bufs[s], cols.b], [wo.b])
                    else:
                        tk.op("dve", lambda e, s=s, kc=kc: e.tensor_copy(out=wo.t[:, kc, :], in_=stg.t[:, s, :]), [stg.bufs[s]], [wo.b])
                src = mrow_d[0:1, 5 * D:6 * D]
                tk.dma(gb.t[:], bass.AP(src.tensor, src.offset, [[0, 128], [1, D]]), reads=[db("mrow")], writes=[gb.b])
                nblk = 2 if (debug and "only_ctx" in debug) else L // 256
                def rows_of(bi):
                    return x1_d[CL + bi * 256:CL + (bi + 1) * 256, :].rearrange("(t p) d -> p t d", p=128)

                def load_c(bi):
                    s = bi % 2
                    tk.dma(mx.t[:, s], mixT_d[:, :, bi * 256:(bi + 1) * 256].rearrange("c p n -> p c n"), reads=[db("mixT")], writes=[mx.bufs[s]])
                    tk.dma(xb.t[:, s], rows_of(bi), reads=[Buf()], writes=[xb.bufs[s]])

                load_c(0)
                for bi in range(nblk):
                    s = bi % 2
                    rows = rows_of(bi)
                    if bi + 1 < nblk:
                        load_c(bi + 1)
                    for t in range(2):
                        for hf in range(2):
                            for kc in range(8):
                                tk.op("pe", lambda e, t=t, hf=hf, kc=kc: e.matmul(pY.t[:, hf, :], lhsT=mx.t[:, s, kc, t * 128:(t + 1) * 128],
                                                                                  rhs=wo.t[:, kc, hf * 512:(hf + 1) * 512], start=(kc == 0), stop=(kc == 7)),
                                      [mx.bufs[s], wo.b], [pY.bufs[hf]])
                            tk.op("dve", lambda e, hf=hf: e.tensor_tensor(out=tt.t[:, hf, :], in0=pY.t[:, hf, :], in1=gb.t[:, hf * 512:(hf + 1) * 512], op=ALU.mult),
                                  [pY.bufs[hf], gb.b], [tt.bufs[hf]])
                            tk.op("pool", lambda e, t=t, hf=hf: e.tensor_tensor(out=xb.t[:, s, t, hf * 512:(hf + 1) * 512], in0=xb.t[:, s, t, hf * 512:(hf + 1) * 512],
                                                                                in1=tt.t[:, hf, :], op=ALU.add), [tt.bufs[hf], xb.bufs[s]], [xb.bufs[s]])
                    tk.dma(rows, xb.t[:, s], reads=[xb.bufs[s]], writes=[db("x1")])
                tk.barrier()


        yf_d = dscr("yf", [L, 512], F32)
        hf_d = dscr("hf_dbg", [2, 128, 512], F32)
        ecw_d = dscr("ecw_dbg", [128, 40], F32)
        xdtd_d = dscr("xdtd_dbg", [128, 512], BF16)
        xs_dbg = dscr("xs_dbg", [128, NT, 512], BF16)
        bmt_dbg = dscr("bmt_dbg", [128, NT, 256], BF16)

        def phase2a():
            with contextlib.ExitStack() as ph, scope("p2a"):
                xs = sb(ph, "sxs", [128, NT, 512], BF16)
                bmt = sb(ph, "sbmt", [128, NT, 256], BF16)
                bmT = sb(ph, "sbmT", [128, 2, T], BF16)
                cmT = sb(ph, "scmT", [128, 2, T], BF16)
                aall = sb(ph, "saall", [128, NT, 16], F32)
                anb = sb(ph, "sanb", [128, 16], F32)
                dsk = sb(ph, "sdsk", [128, 8], F32)
                tk.dma(anb.t[:], bass.AP(a_log_d.tensor, 0, [[0, 128], [1, 16]]), writes=[anb.b])
                tk.dma(dsk.t[:], bass.AP(d_skip_d.tensor, 0, [[0, 128], [1, 8]]), writes=[dsk.b])
                tk.op("act", lambda e: e.activation(out=anb.t[:], in_=anb.t[:], func=AF.Exp), [anb.b], [anb.b])
                tk.op("dve", lambda e: e.tensor_scalar_mul(out=anb.t[:], in0=anb.t[:], scalar1=-1.0), [anb.b], [anb.b])
                tk.op("dve", lambda e: e.tensor_tensor(out=aall.t[:], in0=dtv.t[:], in1=anb.t[:].unsqueeze(1).to_broadcast([128, NT, 16]), op=ALU.mult),
                      [dtv.b, anb.b], [aall.b])
                if debug and 'cv0' in debug:
                    tk.barrier()
                    return
                with contextlib.ExitStack() as cv:
                    xr = sb(cv, "sxr", [128, 2, T], BF16, nb=2)
                    acc = sb(cv, "sacc", [128, T], F32)
                    co = sb(cv, "sco", [128, 2, T], BF16, nb=2)
                    pT = ps(cv, "spT", [128, 2, 8, 128], BF16, nb=2)
                    ng = 0
                    for cc in range(8):
                        s = cc % 2
                        tk.dma(xr.t[:, s, :], xbcT_d[cc], reads=[db("xbcT")], writes=[xr.bufs[s]])
                        tk.op("dve", lambda e, cc=cc: e.tensor_scalar(out=acc.t[:], in0=xr.t[:, s, :], scalar1=cols.t[:, cc, 7:8], scalar2=cols.t[:, cc, 10:11],
                                                                      op0=ALU.mult, op1=ALU.add), [xr.bufs[s], cols.b], [acc.b])
                        if debug and 'cv1' in debug:
                            continue
                        for (lo, hi) in ((0, CL), (CL, T)):
                            for j in (0, 1, 3, 4):
                                sh = j - 2
                                o0 = lo + max(0, -sh)
                                o1 = hi - max(0, sh)
                                tk.op("dve", lambda e, cc=cc, j=j, o0=o0, o1=o1, sh=sh: e.scalar_tensor_tensor(
                                    out=acc.t[:, o0:o1], in0=xr.t[:, s, o0 + sh:o1 + sh], scalar=cols.t[:, cc, 5 + j:6 + j], in1=acc.t[:, o0:o1],
                                    op0=ALU.mult, op1=ALU.add), [xr.bufs[s], cols.b, acc.b], [acc.b])
                        if debug and 'cv2' in debug:
                            continue
                        if cc < 4:
                            dest, dbuf = co.t[:, s, :], co.bufs[s]
                        elif cc < 6:
                            dest, dbuf = bmT.t[:, cc - 4, :], bmT.b
                        else:
                            dest, dbuf = cmT.t[:, cc - 6, :], cmT.b
                        tk.op("act", lambda e, dest=dest: e.activation(out=dest, in_=acc.t[:], func=AF.Silu), [acc.b], [dbuf])
                        if debug and 'cv3' in debug:
                            continue
                        if cc < 6:
                            for tg in range(0, NT, 4):
                                n = min(4, NT - tg)
                                sl = ng % 2
                                ng += 1
                                for k in range(n):
                                    t = tg + k
                                    tk.op("pe", lambda e, k=k, t=t, dest=dest: e.transpose(out=pT.t[:, sl, k, :], in_=dest[:, t * 128:(t + 1) * 128], identity=idb.t[:]),
                                          [dbuf, idb.b], [pT.bufs[sl]])
                                if cc < 4:
                                    tk.op("dve" if ng % 2 else "pool" if False else "dve", lambda e, cc=cc, tg=tg, n=n: e.tensor_copy(out=xs.t[:, tg:tg + n, cc * 128:(cc + 1) * 128], in_=pT.t[:, sl, 0:n, :]),
                                          [pT.bufs[sl]], [xs.b])
                                else:
                                    tk.op("dve", lambda e, cc=cc, tg=tg, n=n: e.tensor_copy(out=bmt.t[:, tg:tg + n, (cc - 4) * 128:(cc - 3) * 128], in_=pT.t[:, sl, 0:n, :]),
                                          [pT.bufs[sl]], [bmt.b])
                    tk.barrier()
                if debug and 'xs_dbg' in debug:
                    tk.dma(xs_dbg, xs.t[:], reads=[xs.b], writes=[db('xsd')])
                    tk.dma(bmt_dbg, bmt.t[:], reads=[bmt.b], writes=[db('bmd')])
                if debug and 'conv_only' in debug:
                    return
                with contextlib.ExitStack() as sw:
                    hst = sb(sw, "shst", [128, 512], F32)
                    hbf = sb(sw, "shbf", [128, 2, 512], BF16, nb=2)
                    ec = sb(sw, "sec", [128, 4, 24], F32, nb=4)
                    aexp = sb(sw, "saexp", [128, 2, 8, 128], F32, nb=2)
                    eseg = sb(sw, "seseg", [128, 3, 8, 128], F32, nb=3)
                    Mt = sb(sw, "sM", [128, 2, 8, 128], BF16, nb=2)
                    cbm = sb(sw, "scbm", [128, 2, 2, 128], F32, nb=2)
                    w8 = sb(sw, "sw8", [128, 2, 8], F32, nb=2)
                    xdt = sb(sw, "sxdt", [128, 3, 8, 64], BF16, nb=3)
                    xdtd = sb(sw, "sxdtd", [128, 3, 8, 64], BF16, nb=3)
                    t1 = sb(sw, "st1", [128, 8, 64], F32)
                    yv = sb(sw, "syv", [128, 2, 512], F32, nb=2)
                    yfi = sb(sw, "syfi", [128, 2, 512], F32, nb=2)
                    zt = sb(sw, "szt", [128, 2, 512], BF16, nb=2)
                    sz = sb(sw, "ssz", [128, 512], F32)
                    sq2 = sb(sw, "ssq2", [128, 512], F32)
                    ss = sb(sw, "sss", [128, 8], F32)
                    ob = sb(sw, "sob", [128, 512], BF16)
                    obT = sb(sw, "sobT", [128, 2, 4, 128], BF16, nb=2)
                    pC = ps(sw, "spC", [128, 512], F32)
                    pSeg = ps(sw, "spSeg", [128, 2, 512], F32)
                    pCB = ps(sw, "spCB", [128, 4, 128], F32)
                    pY = ps(sw, "spY", [128, 512], F32)
                    pYo = ps(sw, "spYo", [128, 512], F32)
                    pSt = ps(sw, "spSt", [128, 512], F32)
                    pTo = ps(sw, "spTo", [128, 8, 128], BF16)
                    nx = 4 if (debug and "only_ctx" in debug) else NT - 2
                    for d in range(2):
                        if d == 0:
                            order = list(range(2 + nx))
                            m_incl, m_excl, m_seg = C_LE, C_GT, C_GT
                        else:
                            order = [1, 0] + list(range(2 + nx - 1, 1, -1))
                            m_incl, m_excl, m_seg = C_GE, C_LT, C_LT
                        tk.op("dve", lambda e: e.memset(hst.t[:], 0.0), [], [hst.b])
                        tk.op("dve", lambda e: e.memset(hbf.t[:, 0, :], 0.0), [], [hbf.bufs[0]])

                        def F0(i):
                            c = order[i]
                            a = aall.t[:, c, d * 8:(d + 1) * 8]
                            for q_, msk in enumerate((m_incl, m_excl, C_ONE)):
                                tk.op("pe", lambda e, q_=q_, msk=msk: e.matmul(pC.t[:, q_ * 8:(q_ + 1) * 8], lhsT=cst.t[:, msk:msk + 128], rhs=a, start=True, stop=True),
                                      [cst.b, aall.b], [pC.b])
                            tk.op("act", lambda e: e.activation(out=ec.t[:, i % 4, :], in_=pC.t[:, 0:24], func=AF.Exp), [pC.b], [ec.bufs[i % 4]])
                            if c < 2:
                                return
                            ae = aexp.t[:, i % 2]
                            tk.op("dve", lambda e: e.tensor_tensor(out=ae, in0=cst.t[:, m_incl:m_incl + 128].unsqueeze(1).to_broadcast([128, 8, 128]),
                                                                   in1=a.unsqueeze(2).to_broadcast([128, 8, 128]), op=ALU.mult), [cst.b, aall.b], [aexp.bufs[i % 2]])
                            for hh in range(2):
                                tk.op("pe", lambda e, hh=hh: e.matmul(pSeg.t[:, hh, :], lhsT=cst.t[:, m_seg:m_seg + 128],
                                                                      rhs=ae[:, hh * 4:(hh + 1) * 4, :].rearrange("p h i -> p (h i)"), start=True, stop=True),
                                      [cst.b, aexp.bufs[i % 2]], [pSeg.b])
                            tk.op("act", lambda e: e.activation(out=eseg.t[:, i % 3].rearrange("p h i -> p (h i)"), in_=pSeg.t[:].rearrange("p a n -> p (a n)"), func=AF.Exp),
                                  [pSeg.b], [eseg.bufs[i % 3]])
                            for g in range(2):
                                tk.op("pe", lambda e, g=g: e.matmul(pCB.t[:, g, :], lhsT=bmT.t[:, g, c * 128:(c + 1) * 128], rhs=cmT.t[:, g, c * 128:(c + 1) * 128],
                                                                    start=True, stop=True), [bmT.b, cmT.b], [pCB.b])

                        def F1(i):
                            c = order[i]
                            dtc = dtv.t[:, c, d * 8:(d + 1) * 8]
                            ecs = ec.t[:, i % 4, :]
                            w8s = w8.t[:, i % 2, :]
                            tk.op("dve", lambda e: e.tensor_tensor(out=w8s, in0=dtc, in1=ecs[:, 8:16], op=ALU.mult), [dtv.b, ec.bufs[i % 4]], [w8.bufs[i % 2]])
                            xsc = xs.t[:, c, :].rearrange("p (h d) -> p h d", h=8)
                            tk.op("pool", lambda e: e.tensor_tensor(out=xdtd.t[:, i % 3], in0=xsc, in1=w8s.unsqueeze(2).to_broadcast([128, 8, 64]), op=ALU.mult),
                                  [xs.b, w8.bufs[i % 2]], [xdtd.bufs[i % 3]])
                            if c < 2:
                                return
                            tk.op("dve", lambda e: e.tensor_tensor(out=cbm.t[:, i % 2], in0=pCB.t[:, 0:2, :], in1=cst.t[:, m_incl:m_incl + 128].unsqueeze(1).to_broadcast([128, 2, 128]), op=ALU.mult),
                                  [pCB.b, cst.b], [cbm.bufs[i % 2]])
                            tk.op("pool", lambda e: e.tensor_tensor(out=xdt.t[:, i % 3], in0=xsc, in1=dtc.unsqueeze(2).to_broadcast([128, 8, 64]), op=ALU.mult),
                                  [xs.b, dtv.b], [xdt.bufs[i % 3]])

                        def F2(i):
                            c = order[i]
                            if c < 2:
                                return
                            tk.op("dve", lambda e: e.tensor_tensor(out=Mt.t[:, i % 2].rearrange("p (g h) i -> p g h i", g=2), in0=eseg.t[:, i % 3].rearrange("p (g h) i -> p g h i", g=2),
                                                                   in1=cbm.t[:, i % 2].unsqueeze(2).to_broadcast([128, 2, 4, 128]), op=ALU.mult),
                                  [eseg.bufs[i % 3], cbm.bufs[i % 2]], [Mt.bufs[i % 2]])

                        def back(i):
                            c = order[i]
                            s_ = i % 2
                            ecs = ec.t[:, i % 4, :]
                            xsc = xs.t[:, c, :].rearrange("p (h d) -> p h d", h=8)
                            for g in range(2):
                                tk.op("pe", lambda e, g=g: e.matmul(pSt.t[:, g * 256:(g + 1) * 256], lhsT=bmt.t[:, c, g * 128:(g + 1) * 128],
                                                                    rhs=xdtd.t[:, i % 3, g * 4:(g + 1) * 4, :].rearrange("p h d -> p (h d)"), start=True, stop=True),
                                      [bmt.b, xdtd.bufs[i % 3]], [pSt.b])
                            tk.op("dve", lambda e: e.tensor_tensor(out=hst.t[:].rearrange("p (h d) -> p h d", h=8), in0=hst.t[:].rearrange("p (h d) -> p h d", h=8),
                                                                   in1=ecs[:, 16:24].unsqueeze(2).to_broadcast([128, 8, 64]), op=ALU.mult), [hst.b, ec.bufs[i % 4]], [hst.b])
                            tk.op("dve", lambda e: e.tensor_tensor(out=hst.t[:], in0=hst.t[:], in1=pSt.t[:], op=ALU.add), [hst.b, pSt.b], [hst.b])
                            tk.op("pool", lambda e: e.tensor_copy(out=hbf.t[:, 1 - s_, :], in_=hst.t[:]), [hst.b], [hbf.bufs[1 - s_]])
                            if c < 2:
                                return
                            for h in range(8):
                                tk.op("pe", lambda e, h=h: e.matmul(pY.t[:, h * 64:(h + 1) * 64], lhsT=Mt.t[:, s_, h, :], rhs=xdt.t[:, i % 3, h, :], start=True, stop=True),
                                      [Mt.bufs[s_], xdt.bufs[i % 3]], [pY.b])
                            for g in range(2):
                                tk.op("pe", lambda e, g=g: e.matmul(pYo.t[:, g * 256:(g + 1) * 256], lhsT=cmT.t[:, g, c * 128:(c + 1) * 128], rhs=hbf.t[:, s_, g * 256:(g + 1) * 256],
                                                                    start=True, stop=True), [cmT.b, hbf.bufs[s_]], [pYo.b])
                            ys = i % 2
                            tk.op("dve", lambda e: e.tensor_tensor(out=t1.t[:], in0=pYo.t[:].rearrange("p (h d) -> p h d", h=8),
                                                                   in1=ecs[:, 0:8].unsqueeze(2).to_broadcast([128, 8, 64]), op=ALU.mult), [pYo.b, ec.bufs[i % 4]], [t1.b])
                            tk.op("dve", lambda e: e.tensor_tensor(out=yv.t[:, ys, :], in0=t1.t[:].rearrange("p h d -> p (h d)"), in1=pY.t[:], op=ALU.add),
                                  [t1.b, pY.b], [yv.bufs[ys]])
                            r0 = (c - 2) * 128
                            if d == 0:
                                tk.dma(yf_d[r0:r0 + 128, :], yv.t[:, ys, :], reads=[yv.bufs[ys]], writes=[db("yf")])
                                return
                            tk.dma(yfi.t[:, ys, :], yf_d[r0:r0 + 128, :], reads=[db("yf")], writes=[yfi.bufs[ys]])
                            tk.dma(zt.t[:, ys, :], z_d[r0:r0 + 128, :], reads=[db("z")], writes=[zt.bufs[ys]])
                            tk.op("pool", lambda e: e.tensor_tensor(out=yv.t[:, ys, :], in0=yv.t[:, ys, :], in1=yfi.t[:, ys, :], op=ALU.add),
                                  [yv.bufs[ys], yfi.bufs[ys]], [yv.bufs[ys]])
                            tk.op("dve", lambda e: e.tensor_tensor(out=t1.t[:], in0=xsc, in1=dsk.t[:].unsqueeze(2).to_broadcast([128, 8, 64]), op=ALU.mult),
                                  [xs.b, dsk.b], [t1.b])
                            tk.op("pool", lambda e: e.tensor_tensor(out=yv.t[:, ys, :], in0=yv.t[:, ys, :], in1=t1.t[:].rearrange("p h d -> p (h d)"), op=ALU.add),
                                  [yv.bufs[ys], t1.b], [yv.bufs[ys]])
                            tk.op("act", lambda e: e.activation(out=sz.t[:], in_=zt.t[:, ys, :], func=AF.Silu), [zt.bufs[ys]], [sz.b])
                            tk.op("dve", lambda e: e.tensor_tensor(out=yv.t[:, ys, :], in0=yv.t[:, ys, :], in1=sz.t[:], op=ALU.mult), [yv.bufs[ys], sz.b], [yv.bufs[ys]])
                            tk.op("pool", lambda e: e.tensor_tensor(out=sq2.t[:], in0=yv.t[:, ys, :], in1=yv.t[:, ys, :], op=ALU.mult), [yv.bufs[ys]], [sq2.b])
                            tk.op("dve", lambda e: e.reduce_sum(out=ss.t[:, 0:1], in_=sq2.t[:], axis=mybir.AxisListType.X), [sq2.b], [ss.b])
                            tk.op("dve", lambda e: e.tensor_scalar(out=ss.t[:, 1:2], in0=ss.t[:, 0:1], scalar1=1.0 / 512, scalar2=EPS, op0=ALU.mult, op1=ALU.add), [ss.b], [ss.b])
                            tk.op("act", lambda e: e.sqrt(out=ss.t[:, 2:3], in_=ss.t[:, 1:2]), [ss.b], [ss.b])
                            tk.op("dve", lambda e: e.reciprocal(out=ss.t[:, 3:4], in_=ss.t[:, 2:3]), [ss.b], [ss.b])
                            tk.op("dve", lambda e: e.tensor_scalar_mul(out=ob.t[:], in0=yv.t[:, ys, :], scalar1=ss.t[:, 3:4]), [yv.bufs[ys], ss.b], [ob.b])
                            for k in range(4):
                                tk.op("pe", lambda e, k=k: e.transpose(out=pTo.t[:, k, :], in_=ob.t[:, k * 128:(k + 1) * 128], identity=idb.t[:]), [ob.b, idb.b], [pTo.b])
                            tk.op("act", lambda e: e.copy(out=obT.t[:, ys], in_=pTo.t[:, 0:4, :]), [pTo.b], [obT.bufs[ys]])
                            tk.dma(mixT_d[0:4, :, r0:r0 + 128].rearrange("c p n -> p c n"), obT.t[:, ys], reads=[obT.bufs[ys]], writes=[db("mixT")])

                        n_it = len(order)
                        for it in range(n_it + 3):
                            if 0 <= it - 3 < n_it:
                                back(it - 3)
                            if 0 <= it - 2 < n_it:
                                F2(it - 2)
                            if 0 <= it - 1 < n_it:
                                F1(it - 1)
                            if it < n_it:
                                F0(it)
                    tk.barrier()

        if ENABLE_SSD and not (debug and "skip_ssd" in debug):
            phase2a()
        phase2b()
        phase2c()
        mid.close()
        blocks3 = []
        for i in range(L // 256):
            blocks3.append((x1_d[CL + i * 256:CL + (i + 1) * 256, :], out_d[i * 256:(i + 1) * 256, :], 0, db("x1"), db("out")))
        if debug and "only_ctx" in debug:
            blocks3 = blocks3[:2]
        ffn_phase("f2", w_ffn2_in_d, w_ffn2_out_d, 2, 8, blocks3, True)

        tk.barrier(("sp",))
    return nc


_CONSTS = None


def _consts():
    global _CONSTS
    if _CONSTS is None:
        c = np.zeros((128, NCONST), np.float32)
        k = np.arange(128)[:, None]
        i = np.arange(128)[None, :]
        c[:, C_ID:C_ID + 128] = (k == i)
        c[:, C_LE:C_LE + 128] = (k <= i)
        c[:, C_GT:C_GT + 128] = (k > i)
        c[:, C_GE:C_GE + 128] = (k >= i)
        c[:, C_LT:C_LT + 128] = (k < i)
        c[:, C_ONE:C_ONE + 128] = 1.0
        c[0, C_SEL:C_SEL + 128] = 1.0
        c[1, C_SEL + 128:C_SEL + 256] = 1.0
        half = 16
        inv = 10000.0 ** (-np.arange(0, half, 2, dtype=np.float32) / half)
        pos = np.arange(L)
        row = (pos // 64).astype(np.float32)
        colp = (pos % 64).astype(np.float32)
        rope = np.zeros((2, 32, L), np.float32)
        for base, p in ((0, row), (16, colp)):
            ang = p[None, :] * inv[:, None]
            cs, sn = np.cos(ang), np.sin(ang)
            rope[0, base:base + 8] = cs
            rope[0, base + 8:base + 16] = cs
            rope[1, base:base + 8] = -sn
            rope[1, base + 8:base + 16] = sn
        _CONSTS = (c, rope)
    return _CONSTS


def make_in_maps(inputs, cores=range(8)):
    f = lambda a: np.ascontiguousarray(np.asarray(a, dtype=np.float32))
    c, rope = _consts()
    shared = {
        "w_ada": f(inputs["w_ada"][0]), "b_ada": f(inputs["b_ada"][0]).reshape(1, -1),
        "norm_ffn1": f(inputs["norm_ffn1"][0]).reshape(1, -1), "w_ffn1_in": f(inputs["w_ffn1_in"][0]),
        "w_ffn1_out": f(inputs["w_ffn1_out"][0]), "norm_mix": f(inputs["norm_mix"][0]).reshape(1, -1),
        "w_in": f(inputs["w_in"][0]), "conv_w": f(inputs["conv_w"][0]), "conv_b": f(inputs["conv_b"][0]).reshape(1, -1),
        "dt_bias": f(inputs["dt_bias"][0]).reshape(1, 16), "a_log": f(inputs["a_log"][0]).reshape(1, 16),
        "d_skip": f(inputs["d_skip"][0]).reshape(1, 8), "ssd_norm": f(inputs["ssd_norm"][0]).reshape(1, -1),
        "q_norm": f(inputs["q_norm"][0]).reshape(1, -1), "w_uq": f(inputs["w_uq"][0]),
        "kv_norm": f(inputs["kv_norm"][0]).reshape(1, -1), "w_ukv": f(inputs["w_ukv"][0]), "w_out": f(inputs["w_out"][0]),
        "norm_ffn2": f(inputs["norm_ffn2"][0]).reshape(1, -1), "w_ffn2_in": f(inputs["w_ffn2_in"][0]),
        "w_ffn2_out": f(inputs["w_ffn2_out"][0]), "final_norm": f(inputs["final_norm"]).reshape(1, -1),
        "consts": c, "rope": rope,
    }
    maps = []
    for b in cores:
        m = dict(shared)
        m["x"] = f(inputs["x"][b])
        m["ctx"] = f(inputs["ctx"][b])
        m["cc"] = np.stack([f(inputs["c"][b]), f(inputs["c_ctx"])], 0)
        maps.append(m)
    return maps


def kernel(**inputs):
    nc = build()
    maps = make_in_maps(inputs)
    res = run_bass_kernel_spmd(nc, maps, core_ids=list(range(8)))
    return np.stack([np.asarray(r["out"], dtype=np.float32) for r in res.results], 0)
```
